# Optimizing a Trainium2 kernel written in Bass

```python
import jax, jax.numpy as jnp
from jax import lax
import numpy as np

D_MODEL = 1024
BATCH = 2
SEQ = 8192
DEPTH = 4
DEC_BATCH = 128
DEC_SEQ = 8
PAST_LEN = 8192
PAGE_SIZE = 128

N_HEADS = 8
N_KV_HEADS = 2
HEAD_DIM = 64
Q_GROUP = N_HEADS // N_KV_HEADS
ATTN_WIDTH = N_HEADS * HEAD_DIM
KV_WIDTH = N_KV_HEADS * HEAD_DIM
WINDOW = 128
CONV_WIDTH = D_MODEL // 2
CONV_K = 3
CHUNK = 128
MLP_WIDTH = D_MODEL // 2
N_SPATIAL_GROUPS = 4
SPATIAL_GROUP_DIM = MLP_WIDTH // N_SPATIAL_GROUPS
N_BRANCHES = 3
EPS = 1e-6
NEG_INF = -1e30

COL_SIZES = (ATTN_WIDTH, KV_WIDTH, KV_WIDTH, ATTN_WIDTH,
             CONV_WIDTH, CONV_WIDTH, CONV_WIDTH, CONV_WIDTH,
             MLP_WIDTH, MLP_WIDTH, MLP_WIDTH,
             N_BRANCHES * D_MODEL)
IN_COLS = sum(COL_SIZES)

kernel_name = "hybrid_swa_shortconv_chunkmlp_decode_step"


def _rmsnorm(x, w):
    xf = x.astype(jnp.float32)
    r = lax.rsqrt(jnp.mean(xf * xf, axis=-1, keepdims=True) + EPS)
    return (xf * r).astype(x.dtype) * w


def _alibi_slopes():
    h = jnp.arange(1, N_HEADS + 1, dtype=jnp.float32)
    return jnp.exp2(-8.0 * h / N_HEADS).reshape(N_KV_HEADS, Q_GROUP)


def _split_cols(h):
    parts = []
    start = 0
    for size in COL_SIZES:
        parts.append(h[..., start:start + size])
        start += size
    return parts


def _mixer_inputs(x, p):
    n, t = x.shape[0], x.shape[1]
    xn = _rmsnorm(x, p["norm_w"])
    h = jnp.einsum("btd,de->bte", xn, p["w_in"])
    q, k, v, z_a, gate_b, gate_c, h_b, z_b, u, v_c, z_c, g = _split_cols(h)
    q = _rmsnorm(q.reshape(n, t, N_KV_HEADS, Q_GROUP, HEAD_DIM), p["q_norm_w"])
    k = _rmsnorm(k.reshape(n, t, N_KV_HEADS, HEAD_DIM), p["k_norm_w"])
    v = v.reshape(n, t, N_KV_HEADS, HEAD_DIM)
    conv_in = gate_c * h_b
    v_c = _rmsnorm(v_c, p["v_norm_w"])
    gates = jax.nn.sigmoid(g + p["b_gate"]).reshape(n, t, N_BRANCHES, D_MODEL)
    return q, k, v, z_a, gate_b, conv_in, z_b, u, v_c, z_c, gates


def _sink_attend(q, k, v, dist, valid, sinks):
    s = jnp.einsum("...qkgd,...skd->...kgqs", q, k).astype(jnp.float32) * (HEAD_DIM ** -0.5)
    s = s - _alibi_slopes()[:, :, None, None] * dist.astype(jnp.float32)
    s = jnp.where(valid, s, NEG_INF)
    sink = sinks.astype(jnp.float32).reshape(N_KV_HEADS, Q_GROUP, 1, 1)
    mx = jnp.maximum(s.max(axis=-1, keepdims=True), sink)
    e = jnp.exp(s - mx)
    prob = e / (e.sum(axis=-1, keepdims=True) + jnp.exp(sink - mx))
    return jnp.einsum("...kgqs,...skd->...qkgd", prob.astype(v.dtype), v)


def _attn_prompt(q, k, v, sinks):
    b, s = q.shape[0], q.shape[1]
    nb = s // WINDOW
    qb = q.reshape(b, nb, WINDOW, N_KV_HEADS, Q_GROUP, HEAD_DIM)
    kb = k.reshape(b, nb, WINDOW, N_KV_HEADS, HEAD_DIM)
    vb = v.reshape(b, nb, WINDOW, N_KV_HEADS, HEAD_DIM)

    def with_prev(t):
        prev = jnp.concatenate([jnp.zeros_like(t[:, :1]), t[:, :-1]], axis=1)
        return jnp.concatenate([prev, t], axis=2)

    kk, vv = with_prev(kb), with_prev(vb)
    i = jnp.arange(WINDOW)[:, None]
    j = jnp.arange(2 * WINDOW)[None, :]
    dist = i + WINDOW - j
    band = (dist >= 0) & (dist < WINDOW)
    has_prev = (jnp.arange(nb)[:, None, None] > 0) | (j[None] >= WINDOW)
    valid = (band[None] & has_prev)[:, None, None]
    out = _sink_attend(qb, kk, vv, dist, valid, sinks)
    return out.reshape(b, s, ATTN_WIDTH)


def _attn_sample(q, k, v, cache_k, cache_v, sinks):
    n, t = q.shape[0], q.shape[1]
    w = cache_k.shape[1]
    kk = jnp.concatenate([cache_k, k], axis=1)
    vv = jnp.concatenate([cache_v, v], axis=1)
    i = jnp.arange(t)[:, None]
    j = jnp.arange(w + t)[None, :]
    dist = i + w - j
    valid = (dist >= 0) & (dist < WINDOW)
    out = _sink_attend(q, kk, vv, dist, valid, sinks)
    return out.reshape(n, t, ATTN_WIDTH), kk[:, -w:], vv[:, -w:]


def _causal_conv(prev, xc, w):
    t = xc.shape[1]
    xp = jnp.concatenate([prev, xc], axis=1)
    out = w[0] * xp[:, 0:t]
    for r in range(1, CONV_K):
        out = out + w[r] * xp[:, r:r + t]
    return out, xp[:, t:]


def _spatial_gate(u, v_c, w_s, b_s):
    n, t = u.shape[0], u.shape[1]
    length = min(t, CHUNK)
    nc = t // length
    mask = jnp.tril(jnp.ones((length, length), dtype=bool))
    wm = jnp.where(mask, w_s[:, :length, :length], 0.0)
    vb = v_c.reshape(n, nc, length, N_SPATIAL_GROUPS, SPATIAL_GROUP_DIM)
    sp = jnp.einsum("gts,bnsgc->bntgc", wm, vb) + b_s[:, :length].T[:, :, None]
    return u * sp.reshape(n, t, MLP_WIDTH)


def _merge(x, y_a, y_b, y_c, z_a, z_b, z_c, gates, p):
    a = jnp.einsum("bte,ed->btd", jax.nn.silu(z_a) * y_a, p["w_out_a"])
    b = jnp.einsum("bte,ed->btd", jax.nn.silu(z_b) * y_b, p["w_out_b"])
    c = jnp.einsum("bte,ed->btd", jax.nn.silu(z_c) * y_c, p["w_out_c"])
    m = gates[:, :, 0] * a + gates[:, :, 1] * b + gates[:, :, 2] * c
    return x + jnp.einsum("btd,de->bte", m, p["w_o"])


def setup_inputs(seed: int = 0) -> dict:
    key = jax.random.key(seed)
    ks = jax.random.split(key, 24)
    f32 = jnp.float32
    w_buf = min(WINDOW, PAST_LEN)

    def nrm(k, shape, scale):
        return jax.random.normal(k, shape, f32) * scale

    return {
        "x_prompt": nrm(ks[0], (BATCH, SEQ, D_MODEL), 1.0),
        "x_sample": nrm(ks[1], (DEC_BATCH, DEC_SEQ, D_MODEL), 1.0),
        "cache_k": nrm(ks[2], (DEPTH, DEC_BATCH, w_buf, N_KV_HEADS, HEAD_DIM), 1.0),
        "cache_v": nrm(ks[3], (DEPTH, DEC_BATCH, w_buf, N_KV_HEADS, HEAD_DIM), 1.0),
        "state_conv": nrm(ks[4], (DEPTH, DEC_BATCH, CONV_K - 1, CONV_WIDTH), 1.0),
        "norm_w": 1.0 + nrm(ks[5], (DEPTH, D_MODEL), 0.02),
        "w_in": nrm(ks[6], (DEPTH, D_MODEL, IN_COLS), D_MODEL ** -0.5),
        "b_gate": nrm(ks[7], (DEPTH, N_BRANCHES * D_MODEL), 0.02),
        "q_norm_w": 1.0 + nrm(ks[8], (DEPTH, HEAD_DIM), 0.02),
        "k_norm_w": 1.0 + nrm(ks[9], (DEPTH, HEAD_DIM), 0.02),
        "sinks": nrm(ks[10], (DEPTH, N_HEADS), 0.5),
        "conv_w": nrm(ks[11], (DEPTH, CONV_K, CONV_WIDTH), CONV_K ** -0.5),
        "v_norm_w": 1.0 + nrm(ks[12], (DEPTH, MLP_WIDTH), 0.02),
        "w_spatial": nrm(ks[13], (DEPTH, N_SPATIAL_GROUPS, CHUNK, CHUNK), CHUNK ** -0.5),
        "b_spatial": 1.0 + nrm(ks[14], (DEPTH, N_SPATIAL_GROUPS, CHUNK), 0.1),
        "w_out_a": nrm(ks[15], (DEPTH, ATTN_WIDTH, D_MODEL), ATTN_WIDTH ** -0.5),
        "w_out_b": nrm(ks[16], (DEPTH, CONV_WIDTH, D_MODEL), CONV_WIDTH ** -0.5),
        "w_out_c": nrm(ks[17], (DEPTH, MLP_WIDTH, D_MODEL), MLP_WIDTH ** -0.5),
        "w_o": nrm(ks[18], (DEPTH, D_MODEL, D_MODEL), D_MODEL ** -0.5),
    }


def reference(x_prompt, x_sample, cache_k, cache_v, state_conv, norm_w, w_in, b_gate,
              q_norm_w, k_norm_w, sinks, conv_w, v_norm_w, w_spatial, b_spatial,
              w_out_a, w_out_b, w_out_c, w_o):
    yp, ys = x_prompt, x_sample
    pk, pv, pc, sk, sv, sc, scv = [], [], [], [], [], [], []
    for l in range(DEPTH):
        p = {"norm_w": norm_w[l], "w_in": w_in[l], "b_gate": b_gate[l],
             "q_norm_w": q_norm_w[l], "k_norm_w": k_norm_w[l], "v_norm_w": v_norm_w[l],
             "w_out_a": w_out_a[l], "w_out_b": w_out_b[l], "w_out_c": w_out_c[l], "w_o": w_o[l]}

        q, k, v, z_a, gate_b, conv_in, z_b, u, v_c, z_c, gates = _mixer_inputs(yp, p)
        y_a = _attn_prompt(q, k, v, sinks[l])
        conv_out, conv_last = _causal_conv(
            jnp.zeros((yp.shape[0], CONV_K - 1, CONV_WIDTH), yp.dtype), conv_in, conv_w[l])
        y_c = _spatial_gate(u, v_c, w_spatial[l], b_spatial[l])
        w_p = min(WINDOW, yp.shape[1])
        pk.append(k[:, -w_p:])
        pv.append(v[:, -w_p:])
        pc.append(conv_last)
        yp = _merge(yp, y_a, gate_b * conv_out, y_c, z_a, z_b, z_c, gates, p)

        q, k, v, z_a, gate_b, conv_in, z_b, u, v_c, z_c, gates = _mixer_inputs(ys, p)
        y_a, new_k, new_v = _attn_sample(q, k, v, cache_k[l], cache_v[l], sinks[l])
        conv_out, conv_last = _causal_conv(state_conv[l], conv_in, conv_w[l])
        y_c = _spatial_gate(u, v_c, w_spatial[l], b_spatial[l])
        sk.append(new_k)
        sv.append(new_v)
        sc.append(conv_last)
        scv.append(v_c)
        ys = _merge(ys, y_a, gate_b * conv_out, y_c, z_a, z_b, z_c, gates, p)

    return (yp, ys, jnp.stack(pk), jnp.stack(pv), jnp.stack(pc),
            jnp.stack(sk), jnp.stack(sv), jnp.stack(sc), jnp.stack(scv))
```

```python
import numpy as np
from contextlib import ExitStack

import concourse.bass as bass
import concourse.mybir as mybir
from concourse.bass_utils import run_bass_kernel_spmd

F32 = mybir.dt.float32
BF16 = mybir.dt.bfloat16
AF = mybir.ActivationFunctionType
ALU = mybir.AluOpType

D = 1024
NCH = 8
NSEQ = 16
TS = 8
EPS = 1e-6
NEG = -30000.0

O_Q, O_K, O_V, O_ZA = 0, 512, 640, 768
O_GB, O_GC, O_HB, O_ZB = 1280, 1792, 2304, 2816
O_U, O_VC, O_ZC, O_G = 3328, 3840, 4352, 4864

PIECES = ([("q", 8, 512), ("kv", 8, 256), ("za", 8, 512)]
          + [("b%d" % c, 8, 512) for c in range(4)]
          + [("vc", 8, 512), ("u", 8, 512), ("zc", 8, 512)]
          + [("g%d" % j, 12, 384) for j in range(8)]
          + [("o0", 8, 512), ("o1", 8, 512)])
PIECE_OFF = {}
_o = 0
for _n, _k, _g in PIECES:
    PIECE_OFF[_n] = (_o, _k, _g)
    _o += _k * _g
PCOLS = _o
SLOT_ELEMS = 12 * 384


class Buf:
    __slots__ = ("name", "writer", "readers", "gen", "excl")

    def __init__(self, name, excl=False):
        self.name = name
        self.writer = None
        self.readers = {}
        self.gen = 0
        self.excl = excl


class _Dummy:
    pass


class Sched:
    ENG = ("pe", "act", "dve", "pool", "sp")

    def __init__(self, nc, ctx, signal=None):
        self.nc = nc
        self.dry = signal is None
        self.signal = {e: set() for e in self.ENG} if self.dry else signal
        self.eng = {"pe": nc.tensor, "act": nc.scalar, "dve": nc.vector, "pool": nc.gpsimd, "sp": nc.sync}
        self.sem = {}
        self.raw = {e: 0 for e in self.ENG}
        self.sig = {e: 0 for e in self.ENG}
        for e in self.ENG:
            self.sem[e] = _Dummy() if self.dry else ctx.enter_context(nc.semaphore("s_" + e))
        self.known = {e: {} for e in self.ENG}
        self.ctx = ctx
        self.dma_sems = {}
        self.nwaits = 0
        self.nops = {e: 0 for e in self.ENG}

    def _waits(self, e, reads, writes):
        evs = {}

        def add(ev):
            if ev is None:
                return
            s, v, src, raw = ev
            if src == "pe" and e == "pe":
                return
            k = id(s)
            if k not in evs or evs[k][1] < v:
                evs[k] = ev

        for b in reads:
            add(b.writer)
            if b.excl:
                for ev in b.readers.values():
                    if ev[2] != e:
                        add(ev)
        for b in writes:
            add(b.writer)
            for ev in b.readers.values():
                add(ev)
        kn = self.known[e]
        for k, (s, v, src, raw) in evs.items():
            if kn.get(k, 0) >= v:
                continue
            if self.dry:
                if src != "dma":
                    self.signal[src].add(raw)
            else:
                assert src == "dma" or raw in self.signal[src], "wait on a non-signalling instruction"
                self.eng[e].wait_ge(s, v)
            kn[k] = v
            self.nwaits += 1

    @staticmethod
    def _commit(ev, reads, writes):
        k = id(ev[0])
        for b in writes:
            b.writer = ev
            b.readers = {}
        for b in reads:
            old = b.readers.get(k)
            if old is None or old[1] < ev[1]:
                b.readers[k] = ev

    def op(self, e, fn, reads=(), writes=()):
        reads = _unlease(reads)
        writes = _unlease(writes)
        self._waits(e, reads, writes)
        self.raw[e] += 1
        self.nops[e] += 1
        raw = self.raw[e]
        if self.dry:
            ev = (self.sem[e], raw, e, raw)
        else:
            ins = fn(self.eng[e])
            if raw in self.signal[e]:
                self.sig[e] += 1
                ins.then_inc(self.sem[e], 1)
            ev = (self.sem[e], self.sig[e], e, raw)
        self._commit(ev, reads, writes)
        return ev

    def dma(self, q, semname, out, in_, reads=(), writes=(), **kw):
        if semname not in self.dma_sems:
            self.dma_sems[semname] = [_Dummy() if self.dry else self.ctx.enter_context(self.nc.semaphore("d_" + semname)), 0]
        ent = self.dma_sems[semname]
        reads = _unlease(reads)
        writes = _unlease(writes)
        self._waits(q, reads, writes)
        ent[1] += 16
        if not self.dry:
            ins = self.eng[q].dma_start(out=out, in_=in_, **kw)
            ins.then_inc(ent[0], 16)
        ev = (ent[0], ent[1], "dma", None)
        self._commit(ev, reads, writes)
        return ev

    def wait_all_dma(self, e):
        if self.dry:
            return
        for name, (s, v) in self.dma_sems.items():
            if v > 0 and self.known[e].get(id(s), 0) < v:
                self.eng[e].wait_ge(s, v)
                self.known[e][id(s)] = v


class Lease:
    __slots__ = ("buf", "gen")

    def __init__(self, buf, gen):
        self.buf = buf
        self.gen = gen


def _unlease(bs):
    out = []
    for b in bs:
        if isinstance(b, Lease):
            assert b.gen == b.buf.gen, "stale lease on %s" % b.buf.name
            b = b.buf
        out.append(b)
    return out


class Pool:
    def __init__(self, aps, name, excl=False, bufs=None):
        self.aps = aps
        self.bufs = bufs if bufs is not None else [Buf("%s%d" % (name, i), excl) for i in range(len(aps))]
        self.i = 0

    def get(self):
        i = self.i
        self.i = (i + 1) % len(self.aps)
        self.bufs[i].gen += 1
        return self.aps[i], Lease(self.bufs[i], self.bufs[i].gen)


def default_sts():
    return [
        [dict(kind="P", blocks=[0, 1, 2, 3]), dict(kind="P", blocks=[4, 5, 6])],
        [dict(kind="P", blocks=[7, 8, 9, 10]), dict(kind="P", blocks=[11, 12, 13])],
        [dict(kind="P", blocks=[14, 15, 16, 17]), dict(kind="P", blocks=[18, 19]), dict(kind="S", blocks=["S"])],
    ]


class StopBuild(Exception):
    pass


def build(depth, sts, npb, first_out_gb, nslots=5, dbg=None, stop_after=None, variant=0):
    L = depth
    last_gb = npb - 1
    nc = bass.Bass("TRN2", target_bir_lowering=False)

    def din(name, shape):
        return nc.dram_tensor(name, list(shape), F32, kind="ExternalInput").ap()

    def dout(name, shape):
        return nc.dram_tensor(name, list(shape), F32, kind="ExternalOutput").ap()

    n_out_blocks = npb - first_out_gb
    xp = din("xp", [npb * 128, D])
    xs = din("xs", [128, D])
    ck_d = din("ck", [L, NSEQ, 128, 128])
    cv_d = din("cv", [L, NSEQ, 128, 128])
    sconv_d = din("sconv", [L, NSEQ * 2, 512])
    wp_d = din("wp", [L, 128, PCOLS])
    nw_d = din("nw", [128, L * 8])
    bg_d = din("bg", [128, L * 24])
    qw_d = din("qw", [128, L])
    kw_d = din("kw", [128, L])
    sk_d = din("sk4", [128, L * 4])
    cw_d = din("cw", [128, L * 12])
    vnw_d = din("vnw", [L, 128, 512])
    wsp_d = din("wsp", [L, 4, 128, 128])
    bspt_d = din("bspt", [L, 128, 512])
    bspts_d = din("bspts", [L, 128, 512])
    ident_d = din("ident", [128, 128])
    tri_d = din("tri", [128, 128])
    bdm_d = din("bdm", [128, 128])
    bd64_d = din("bd64", [128, 128])
    repi_d = din("repi", [8, 128])
    bias_d = din("biasall", [10, 128, 512])

    yp_o = dout("yp", [n_out_blocks * 128, D])
    ys_o = dout("ys", [128, D])
    pk_o = dout("pk", [L, 128, 128])
    pv_o = dout("pv", [L, 128, 128])
    pc_o = dout("pc", [L, 2, 512])
    sk_o = dout("sko", [L, NSEQ, 128, 128])
    sv_o = dout("svo", [L, NSEQ, 128, 128])
    sc_o = dout("sco", [L, NSEQ * 2, 512])
    scv_o = dout("scv", [L, 128, 512])
    dbg_o = None
    if dbg:
        dbg_o = dout("dbg", [128, dbg])

    with ExitStack() as ctx:
        def sb(name, shape, dt):
            return ctx.enter_context(nc.sbuf_tensor(name, list(shape), dt))

        STW = max(sum(len(t["blocks"]) for t in st) for st in sts) * 128

        xT = sb("xT", [128, NCH, STW], F32)
        xnT = sb("xnT", [128, NCH, STW], BF16)
        PA = sb("PA", [128, 4, STW], BF16)
        PB = sb("PB", [128, 4, STW], BF16)
        PC = sb("PC", [128, 4, STW], BF16)
        QT = PB
        MBR = sb("MBR", [128, NCH * STW], BF16)
        MB = MBR[:].rearrange("p (c n) -> p c n", c=NCH)
        YX = MBR[:].bitcast(F32).rearrange("p (c n) -> p c n", c=4)
        CE = sb("CE", [128, 2 + STW], F32)
        CES = sb("CES", [128, NSEQ, 10], F32)
        KTe = sb("KTe", [128, STW + 128], BF16)
        Ve = sb("Ve", [128, STW // 128 + 1, 128], BF16)
        KTb = sb("KTb", [128, L, 128], BF16)
        Vb = sb("Vb", [128, L, 128], BF16)
        CEb = sb("CEb", [128, L, 4, 2], F32)
        TMPS = [sb("tmp%d" % i, [128, 512], F32) for i in range(7)]
        BTS = [sb("bt%d" % i, [128, 512], BF16) for i in range(6)]
        NEGH = sb("NEGH", [128, 64], F32)
        R2S = [sb("r2s%d" % i, [128, 128], F32) for i in range(8)]
        SMALL = [sb("sm%d" % i, [128, 256], F32) for i in range(2)]
        IO = [sb("io%d" % i, [128, D], F32) for i in range(2)]
        KOs = sb("KOs", [128, 128], F32)
        VOs = KOs
        CKT = sb("CKT", [128, NSEQ, 128], BF16)
        STG = sb("STG", [96, 512], F32)
        SCo = STG[0:32, :]
        PCs = STG[32:34, :]
        SCs = STG[64:96, :]
        C32 = sb("C32", [128, 32], F32)
        SSS = [sb("SS%d" % i, [128, 4], F32) for i in range(4)]
        NRT = [sb("NRT%d" % i, [128, 8], F32) for i in range(3)]
        RPS = [sb("RP%d" % i, [128, 8], F32) for i in range(6)]
        IDENT = sb("IDENT", [128, 128], F32)
        IDENTB = sb("IDENTB", [128, 128], BF16)
        ONESB = sb("ONESB", [128, 128], BF16)
        BD64 = sb("BD64", [128, 128], BF16)
        TRI = sb("TRI", [128, 128], BF16)
        BDM = sb("BDM", [128, 128], BF16)
        REPI = sb("REPI", [8, 128], BF16)
        BIAS = sb("BIAS", [128, 8, 512], F32)
        WMT = sb("WMT", [128, L, 4, 128], BF16)
        WMTS = sb("WMTS", [128, L, 4, 128], BF16)
        WSPL = sb("WSPL", [128, 4, 128], BF16)
        VNW = sb("VNW", [128, 512], F32)
        BSPT = sb("BSPT", [128, 512], F32)
        BSPTS = sb("BSPTS", [128, 512], F32)
        NW = sb("NW", [128, L * 8], F32)
        BGH = sb("BGH", [128, L * 24], F32)
        QW = sb("QW", [128, L], F32)
        KW = sb("KW", [128, L], F32)
        SE4 = sb("SE4", [128, L * 4], F32)
        CW = sb("CW", [128, L * 12], F32)
        EPSC = sb("EPSC", [128, 1], F32)
        print("sbuf before ring:", nc.sbuf_bytes_remaining, "need", nslots * SLOT_ELEMS * 2)
        RING = [sb("ring%d" % i, [128, SLOT_ELEMS], BF16) for i in range(nslots)]
        PSB = [ctx.enter_context(nc.psum_tensor("ps%d" % i, [128, 512], F32)) for i in range(8)]

        block = ctx.enter_context(nc.Block())
        def emit_all(S):

            psum = Pool([p[:] for p in PSB], "ps", excl=True)
            psum_lo = Pool([p[:] for p in PSB[0:5]], "ps", bufs=psum.bufs[0:5])
            psum_hi = Pool([p[:] for p in PSB[5:8]], "ps", bufs=psum.bufs[5:8])
            tmps = Pool([t[:] for t in TMPS], "tmp")
            bts = Pool([t[:] for t in BTS], "bt")
            smalls = Pool([t[:] for t in SMALL], "sm")
            r2s = Pool([t[:] for t in R2S], "r2s")
            sss = Pool([t[:] for t in SSS], "ss")
            rps = Pool([t[:] for t in RPS], "rp")

            B = {}

            def buf(name):
                if name not in B:
                    B[name] = Buf(name)
                return B[name]

            ntile_max = max(len(st) for st in sts)
            b_xT = [[buf("xT%d_%d" % (t, c)) for c in range(NCH)] for t in range(ntile_max)]
            b_xnT = [buf("xnT%d" % t) for t in range(ntile_max)]
            b_PA = [buf("PA%d" % t) for t in range(ntile_max)]
            b_PB = [buf("PB%d" % t) for t in range(ntile_max)]
            b_PC = [buf("PC%d" % t) for t in range(ntile_max)]
            b_MB = buf("MBR")
            b_MBt = [buf("MBt%d" % t) for t in range(ntile_max)]
            b_CE = buf("CE")
            b_CES = buf("CES")
            b_KT = [buf("KT%d" % i) for i in range(STW // 128 + 1)]
            b_V = [buf("V%d" % i) for i in range(STW // 128 + 1)]
            b_KTb = [buf("KTb%d" % l) for l in range(L)]
            b_Vb = [buf("Vb%d" % l) for l in range(L)]
            b_CEb = [buf("CEb%d" % l) for l in range(L)]
            b_IO = [buf("IO0"), buf("IO1")]
            b_const = buf("const")
            b_ring = [buf("ring%d" % i) for i in range(nslots)]

            cq = "sp"
            for dst, src in [(IDENT, ident_d), (NW, nw_d), (BGH, bg_d), (QW, qw_d), (KW, kw_d), (SE4, sk_d), (CW, cw_d)]:
                S.dma(cq, "const", dst[:], src, writes=[b_const])
            S.dma(cq, "const", BIAS[:, 0:4, :], bias_d[0:4].rearrange("k p n -> p k n"), writes=[b_const])
            for dst, src in [(IDENTB, ident_d), (TRI, tri_d), (BDM, bdm_d), (BD64, bd64_d), (REPI, repi_d)]:
                S.dma("pool", "constp", dst[:], src, writes=[b_const])
            S.op("dve", lambda e: e.memset(ONESB[:], 1.0), writes=[b_const])
            S.op("dve", lambda e: e.memset(NEGH[:], -0.5), writes=[b_const])
            S.op("dve", lambda e: e.memset(EPSC[:], EPS), writes=[b_const])
            S.op("dve", lambda e: e.memset(KTb[:], 0.0), writes=b_KTb)
            S.op("dve", lambda e: e.memset(Vb[:], 0.0), writes=b_Vb)
            S.op("dve", lambda e: e.memset(CEb[:], 0.0), writes=b_CEb)
            S.op("dve", lambda e: e.tensor_scalar(BGH[:], BGH[:], 0.5, None, op0=ALU.mult), reads=[b_const], writes=[b_const])
            S.op("act", lambda e: e.activation(SE4[:], SE4[:], AF.Exp), reads=[b_const], writes=[b_const])
            for l in range(L):
                bw = buf("WSPL")
                S.dma("pool", "wspl", WSPL[:], wsp_d[l].rearrange("g t s -> t g s"), writes=[bw])
                pst, bp = psum.get()
                pstb = pst.bitcast(BF16)
                for g in range(4):
                    S.op("pe", lambda e, g=g: e.transpose(pstb[:, g * 128:(g + 1) * 128], WSPL[:, g, :], IDENTB[:]),
                         reads=[bw, b_const], writes=[bp])
                for g in range(4):
                    S.op("dve", lambda e, g=g, l=l: e.tensor_tensor(WMT[:, l, g, :], pstb[:, g * 128:(g + 1) * 128], TRI[:], op=ALU.mult),
                         reads=[bp, b_const], writes=[b_const])
                ps2, bp2 = psum.get()
                for g in range(4):
                    S.op("pe", lambda e, g=g, l=l: e.matmul(
                        ps2[:, g * 128:(g + 1) * 128].rearrange("p (n t) -> p n t", n=NSEQ), REPI[:],
                        WMT[0:8, l, g, 0:8].unsqueeze(1).to_broadcast([8, NSEQ, 8]), start=True, stop=True),
                        reads=[b_const], writes=[bp2])
                for g in range(4):
                    S.op("dve", lambda e, g=g, l=l: e.tensor_tensor(WMTS[:, l, g, :], ps2[:, g * 128:(g + 1) * 128], BDM[:], op=ALU.mult),
                         reads=[bp2, b_const], writes=[b_const])

            for l in range(L):
                S.dma("sp", "cachecopy", sk_o[l, :, 0:128 - TS, :], ck_d[l, :, TS:128, :])
                S.dma("sp", "cachecopy", sv_o[l, :, 0:128 - TS, :], cv_d[l, :, TS:128, :])

            schedule = []
            for si in range(len(sts)):
                for l in range(L):
                    for (pn, pk_, pg) in PIECES:
                        schedule.append((l, pn))
            wstate = dict(next=0, cur=-1)

            def pump(limit):
                while wstate["next"] < min(limit, len(schedule)):
                    k = wstate["next"]
                    l, pn = schedule[k]
                    off, kk, g = PIECE_OFF[pn]
                    slot = k % nslots
                    S.dma("pool", "ring%d" % slot, RING[slot][:, 0:kk * g], wp_d[l, :, off:off + kk * g], writes=[b_ring[slot]])
                    wstate["next"] = k + 1

            def piece(pn):
                wstate["cur"] += 1
                k = wstate["cur"]
                assert schedule[k][1] == pn, (schedule[k], pn)
                pump(k + 1)
                off, kk, g = PIECE_OFF[pn]
                slot = k % nslots
                return RING[slot][:, 0:kk * g].rearrange("p (k g) -> p k g", k=kk), b_ring[slot]

            def piece_done():
                pump(wstate["cur"] + nslots + 1)

            pump(nslots)

            def mm_group(ps_ap, bp, pairs, reads, **kw):
                n = len(pairs)
                for i, (lh, rh) in enumerate(pairs):
                    S.op("pe", lambda e, lh=lh, rh=rh, i=i: e.matmul(ps_ap, lh, rh, start=(i == 0), stop=(i == n - 1), **kw),
                         reads=reads, writes=[bp])

            def inproj(ps_ap, bp, w_ap, bw, col0, ncols, tcols, t):
                mm_group(ps_ap, bp, [(w_ap[:, k, col0:col0 + ncols], xnT[:, k, tcols]) for k in range(NCH)],
                         reads=[bw, b_xnT[t]])

            def rstd_part1a(sq_list, sq_bufs, n, H, rhs_sel, scale, r=None, br=None):
                nb = n // 128
                pst, bpst = psum_hi.get()
                for b in range(nb):
                    mm_group(pst[:, b * H:(b + 1) * H], bpst, [(sq[:, b * 128:(b + 1) * 128], rhs_sel) for sq in sq_list],
                             reads=list(sq_bufs) + [b_const])
                if r is None:
                    r, br = rps.get()
                w = nb * H
                S.op("act", lambda e: e.activation(r[:, 0:w], pst[:, 0:w], AF.Identity, bias=EPSC[:, 0:1], scale=scale),
                     reads=[bpst, b_const], writes=[br])
                S.op("pool", lambda e: e.tensor_tensor(r[:, 0:w], r[:, 0:w], NEGH[:, 0:w], op=ALU.pow),
                     reads=[br, b_const], writes=[br])
                return r, br

            def rstd_part1b(r, br, n, H):
                r2l = []
                for b in range(n // 128):
                    r2, br2 = r2s.get()
                    S.op("dve", lambda e, b=b, r2=r2: e.tensor_copy(
                        r2.rearrange("p (h d) -> p h d", h=H), r[:, b * H:(b + 1) * H].unsqueeze(2).to_broadcast([128, H, 128 // H])),
                        reads=[br], writes=[br2])
                    r2l.append((r2, br2))
                return r2l

            def rstd_part1(sq_list, sq_bufs, n, H, rhs_sel, scale):
                r, br = rstd_part1a(sq_list, sq_bufs, n, H, rhs_sel, scale)
                return rstd_part1b(r, br, n, H)

            def rstd_part2(r2l, n, to_sbuf):
                pbc, bpbc = psum_hi.get()
                for b, (r2, br2) in enumerate(r2l):
                    S.op("pe", lambda e, b=b, r2=r2: e.transpose(pbc[:, b * 128:(b + 1) * 128], r2, IDENT[:]),
                         reads=[br2, b_const], writes=[bpbc])
                if not to_sbuf:
                    return pbc, bpbc, (pbc, bpbc)
                rs, brs = tmps.get()
                S.op("act", lambda e: e.copy(rs[:, 0:n], pbc[:, 0:n]), reads=[bpbc], writes=[brs])
                return rs, brs, (pbc, bpbc)

            def run_pipeline(items, order):
                ns = len(items[0])
                for step in range(len(items) + ns - 1):
                    for s_ in order:
                        k = step - s_
                        if 0 <= k < len(items):
                            items[k][s_]()

            def silu2(ps_ap, bp, n):
                th, bth = tmps.get()
                S.op("act", lambda e: e.activation(th[:, 0:n], ps_ap, AF.Tanh, scale=0.5), reads=[bp], writes=[bth])
                t2, bt2 = tmps.get()
                S.op("dve", lambda e: e.scalar_tensor_tensor(t2[:, 0:n], th[:, 0:n], 1.0, ps_ap, op0=ALU.add, op1=ALU.mult),
                     reads=[bth, bp], writes=[bt2])
                return t2, bt2

            def transpose_out(src_ap, bsrc, rows, ncols_src, dst_sb, bdst, dst_cols, ps_lease=None):
                ps, bp = ps_lease if ps_lease is not None else psum.get()
                S.op("pe", lambda e: e.transpose(ps[0:ncols_src, 0:rows], src_ap, IDENT[0:rows, 0:rows]),
                     reads=[bsrc, b_const], writes=[bp])
                S.op("act", lambda e: e.copy(dst_sb[0:ncols_src, dst_cols], ps[0:ncols_src, 0:rows]), reads=[bp], writes=[bdst])

            cur_si = [0]

            def stage(name):
                if stop_after == name or stop_after == "%d:%s" % (cur_si[0], name):
                    raise StopBuild()

            try:
              stage("setup")
              for si, st in enumerate(sts):
                  cur_si[0] = si
                  tiles = []
                  col = 0
                  for t, tl in enumerate(st):
                      n = 128 * len(tl["blocks"])
                      tiles.append(dict(kind=tl["kind"], c0=col, n=n, blocks=tl["blocks"], t=t))
                      col += n
                  nblk = col // 128
                  has_sample = any(tl["kind"] == "S" for tl in tiles)
                  n_prompt_blk = sum(len(tl["blocks"]) for tl in tiles if tl["kind"] == "P")
                  first_blk_is_out = None

                  io_i = 0
                  for tl in tiles:
                      for bi, gb in enumerate(tl["blocks"]):
                          c0 = tl["c0"] + bi * 128
                          src = xs if gb == "S" else xp[gb * 128:(gb + 1) * 128, :]
                          io, bio = IO[io_i % 2], b_IO[io_i % 2]
                          S.dma("sp", "io%d" % (io_i % 2), io[:], src, writes=[bio])
                          io_i += 1
                          for h in range(2):
                              ps, bp = psum.get()
                              for c in range(4):
                                  cc = h * 4 + c
                                  S.op("pe", lambda e, cc=cc, c=c, ps=ps, io=io: e.transpose(ps[:, c * 128:(c + 1) * 128], io[:, cc * 128:(cc + 1) * 128], IDENT[:]),
                                       reads=[bio, b_const], writes=[bp])
                              eng = "act" if h == 0 else "dve"
                              if eng == "act":
                                  S.op("act", lambda e, h=h, ps=ps, c0=c0: e.copy(xT[:, h * 4:(h + 1) * 4, c0:c0 + 128], ps.rearrange("p (c n) -> p c n", c=4)),
                                       reads=[bp], writes=b_xT[tl["t"]][h * 4:(h + 1) * 4])
                              else:
                                  S.op("dve", lambda e, h=h, ps=ps, c0=c0: e.tensor_copy(xT[:, h * 4:(h + 1) * 4, c0:c0 + 128], ps.rearrange("p (c n) -> p c n", c=4)),
                                       reads=[bp], writes=b_xT[tl["t"]][h * 4:(h + 1) * 4])

                  stage('load')
                  if si == 0:
                      S.dma("sp", "biasx", BIAS[:, 4:6, :], bias_d[4:6].rearrange("k p n -> p k n"), writes=[buf("biasx")])
                  if has_sample:
                      S.dma("sp", "biasx", BIAS[:, 4:8, :], bias_d[6:10].rearrange("k p n -> p k n"), writes=[buf("biasx")])
                  b_biasx = buf("biasx")

                  tiles_full = tiles

                  def tiles_at(l_):
                      min_gb = first_out_gb - (L - l_)
                      res = []
                      for lo in (min_gb, min_gb + 1 if min_gb >= 0 else min_gb):
                          lst = []
                          for tl in tiles_full:
                              keep = [(i, gb) for i, gb in enumerate(tl["blocks"]) if gb == "S" or gb >= lo]
                              if not keep:
                                  continue
                              i0_ = keep[0][0]
                              lst.append(dict(kind=tl["kind"], c0=tl["c0"] + 128 * i0_, n=128 * len(keep), blocks=[gb for _, gb in keep], t=tl["t"]))
                          res.append(lst)
                      return res

                  norm_pre = {}

                  def norm_AB(l_, tl):
                      t, c0, n = tl["t"], tl["c0"], tl["n"]
                      tc_ = slice(c0, c0 + n)
                      for c in range(NCH):
                          S.op("act", lambda e, c=c: e.activation(xnT[:, c, tc_], xT[:, c, tc_], AF.Square),
                               reads=[b_xT[t][c]], writes=[b_xnT[t]])
                      norm_pre[(l_, t)] = rstd_part1a([xnT[:, c, tc_] for c in range(NCH)], [b_xnT[t]], n, 1, ONESB[:, 0:1], 1.0 / D,
                                                      r=NRT[t][:], br=buf("NRT%d" % t))

                  norm_done = set()

                  def norm_C(l_, tl):
                      t, c0, n = tl["t"], tl["c0"], tl["n"]
                      tc_ = slice(c0, c0 + n)
                      rr, brr = norm_pre.pop((l_, t))
                      r, br, _ = rstd_part2(rstd_part1b(rr, brr, n, 1), n, False)
                      for c in range(NCH):
                          S.op("dve", lambda e, c=c: e.scalar_tensor_tensor(
                              xnT[:, c, tc_], xT[:, c, tc_], NW[:, l_ * 8 + c:l_ * 8 + c + 1], r[:, 0:n], op0=ALU.mult, op1=ALU.mult),
                              reads=[b_xT[t][c], br, b_const], writes=[b_xnT[t]])
                      norm_done.add((l_, t))

                  for l in range(L):
                      tiles_kv, tiles = tiles_at(l)
                      b_lay = buf("laycst")
                      S.dma("sp", "laycst", VNW[:], vnw_d[l], writes=[b_lay])
                      S.dma("sp", "laycst", BSPT[:], bspt_d[l], writes=[b_lay])
                      if has_sample:
                          S.dma("sp", "laycst", BSPTS[:], bspts_d[l], writes=[b_lay])

                      S.op("pool", lambda e: e.tensor_copy(KTe[:, 0:128], KTb[:, l, :]), reads=[b_KTb[l]], writes=[b_KT[0]])
                      S.op("pool", lambda e: e.tensor_copy(Ve[:, 0, :], Vb[:, l, :]), reads=[b_Vb[l]], writes=[b_V[0]])
                      if has_sample:
                          S.dma("pool", "ckl", IO[0][:].bitcast(BF16)[:, 0:NSEQ * 128].rearrange("p (n f) -> p n f", n=NSEQ),
                                ck_d[l].rearrange("n j f -> j n f"), writes=[b_IO[0]])
                          S.dma("pool", "cvl", IO[1][:].bitcast(BF16)[:, 0:NSEQ * 128].rearrange("p (n f) -> p n f", n=NSEQ),
                                cv_d[l].rearrange("n j f -> j n f"), writes=[b_IO[1]])
                          CK = IO[0][:].bitcast(BF16)[:, 0:NSEQ * 128].rearrange("p (n f) -> p n f", n=NSEQ)
                          CV = IO[1][:].bitcast(BF16)[:, 0:NSEQ * 128].rearrange("p (n f) -> p n f", n=NSEQ)
                      wts = {}

                      def mk_norm(tl):
                          t = tl["t"]

                          def fA():
                              if (l, t) not in norm_pre and (l, t) not in norm_done:
                                  norm_AB(l, tl)

                          def fC():
                              if (l, t) not in norm_done:
                                  norm_C(l, tl)
                              norm_done.discard((l, t))
                          return [fA, (lambda: None), fC, (lambda: None)]

                      def mk_q(tl, g, first, last):
                          t, c0, n = tl["t"], tl["c0"], tl["n"]
                          tc_ = slice(c0, c0 + n)
                          st_ = {}

                          def fA():
                              wq, bwq = wts["q"]
                              ps, bp = psum_lo.get()
                              inproj(ps[:, 0:n], bp, wq, bwq, g * 128, 128, tc_, t)
                              sq, bsq = bts.get()
                              S.op("act", lambda e: e.activation(sq[:, 0:n], ps[:, 0:n], AF.Square), reads=[bp], writes=[bsq])
                              st_.update(ps=ps, bp=bp, sq=sq, bsq=bsq)

                          def fB():
                              st_["rr"] = rstd_part1a([st_["sq"][:, 0:n]], [st_["bsq"]], n, 2, BD64[:, 0:128:64], 1.0 / 64)

                          def fR():
                              st_["r2l"] = rstd_part1b(st_["rr"][0], st_["rr"][1], n, 2)

                          def fC():
                              r, br, _ = rstd_part2(st_["r2l"], n, True)
                              ps, bp = st_["ps"], st_["bp"]
                              S.op("dve", lambda e: e.scalar_tensor_tensor(
                                  QT[:, g, tc_], ps[:, 0:n], QW[:, l:l + 1], r[:, 0:n], op0=ALU.mult, op1=ALU.mult),
                                  reads=[bp, br, b_const], writes=[b_PB[t]])
                          return [fA, fB, fR, fC]

                      def mk_k(tl, first, last):
                          t, c0, n = tl["t"], tl["c0"], tl["n"]
                          tc_ = slice(c0, c0 + n)
                          st_ = {}

                          def fA():
                              wkv, bwkv = wts["kv"]
                              ps, bp = psum_lo.get()
                              inproj(ps[:, 0:n], bp, wkv, bwkv, 0, 128, tc_, t)
                              sq, bsq = bts.get()
                              S.op("act", lambda e: e.activation(sq[:, 0:n], ps[:, 0:n], AF.Square), reads=[bp], writes=[bsq])
                              st_.update(ps=ps, bp=bp, sq=sq, bsq=bsq)

                          def fB():
                              st_["rr"] = rstd_part1a([st_["sq"][:, 0:n]], [st_["bsq"]], n, 2, BD64[:, 0:128:64], 1.0 / 64)

                          def fR():
                              st_["r2l"] = rstd_part1b(st_["rr"][0], st_["rr"][1], n, 2)

                          def fC():
                              r, br, pfree = rstd_part2(st_["r2l"], n, True)
                              ps, bp = st_["ps"], st_["bp"]
                              kslots = [b_KT[1 + (c0 // 128) + i] for i in range(n // 128)]
                              S.op("dve", lambda e: e.scalar_tensor_tensor(
                                  KTe[:, 128 + c0:128 + c0 + n], ps[:, 0:n], KW[:, l:l + 1], r[:, 0:n], op0=ALU.mult, op1=ALU.mult),
                                  reads=[bp, br, b_const], writes=kslots)
                              for bi, gb in enumerate(tl["blocks"]):
                                  if (gb == "S") or (gb == last_gb):
                                      kf_, bkf = tmps.get()
                                      KF = kf_[:, 0:128]
                                      S.op("dve", lambda e, bi=bi: e.scalar_tensor_tensor(
                                          KF, ps[:, bi * 128:(bi + 1) * 128], KW[:, l:l + 1], r[:, bi * 128:(bi + 1) * 128], op0=ALU.mult, op1=ALU.mult),
                                          reads=[bp, br, b_const], writes=[bkf])
                                      bko = buf("KOs")
                                      transpose_out(KF, bkf, 128, 128, KOs, bko, slice(0, 128), ps_lease=pfree)
                                      if gb == "S":
                                          for nn in range(NSEQ):
                                              S.dma("sp", "kout", sk_o[l, nn, 128 - TS:128, :], KOs[nn * TS:(nn + 1) * TS, :], reads=[bko])
                                      else:
                                          S.dma("sp", "kout", pk_o[l], KOs[:], reads=[bko])
                          return [fA, fB, fR, fC]

                      def mk_v(tl, last):
                          t, c0, n = tl["t"], tl["c0"], tl["n"]

                          def fA():
                              wkv, bwkv = wts["kv"]
                              psv, bpv = psum_lo.get()
                              for bi, gb in enumerate(tl["blocks"]):
                                  bc = c0 + bi * 128
                                  mm_group(psv[:, bi * 128:(bi + 1) * 128], bpv, [(xnT[:, k, bc:bc + 128], wkv[:, k, 128:256]) for k in range(NCH)],
                                           reads=[bwkv, b_xnT[t]])
                              vs0 = 1 + c0 // 128
                              nb = n // 128
                              S.op("act", lambda e: e.copy(Ve[:, vs0:vs0 + nb, :], psv[:, 0:n].rearrange("p (b f) -> p b f", b=nb)),
                                   reads=[bpv], writes=[b_V[vs0 + i] for i in range(nb)])
                              for bi, gb in enumerate(tl["blocks"]):
                                  if (gb == "S") or (gb == last_gb):
                                      bvo = buf("KOs")
                                      S.op("act", lambda e, bi=bi: e.copy(VOs[:], psv[:, bi * 128:(bi + 1) * 128]), reads=[bpv], writes=[bvo])
                                      if gb == "S":
                                          for nn in range(NSEQ):
                                              S.dma("sp", "vout", sv_o[l, nn, 128 - TS:128, :], VOs[nn * TS:(nn + 1) * TS, :], reads=[bvo])
                                      else:
                                          S.dma("sp", "vout", pv_o[l], VOs[:], reads=[bvo])

                          return [fA, (lambda: None), (lambda: None), (lambda: None)]

                      wts["q"] = piece("q")
                      wts["kv"] = piece("kv")
                      items = [mk_norm(tl) for tl in tiles_kv]
                      full_t = {tl["t"]: tl for tl in tiles}
                      rest = []
                      for i_norm, tk in enumerate(tiles_kv):
                          if tk["t"] in full_t:
                              rest += [(i_norm, mk_q(full_t[tk["t"]], g, False, False)) for g in range(4)]
                          rest.append((i_norm, mk_k(tk, False, False)))
                          rest.append((i_norm, mk_v(tk, False)))
                      pad = 1
                      for p_, (i_norm, _) in enumerate(rest):
                          pad = max(pad, i_norm + 3 - (len(tiles_kv) + p_))
                      for _ in range(pad):
                          items.append([(lambda: None)] * 4)
                      items += [it for _, it in rest]
                      run_pipeline(items, [0, 1, 2, 3])
                      piece_done()
                      if has_sample:
                          b_CKT = buf("CKT")
                          for q4 in range(NSEQ // 4):
                              ps, bp = psum.get()
                              psb = ps.bitcast(BF16)
                              for i in range(4):
                                  nn = q4 * 4 + i
                                  S.op("pe", lambda e, nn=nn, i=i, psb=psb: e.transpose(psb[:, i * 128:(i + 1) * 128], CK[:, nn, :], IDENTB[:]),
                                       reads=[b_IO[0], b_const], writes=[bp])
                              S.op("act", lambda e, q4=q4, psb=psb: e.copy(CKT[:, q4 * 4:(q4 + 1) * 4, :], psb[:, 0:512].rearrange("p (n f) -> p n f", n=4)),
                                   reads=[bp], writes=[b_CKT])


                      stage('Akv')
                      def mk_att(tl, bi, gb, kv):
                          t, c0, n = tl["t"], tl["c0"], tl["n"]
                          bc = c0 + bi * 128
                          bslot = bc // 128
                          pr = slice(64 * kv, 64 * kv + 64)
                          st_ = {}

                          def score(lhsT, rhs_ap, reads, bias_idx, bias_buf, out_view=None):
                              ps, bp = psum.get()
                              S.op("pe", lambda e: e.matmul(ps[:].rearrange("p (g t) -> p g t", g=4), lhsT, rhs_ap, start=True, stop=True),
                                   reads=reads, writes=[bp])
                              return ps, bp

                          def softexp(ps, bp, bias_idx, bias_buf):
                              sbm, bsb = tmps.get()
                              S.op("act", lambda e: e.activation(sbm[:], ps[:], AF.Exp, scale=0.125), reads=[bp], writes=[bsb])
                              pt, bpt = bts.get()
                              S.op("pool", lambda e: e.tensor_tensor(pt[:], sbm[:], BIAS[:, bias_idx, :], op=ALU.mult),
                                   reads=[bsb, bias_buf], writes=[bpt])
                              return pt, bpt

                          def fA():
                              if gb == "S":
                                  ps, bp = psum.get()
                                  for nn in range(NSEQ):
                                      S.op("pe", lambda e, nn=nn: e.matmul(
                                          ps[:, nn * 32:(nn + 1) * 32].rearrange("p (g t) -> p g t", g=4),
                                          CKT[pr, nn, :], QT[pr, :, bc + nn * TS:bc + (nn + 1) * TS], start=True, stop=True),
                                          reads=[b_CKT, b_PB[t]], writes=[bp])
                                  st_["ptc"] = softexp(ps, bp, 4 + kv, b_biasx)
                                  ps, bp = score(KTe[pr, 128 + bc:128 + bc + 128], QT[pr, :, bc:bc + 128], [b_KT[bslot + 1], b_PB[t]], None, None)
                                  st_["ptn"] = softexp(ps, bp, 6 + kv, b_biasx)
                              else:
                                  first = (si == 0 and gb == first_out_gb)
                                  kbs = [(bslot, (4 + kv) if first else kv, b_biasx if first else b_const), (bslot + 1, 2 + kv, b_const)]
                                  pts = []
                                  for (slot, bidx, bb) in kbs:
                                      ps, bp = score(KTe[pr, slot * 128:(slot + 1) * 128], QT[pr, :, bc:bc + 128], [b_KT[slot], b_PB[t]], None, None)
                                      pt, bpt = softexp(ps, bp, bidx, bb)
                                      pts.append((pt, bpt, slot))
                                  st_["pts"] = pts

                          def fB():
                              pso, bpo = psum.get()
                              if gb == "S":
                                  ptc, bptc = st_["ptc"]
                                  ptn, bptn = st_["ptn"]
                                  ptn3 = ptn[:].rearrange("p (g t) -> p g t", g=4)
                                  ptc4 = ptc[:].rearrange("p (n g t) -> p n g t", n=NSEQ, g=4)
                                  for par in range(2):
                                      orow = slice(64 * par, 64 * par + 64)
                                      tp = dict(tile_position=(0, 64)) if par == 1 else {}
                                      for which in range(2):
                                          ocol = 256 * which
                                          oap = pso[orow, ocol:ocol + 256].rearrange("p (c q) -> p c q", c=2)
                                          lhs_new = Ve[:, bslot + 1, pr] if which == 0 else ONESB[:, 0:64]
                                          S.op("pe", lambda e: e.matmul(
                                              oap, lhs_new, ptn3[:, par::2, :], start=True, stop=False, skip_group_check=True, **tp),
                                              reads=[bptn, b_V[bslot + 1], b_const], writes=[bpo])
                                          for nn in range(NSEQ):
                                              lhs_c = CV[:, nn, pr] if which == 0 else ONESB[:, 0:64]
                                              S.op("pe", lambda e, nn=nn, lhs_c=lhs_c: e.matmul(
                                                  oap[:, :, nn * TS:(nn + 1) * TS], lhs_c, ptc4[:, nn, par::2, :],
                                                  start=False, stop=(nn == NSEQ - 1), skip_group_check=True, **tp),
                                                  reads=[bptc, b_IO[1], b_const], writes=[bpo])
                              else:
                                  pts = st_["pts"]
                                  for par in range(2):
                                      orow = slice(64 * par, 64 * par + 64)
                                      tp = dict(tile_position=(0, 64)) if par == 1 else {}
                                      for which in range(2):
                                          ocol = 256 * which
                                          oap = pso[orow, ocol:ocol + 256].rearrange("p (c q) -> p c q", c=2)
                                          for i, (pt, bpt, slot) in enumerate(pts):
                                              lhs = Ve[:, slot, pr] if which == 0 else ONESB[:, 0:64]
                                              pt3 = pt[:].rearrange("p (g t) -> p g t", g=4)
                                              S.op("pe", lambda e, lhs=lhs, pt3=pt3, i=i: e.matmul(
                                                  oap, lhs, pt3[:, par::2, :], start=(i == 0), stop=(i == 1), skip_group_check=True, **tp),
                                                  reads=[bpt, b_V[slot], b_const], writes=[bpo])
                              ds, bds = smalls.get()
                              for c2 in range(2):
                                  S.op("dve", lambda e, c2=c2: e.tensor_scalar(
                                      ds[:, c2 * 128:(c2 + 1) * 128], pso[:, 256 + c2 * 128:256 + (c2 + 1) * 128],
                                      SE4[:, l * 4 + kv * 2 + c2:l * 4 + kv * 2 + c2 + 1], None, op0=ALU.add),
                                      reads=[bpo, b_const], writes=[bds])
                              S.op("dve", lambda e: e.reciprocal(ds[:], ds[:]), reads=[bds], writes=[bds])
                              S.op("dve", lambda e: e.tensor_tensor(
                                  YX[:, kv * 2:(kv + 1) * 2, bc:bc + 128], pso[:, 0:256].rearrange("p (c q) -> p c q", c=2),
                                  ds[:].rearrange("p (c q) -> p c q", c=2), op=ALU.mult),
                                  reads=[bpo, bds], writes=[b_MB] + b_MBt)
                          return [fA, (lambda: None), fB]

                      items = [mk_att(tl, bi, gb, kv) for tl in tiles for bi, gb in enumerate(tl["blocks"]) for kv in range(2)]
                      run_pipeline(items, [0, 1, 2])

                      stage('Aattn')
                      wza, bwza = piece("za")
                      for tl in tiles:
                          t, c0, n = tl["t"], tl["c0"], tl["n"]
                          tc_ = slice(c0, c0 + n)
                          for c2 in range(4):
                              ps, bp = psum.get()
                              inproj(ps[:, 0:n], bp, wza, bwza, c2 * 128, 128, tc_, t)
                              t2, bt2 = silu2(ps[:, 0:n], bp, n)
                              S.op("pool", lambda e, t2=t2, c2=c2: e.tensor_tensor(PA[:, c2, tc_], t2[:, 0:n], YX[:, c2, tc_], op=ALU.mult),
                                   reads=[bt2, b_MB], writes=[b_PA[t]])
                      piece_done()
                      lp = n_prompt_blk
                      S.op("pool", lambda e: e.tensor_copy(KTb[:, l, :], KTe[:, lp * 128:(lp + 1) * 128]), reads=[b_KT[lp]], writes=[b_KTb[l]])
                      S.op("pool", lambda e: e.tensor_copy(Vb[:, l, :], Ve[:, lp, :]), reads=[b_V[lp]], writes=[b_Vb[l]])

                      stage('A')
                      if has_sample:
                          bsc = buf("SCin")
                          S.dma("sp", "scin", SCs[:], sconv_d[l], writes=[bsc])
                      for c in range(4):
                          wb, bwb = piece("b%d" % c)
                          S.op("dve", lambda e, c=c: e.tensor_copy(CE[:, 0:2], CEb[:, l, c, :]), reads=[b_CEb[l]], writes=[b_CE])
                          if has_sample:
                              ps, bp = psum.get()
                              S.op("pe", lambda e, ps=ps, c=c: e.transpose(ps[:, 0:32], SCs[:, c * 128:(c + 1) * 128], IDENT[64:96, 64:96]),
                                   reads=[bsc, b_const], writes=[bp])
                              S.op("act", lambda e, ps=ps: e.copy(CES[:, :, 0:2], ps[:, 0:32].rearrange("p (n r) -> p n r", r=2)),
                                   reads=[bp], writes=[b_CES])
                          for tl in tiles_kv:
                              t, c0, n = tl["t"], tl["c0"], tl["n"]
                              tc_ = slice(c0, c0 + n)
                              deferred = []
                              ps1, bp1 = psum.get()
                              inproj(ps1[:, 0:n], bp1, wb, bwb, 0, 128, tc_, t)
                              gc, bgc = tmps.get()
                              S.op("act", lambda e, ps1=ps1, gc=gc: e.copy(gc[:, 0:n], ps1[:, 0:n]), reads=[bp1], writes=[bgc])
                              ps2, bp2 = psum.get()
                              inproj(ps2[:, 0:n], bp2, wb, bwb, 128, 128, tc_, t)
                              co, bco = tmps.get()
                              if tl["kind"] == "P":
                                  S.op("dve", lambda e, ps2=ps2, gc=gc: e.tensor_tensor(CE[:, 2 + c0:2 + c0 + n], ps2[:, 0:n], gc[:, 0:n], op=ALU.mult),
                                       reads=[bp2, bgc], writes=[b_CE])
                                  for r_ in range(3):
                                      src = CE[:, c0 + r_:c0 + r_ + n]
                                      wcol = CW[:, l * 12 + r_ * 4 + c:l * 12 + r_ * 4 + c + 1]
                                      if r_ == 0:
                                          S.op("dve", lambda e, src=src, wcol=wcol, co=co: e.tensor_scalar(co[:, 0:n], src, wcol, None, op0=ALU.mult),
                                               reads=[b_CE, b_const], writes=[bco])
                                      else:
                                          S.op("dve", lambda e, src=src, wcol=wcol, co=co: e.scalar_tensor_tensor(
                                              co[:, 0:n], src, wcol, co[:, 0:n], op0=ALU.mult, op1=ALU.add),
                                              reads=[b_CE, b_const, bco], writes=[bco])
                                  if last_gb in tl["blocks"]:
                                      e0 = 2 + c0 + n - 2
                                      bpc = buf("PCs")
                                      deferred.append(lambda e0=e0, bpc=bpc, c=c: transpose_out(
                                          CE[:, e0:e0 + 2], b_CE, 128, 2, PCs, bpc, slice(c * 128, (c + 1) * 128)))
                              else:
                                  ces_in = CES[:, :, 2:10]
                                  S.op("dve", lambda e, ps2=ps2, gc=gc: e.tensor_tensor(
                                      ces_in, ps2[:, 0:n].rearrange("p (n t) -> p n t", t=TS), gc[:, 0:n].rearrange("p (n t) -> p n t", t=TS), op=ALU.mult),
                                      reads=[bp2, bgc], writes=[b_CES])
                                  co3 = co[:, 0:n].rearrange("p (n t) -> p n t", t=TS)
                                  for r_ in range(3):
                                      src = CES[:, :, r_:r_ + TS]
                                      wcol = CW[:, l * 12 + r_ * 4 + c:l * 12 + r_ * 4 + c + 1]
                                      if r_ == 0:
                                          S.op("dve", lambda e, src=src, wcol=wcol, co3=co3: e.tensor_scalar(co3, src, wcol, None, op0=ALU.mult),
                                               reads=[b_CES, b_const], writes=[bco])
                                      else:
                                          S.op("dve", lambda e, src=src, wcol=wcol, co3=co3: e.scalar_tensor_tensor(
                                              co3, src, wcol, co3, op0=ALU.mult, op1=ALU.add),
                                              reads=[b_CES, b_const, bco], writes=[bco])
                                  bc32 = buf("C32")
                                  S.op("dve", lambda e: e.tensor_copy(C32[:].rearrange("p (n r) -> p n r", r=2), CES[:, :, 8:10]),
                                       reads=[b_CES], writes=[bc32])
                                  bscs = buf("SCs2")
                                  deferred.append(lambda bc32=bc32, bscs=bscs, c=c: transpose_out(
                                      C32[:], bc32, 128, 32, SCo, bscs, slice(c * 128, (c + 1) * 128)))
                              ps3, bp3 = psum.get()
                              inproj(ps3[:, 0:n], bp3, wb, bwb, 256, 128, tc_, t)
                              S.op("dve", lambda e, ps3=ps3, co=co: e.tensor_tensor(co[:, 0:n], ps3[:, 0:n], co[:, 0:n], op=ALU.mult),
                                   reads=[bp3, bco], writes=[bco])
                              ps4, bp4 = psum.get()
                              inproj(ps4[:, 0:n], bp4, wb, bwb, 384, 128, tc_, t)
                              for fn_ in deferred:
                                  fn_()
                              t2, bt2 = silu2(ps4[:, 0:n], bp4, n)
                              S.op("pool", lambda e, t2=t2, co=co, c=c: e.tensor_tensor(PB[:, c, tc_], t2[:, 0:n], co[:, 0:n], op=ALU.mult),
                                   reads=[bt2, bco], writes=[b_PB[t]])
                          e0 = 2 + n_prompt_blk * 128 - 2
                          S.op("dve", lambda e, c=c, e0=e0: e.tensor_copy(CEb[:, l, c, :], CE[:, e0:e0 + 2]), reads=[b_CE], writes=[b_CEb[l]])
                          piece_done()
                      if last_gb in [gb for tl in tiles_kv for gb in tl["blocks"]]:
                          S.dma("sp", "pcout", pc_o[l], PCs[:], reads=[buf("PCs")])
                      if has_sample:
                          S.dma("sp", "scout", sc_o[l], SCo[:], reads=[buf("SCs2")])

                      stage('B')
                      wvc, bwvc = piece("vc")

                      def mk_vc(tl, bi, gb):
                          t, c0, n = tl["t"], tl["c0"], tl["n"]
                          bc = c0 + bi * 128
                          st_ = {}

                          def fA():
                              psv, bpv = psum.get()
                              mm_group(psv[:], bpv, [(xnT[:, k, bc:bc + 128], wvc[:, k, :]) for k in range(NCH)], reads=[bwvc, b_xnT[t]])
                              junk, bj = tmps.get()
                              SS, bss = sss.get()
                              S.op("act", lambda e: e.activation(junk[:], psv[:], AF.Square, accum_out=SS[:, 0:1]),
                                   reads=[bpv], writes=[bj, bss])
                              S.op("dve", lambda e: e.tensor_scalar(SS[:, 1:2], SS[:, 0:1], 1.0 / 512, EPS, op0=ALU.mult, op1=ALU.add),
                                   reads=[bss], writes=[bss])
                              S.op("pool", lambda e: e.tensor_tensor(SS[:, 2:3], SS[:, 1:2], NEGH[:, 0:1], op=ALU.pow), reads=[bss, b_const], writes=[bss])
                              st_.update(psv=psv, bpv=bpv, SS=SS, bss=bss)

                          def fB():
                              psv, bpv, SS, bss = st_["psv"], st_["bpv"], st_["SS"], st_["bss"]
                              vcn, bvcn = bts.get()
                              S.op("dve", lambda e: e.scalar_tensor_tensor(
                                  vcn[:], psv[:], SS[:, 2:3], VNW[:], op0=ALU.mult, op1=ALU.mult), reads=[bpv, bss, b_lay], writes=[bvcn])
                              if gb == "S":
                                  vf, bvf = tmps.get()
                                  S.op("dve", lambda e: e.scalar_tensor_tensor(
                                      vf[:], psv[:], SS[:, 2:3], VNW[:], op0=ALU.mult, op1=ALU.mult), reads=[bpv, bss, b_lay], writes=[bvf])
                                  S.dma("sp", "scvout", scv_o[l], vf[:], reads=[bvf])
                              st_.update(vcn=vcn, bvcn=bvcn)

                          def fC():
                              vcn, bvcn = st_["vcn"], st_["bvcn"]
                              pss, bps = psum.get()
                              wm = WMTS if gb == "S" else WMT
                              for g in range(4):
                                  S.op("pe", lambda e, g=g: e.matmul(
                                      pss[:, g * 128:(g + 1) * 128], vcn[:, g * 128:(g + 1) * 128], wm[:, l, g, :], start=True, stop=True),
                                      reads=[bvcn, b_const], writes=[bps])
                              bt_ = BSPTS if gb == "S" else BSPT
                              S.op("dve", lambda e: e.tensor_tensor(
                                  YX[:, :, bc:bc + 128], pss[:].rearrange("p (g t) -> p g t", g=4), bt_[:].rearrange("p (g t) -> p g t", g=4), op=ALU.add),
                                  reads=[bps, b_lay], writes=[b_MB] + b_MBt)
                          return [fA, fB, fC]

                      items = [mk_vc(tl, bi, gb) for tl in tiles for bi, gb in enumerate(tl["blocks"])]
                      run_pipeline(items, [1, 0, 2])
                      piece_done()
                      wu, bwu = piece("u")
                      for tl in tiles:
                          t, c0, n = tl["t"], tl["c0"], tl["n"]
                          tc_ = slice(c0, c0 + n)
                          for g in range(4):
                              ps, bp = psum.get()
                              inproj(ps[:, 0:n], bp, wu, bwu, g * 128, 128, tc_, t)
                              S.op("dve", lambda e, ps=ps, g=g: e.tensor_tensor(YX[:, g, tc_], ps[:, 0:n], YX[:, g, tc_], op=ALU.mult),
                                   reads=[bp, b_MB], writes=[b_MB] + b_MBt)
                      piece_done()
                      wzc, bwzc = piece("zc")
                      for tl in tiles:
                          t, c0, n = tl["t"], tl["c0"], tl["n"]
                          tc_ = slice(c0, c0 + n)
                          for g in range(4):
                              ps, bp = psum.get()
                              inproj(ps[:, 0:n], bp, wzc, bwzc, g * 128, 128, tc_, t)
                              t2, bt2 = silu2(ps[:, 0:n], bp, n)
                              S.op("pool", lambda e, t2=t2, g=g: e.tensor_tensor(PC[:, g, tc_], t2[:, 0:n], YX[:, g, tc_], op=ALU.mult),
                                   reads=[bt2, b_MB], writes=[b_PC[t]])
                      piece_done()

                      stage('C')
                      Ps = [(PA, b_PA), (PB, b_PB), (PC, b_PC)]
                      for j in range(8):
                          wg, bwg = piece("g%d" % j)
                          for tl in tiles:
                              t, c0, n = tl["t"], tl["c0"], tl["n"]
                              tc_ = slice(c0, c0 + n)
                              acc = None
                              for i in range(3):
                                  Pi, bPi = Ps[i]
                                  psa, bpa = psum.get()
                                  mm_group(psa[:, 0:n], bpa, [(wg[:, 8 + k, i * 128:(i + 1) * 128], Pi[:, k, tc_]) for k in range(4)],
                                           reads=[bwg, bPi[t]])
                                  psg, bpg = psum.get()
                                  inproj(psg[:, 0:n], bpg, wg, bwg, i * 128, 128, tc_, t)
                                  tg, btg = tmps.get()
                                  bcol = l * 24 + i * 8 + j
                                  S.op("act", lambda e, psg=psg, tg=tg, bcol=bcol: e.activation(
                                      tg[:, 0:n], psg[:, 0:n], AF.Tanh, bias=BGH[:, bcol:bcol + 1], scale=0.5), reads=[bpg, b_const], writes=[btg])
                                  S.op("dve", lambda e, psa=psa, tg=tg: e.scalar_tensor_tensor(
                                      tg[:, 0:n], tg[:, 0:n], 1.0, psa[:, 0:n], op0=ALU.add, op1=ALU.mult), reads=[btg, bpa], writes=[btg])
                                  if i == 0:
                                      acc, bacc = tg, btg
                                  elif i == 1:
                                      S.op("pool", lambda e, acc=acc, tg=tg: e.tensor_tensor(acc[:, 0:n], acc[:, 0:n], tg[:, 0:n], op=ALU.add),
                                           reads=[bacc, btg], writes=[bacc])
                                  else:
                                      S.op("pool", lambda e, acc=acc, tg=tg, j=j: e.tensor_tensor(MB[:, j, tc_], acc[:, 0:n], tg[:, 0:n], op=ALU.add),
                                           reads=[bacc, btg], writes=[b_MB, b_MBt[t]])
                          piece_done()

                      stage('G')
                      for h in range(2):
                          wo, bwo = piece("o%d" % h)
                          for tl in tiles:
                              t, c0, n = tl["t"], tl["c0"], tl["n"]
                              tc_ = slice(c0, c0 + n)
                              for e4 in range(4):
                                  ec = h * 4 + e4
                                  ps, bp = psum.get()
                                  mm_group(ps[:, 0:n], bp, [(wo[:, k, e4 * 128:(e4 + 1) * 128], MB[:, k, tc_]) for k in range(NCH)],
                                           reads=[bwo, b_MBt[t]])
                                  S.op("dve", lambda e, ps=ps, ec=ec: e.scalar_tensor_tensor(
                                      xT[:, ec, tc_], ps[:, 0:n], 0.25, xT[:, ec, tc_], op0=ALU.mult, op1=ALU.add),
                                      reads=[bp, b_xT[t][ec]], writes=[b_xT[t][ec]])
                                  if h == 1 and l + 1 < L and e4 == 0:
                                      for tn in tiles_at(l + 1)[0]:
                                          if (l + 1, tn["t"]) in norm_pre:
                                              norm_C(l + 1, tn)
                              if h == 1 and l + 1 < L:
                                  for tn in tiles_at(l + 1)[0]:
                                      if tn["t"] == t:
                                          norm_AB(l + 1, tn)
                          piece_done()

                  stage('O')
                  io_i = 0
                  for tl in tiles_full:
                      for bi, gb in enumerate(tl["blocks"]):
                          if gb != "S" and gb < first_out_gb:
                              continue
                          c0 = tl["c0"] + bi * 128
                          io, bio = IO[io_i % 2], b_IO[io_i % 2]
                          for h in range(2):
                              ps, bp = psum.get()
                              for c in range(4):
                                  cc = h * 4 + c
                                  S.op("pe", lambda e, cc=cc, c=c, ps=ps, c0=c0: e.transpose(ps[:, c * 128:(c + 1) * 128], xT[:, cc, c0:c0 + 128], IDENT[:]),
                                       reads=[b_xT[tl["t"]][cc], b_const], writes=[bp])
                              if h == 0:
                                  S.op("act", lambda e, ps=ps, io=io: e.copy(io[:, 0:512], ps[:]), reads=[bp], writes=[bio])
                              else:
                                  S.op("dve", lambda e, ps=ps, io=io: e.tensor_copy(io[:, 512:1024], ps[:]), reads=[bp], writes=[bio])
                          dst = ys_o if gb == "S" else yp_o[(gb - first_out_gb) * 128:(gb - first_out_gb + 1) * 128, :]
                          S.dma("sp", "io%d" % (io_i % 2), dst, io[:], reads=[bio])
                          io_i += 1

            except StopBuild:
                pass
            if stop_after is None:
                assert wstate["cur"] == len(schedule) - 1
            S.wait_all_dma("sp")
            for e in ("pe", "act", "dve", "pool"):
                pass
            build.stats = dict(nops=dict(S.nops), nwaits=S.nwaits, nsig=dict(S.sig))

        dry = Sched(nc, ctx, signal=None)
        emit_all(dry)
        emit_all(Sched(nc, ctx, signal=dry.signal))
    return nc


def _piece_cols():
    cols = {}
    qperm = np.empty(512, np.int64)
    for g in range(4):
        for kv in range(2):
            qperm[g * 128 + kv * 64:g * 128 + kv * 64 + 64] = O_Q + (kv * 4 + g) * 64 + np.arange(64)
    cols["q"] = qperm
    cols["kv"] = np.concatenate([O_K + np.arange(128), O_V + np.arange(128)])
    cols["za"] = O_ZA + np.arange(512)
    for c in range(4):
        r = np.arange(128) + c * 128
        cols["b%d" % c] = np.concatenate([O_GC + r, O_HB + r, O_GB + r, O_ZB + r])
    cols["vc"] = O_VC + np.arange(512)
    cols["u"] = O_U + np.arange(512)
    cols["zc"] = O_ZC + np.arange(512)
    for j in range(8):
        r = np.arange(128) + j * 128
        cols["g%d" % j] = np.concatenate([O_G + r, O_G + 1024 + r, O_G + 2048 + r])
    return cols


def pack_weights(w_in, w_out_a, w_out_b, w_out_c, w_o):
    L = w_in.shape[0]
    cols = _piece_cols()
    wp = np.empty((L, 128, PCOLS), np.float32)
    for l in range(L):
        for (pn, kk, g) in PIECES:
            off = PIECE_OFF[pn][0]
            if pn.startswith("g"):
                j = int(pn[1:])
                a = w_in[l][:, cols[pn]].reshape(8, 128, g).transpose(1, 0, 2)
                wo = np.concatenate([w_out_a[l][:, j * 128:(j + 1) * 128], w_out_b[l][:, j * 128:(j + 1) * 128],
                                     w_out_c[l][:, j * 128:(j + 1) * 128]], axis=1)
                b = wo.reshape(4, 128, g).transpose(1, 0, 2)
                blk = np.concatenate([a, b], axis=1)
            elif pn.startswith("o"):
                h = int(pn[1:])
                blk = w_o[l][:, h * 512:(h + 1) * 512].reshape(8, 128, g).transpose(1, 0, 2)
            else:
                blk = w_in[l][:, cols[pn]].reshape(8, 128, g).transpose(1, 0, 2)
            wp[l, :, off:off + kk * g] = blk.reshape(128, kk * g)
    return wp


def attn_bias_tiles(first_masked):
    h = np.arange(1, 9, dtype=np.float64)
    slopes = np.exp2(-8.0 * h / 8).reshape(2, 4)
    out = np.empty((10, 128, 512), np.float32)
    j = np.arange(128)[:, None]
    i = np.arange(128)[None, :]
    for kv in range(2):
        for g in range(4):
            s = slopes[kv, g]
            dp = i + 128 - j
            out[0 + kv][:, g * 128:(g + 1) * 128] = np.where(dp < 128, -s * dp, NEG)
            dc = i - j
            out[2 + kv][:, g * 128:(g + 1) * 128] = np.where(dc >= 0, -s * dc, NEG)
        out[4 + kv] = NEG if first_masked else out[0 + kv]
        t = np.arange(TS)[None, :]
        for g in range(4):
            s = slopes[kv, g]
            d = t + 128 - j
            tile_ = np.where(j > t, -s * d, NEG)
            for n in range(NSEQ):
                out[6 + kv][:, n * 32 + g * 8:n * 32 + g * 8 + 8] = tile_
        rn = (np.arange(128) // TS)[:, None]
        rs = (np.arange(128) % TS)[:, None]
        cn = (np.arange(128) // TS)[None, :]
        ct = (np.arange(128) % TS)[None, :]
        for g in range(4):
            s = slopes[kv, g]
            ok = (rn == cn) & (rs <= ct)
            out[8 + kv][:, g * 128:(g + 1) * 128] = np.where(ok, -s * (ct - rs), NEG)
    return np.exp(out.astype(np.float64)).astype(np.float32)


def host_common(inputs, L):
    f = np.float32
    norm_w, b_gate = inputs["norm_w"], inputs["b_gate"]
    c = {}
    c["wp"] = pack_weights(inputs["w_in"], inputs["w_out_a"], inputs["w_out_b"], inputs["w_out_c"], inputs["w_o"])
    c["nw"] = np.ascontiguousarray(norm_w.reshape(L, 8, 128).transpose(2, 0, 1).reshape(128, L * 8)).astype(f)
    c["bg"] = np.ascontiguousarray(b_gate.reshape(L, 24, 128).transpose(2, 0, 1).reshape(128, L * 24)).astype(f)
    c["qw"] = np.ascontiguousarray(np.tile(inputs["q_norm_w"], (1, 2)).T).astype(f)
    c["kw"] = np.ascontiguousarray(np.tile(inputs["k_norm_w"], (1, 2)).T).astype(f)
    sk = np.empty((128, L * 4), f)
    for l in range(L):
        for kv in range(2):
            for c2 in range(2):
                for par in range(2):
                    sk[par * 64:(par + 1) * 64, l * 4 + kv * 2 + c2] = inputs["sinks"][l, kv * 4 + 2 * c2 + par]
    c["sk4"] = sk
    c["cw"] = np.ascontiguousarray(inputs["conv_w"].reshape(L, 3, 4, 128).transpose(3, 0, 1, 2).reshape(128, L * 12)).astype(f)
    c["vnw"] = np.ascontiguousarray(np.broadcast_to(inputs["v_norm_w"][:, None, :], (L, 128, 512))).astype(f)
    c["wsp"] = np.ascontiguousarray(inputs["w_spatial"]).astype(f)
    bs = inputs["b_spatial"]
    c["bspt"] = np.ascontiguousarray(np.broadcast_to(bs.reshape(L, 1, 512), (L, 128, 512))).astype(f)
    bss = np.tile(bs[:, :, :TS], (1, 1, NSEQ))
    c["bspts"] = np.ascontiguousarray(np.broadcast_to(bss.reshape(L, 1, 512), (L, 128, 512))).astype(f)
    c["ident"] = np.eye(128, dtype=f)
    s_ = np.arange(128)
    c["tri"] = (s_[:, None] <= s_[None, :]).astype(f)
    c["bdm"] = ((s_[:, None] // TS) == (s_[None, :] // TS)).astype(f)
    c["bd64"] = ((s_[:, None] // 64) == (s_[None, :] // 64)).astype(f)
    c["repi"] = (np.arange(8)[:, None] == (s_[None, :] % TS)).astype(f)
    return c


def kernel(**inputs):
    L = 4
    n_cores = 8
    x_prompt = np.asarray(inputs["x_prompt"], np.float32)
    x_sample = np.asarray(inputs["x_sample"], np.float32)
    inputs = {k: np.asarray(v) for k, v in inputs.items()}
    B_, S_, _ = x_prompt.shape
    npb, halo = 20, 4
    cores_per_b = n_cores // B_
    tok_per_core = S_ // cores_per_b
    common = host_common(inputs, L)
    bias_mid = attn_bias_tiles(False)
    bias_first = attn_bias_tiles(True)
    in_maps = []
    for c in range(n_cores):
        b, q = divmod(c, cores_per_b)
        t0 = q * tok_per_core
        xp = np.zeros((npb * 128, D), np.float32)
        if q == 0:
            xp[halo * 128:] = x_prompt[b, t0:t0 + tok_per_core]
        else:
            xp[:] = x_prompt[b, t0 - halo * 128:t0 + tok_per_core]
        n0 = c * NSEQ
        m = dict(common)
        m["xp"] = xp
        m["xs"] = np.ascontiguousarray(x_sample[n0:n0 + NSEQ].reshape(NSEQ * TS, D))
        m["ck"] = np.ascontiguousarray(inputs["cache_k"][:, n0:n0 + NSEQ].reshape(L, NSEQ, 128, 128))
        m["cv"] = np.ascontiguousarray(inputs["cache_v"][:, n0:n0 + NSEQ].reshape(L, NSEQ, 128, 128))
        m["sconv"] = np.ascontiguousarray(inputs["state_conv"][:, n0:n0 + NSEQ].reshape(L, NSEQ * 2, 512))
        m["biasall"] = bias_first if q == 0 else bias_mid
        in_maps.append(m)
    nc = build(L, default_sts(), npb, halo)
    res = run_bass_kernel_spmd(nc, in_maps, core_ids=list(range(n_cores)))
    R = res.results
    yp = np.concatenate([R[c]["yp"] for c in range(n_cores)], axis=0).reshape(B_, S_, D)
    ys = np.concatenate([R[c]["ys"] for c in range(n_cores)], axis=0).reshape(n_cores * NSEQ, TS, D)
    last = [b * cores_per_b + cores_per_b - 1 for b in range(B_)]
    pk = np.stack([R[c]["pk"] for c in last], axis=1).reshape(L, B_, 128, 2, 64)
    pv = np.stack([R[c]["pv"] for c in last], axis=1).reshape(L, B_, 128, 2, 64)
    pc = np.stack([R[c]["pc"] for c in last], axis=1).reshape(L, B_, 2, 512)
    sk = np.concatenate([R[c]["sko"] for c in range(n_cores)], axis=1).reshape(L, n_cores * NSEQ, 128, 2, 64)
    sv = np.concatenate([R[c]["svo"] for c in range(n_cores)], axis=1).reshape(L, n_cores * NSEQ, 128, 2, 64)
    sc = np.concatenate([R[c]["sco"].reshape(L, NSEQ, 2, 512) for c in range(n_cores)], axis=1)
    scv = np.concatenate([R[c]["scv"].reshape(L, NSEQ, TS, 512) for c in range(n_cores)], axis=1)
    f = np.float32
    return (yp.astype(f), ys.astype(f), pk.astype(f), pv.astype(f), pc.astype(f),
            sk.astype(f), sv.astype(f), sc.astype(f), scv.astype(f))
```

```python
import numpy as np
from contextlib import ExitStack

import concourse.bass as bass
import concourse.mybir as mybir
from concourse.bass_utils import run_bass_kernel_spmd

F32 = mybir.dt.float32
BF16 = mybir.dt.bfloat16
AF = mybir.ActivationFunctionType
ALU = mybir.AluOpType

D = 1024
NCH = 8
NSEQ = 16
TS = 8
EPS = 1e-6
NEG = -30000.0

O_Q, O_K, O_V, O_ZA = 0, 512, 640, 768
O_GB, O_GC, O_HB, O_ZB = 1280, 1792, 2304, 2816
O_U, O_VC, O_ZC, O_G = 3328, 3840, 4352, 4864

PIECES = ([("q", 8, 512), ("kv", 8, 256), ("za", 8, 512)]
          + [("b%d" % c, 8, 512) for c in range(4)]
          + [("vc", 8, 512), ("u", 8, 512), ("zc", 8, 512)]
          + [("g%d" % j, 12, 384) for j in range(8)]
          + [("o0", 8, 512), ("o1", 8, 512)])
PIECE_OFF = {}
_o = 0
for _n, _k, _g in PIECES:
    PIECE_OFF[_n] = (_o, _k, _g)
    _o += _k * _g
PCOLS = _o
SLOT_ELEMS = 12 * 384


class Buf:
    __slots__ = ("name", "writer", "readers", "gen", "excl")

    def __init__(self, name, excl=False):
        self.name = name
        self.writer = None
        self.readers = {}
        self.gen = 0
        self.excl = excl


class _Dummy:
    pass


class Sched:
    ENG = ("pe", "act", "dve", "pool", "sp")

    def __init__(self, nc, ctx, signal=None):
        self.nc = nc
        self.dry = signal is None
        self.signal = {e: set() for e in self.ENG} if self.dry else signal
        self.eng = {"pe": nc.tensor, "act": nc.scalar, "dve": nc.vector, "pool": nc.gpsimd, "sp": nc.sync}
        self.sem = {}
        self.raw = {e: 0 for e in self.ENG}
        self.sig = {e: 0 for e in self.ENG}
        for e in self.ENG:
            self.sem[e] = _Dummy() if self.dry else ctx.enter_context(nc.semaphore("s_" + e))
        self.known = {e: {} for e in self.ENG}
        self.ctx = ctx
        self.dma_sems = {}
        self.nwaits = 0
        self.nops = {e: 0 for e in self.ENG}

    def _waits(self, e, reads, writes):
        evs = {}

        def add(ev):
            if ev is None:
                return
            s, v, src, raw = ev
            if src == "pe" and e == "pe":
                return
            k = id(s)
            if k not in evs or evs[k][1] < v:
                evs[k] = ev

        for b in reads:
            add(b.writer)
            if b.excl:
                for ev in b.readers.values():
                    if ev[2] != e:
                        add(ev)
        for b in writes:
            add(b.writer)
            for ev in b.readers.values():
                add(ev)
        kn = self.known[e]
        for k, (s, v, src, raw) in evs.items():
            if kn.get(k, 0) >= v:
                continue
            if self.dry:
                if src != "dma":
                    self.signal[src].add(raw)
            else:
                assert src == "dma" or raw in self.signal[src], "wait on a non-signalling instruction"
                self.eng[e].wait_ge(s, v)
            kn[k] = v
            self.nwaits += 1

    @staticmethod
    def _commit(ev, reads, writes):
        k = id(ev[0])
        for b in writes:
            b.writer = ev
            b.readers = {}
        for b in reads:
            old = b.readers.get(k)
            if old is None or old[1] < ev[1]:
                b.readers[k] = ev

    def op(self, e, fn, reads=(), writes=()):
        reads = _unlease(reads)
        writes = _unlease(writes)
        self._waits(e, reads, writes)
        self.raw[e] += 1
        self.nops[e] += 1
        raw = self.raw[e]
        if self.dry:
            ev = (self.sem[e], raw, e, raw)
        else:
            ins = fn(self.eng[e])
            if raw in self.signal[e]:
                self.sig[e] += 1
                ins.then_inc(self.sem[e], 1)
            ev = (self.sem[e], self.sig[e], e, raw)
        self._commit(ev, reads, writes)
        return ev

    def dma(self, q, semname, out, in_, reads=(), writes=(), **kw):
        if semname not in self.dma_sems:
            self.dma_sems[semname] = [_Dummy() if self.dry else self.ctx.enter_context(self.nc.semaphore("d_" + semname)), 0]
        ent = self.dma_sems[semname]
        reads = _unlease(reads)
        writes = _unlease(writes)
        self._waits(q, reads, writes)
        ent[1] += 16
        if not self.dry:
            ins = self.eng[q].dma_start(out=out, in_=in_, **kw)
            ins.then_inc(ent[0], 16)
        ev = (ent[0], ent[1], "dma", None)
        self._commit(ev, reads, writes)
        return ev

    def wait_all_dma(self, e):
        if self.dry:
            return
        for name, (s, v) in self.dma_sems.items():
            if v > 0 and self.known[e].get(id(s), 0) < v:
                self.eng[e].wait_ge(s, v)
                self.known[e][id(s)] = v


class Lease:
    __slots__ = ("buf", "gen")

    def __init__(self, buf, gen):
        self.buf = buf
        self.gen = gen


def _unlease(bs):
    out = []
    for b in bs:
        if isinstance(b, Lease):
            assert b.gen == b.buf.gen, "stale lease on %s" % b.buf.name
            b = b.buf
        out.append(b)
    return out


class Pool:
    def __init__(self, aps, name, excl=False, bufs=None):
        self.aps = aps
        self.bufs = bufs if bufs is not None else [Buf("%s%d" % (name, i), excl) for i in range(len(aps))]
        self.i = 0

    def get(self):
        i = self.i
        self.i = (i + 1) % len(self.aps)
        self.bufs[i].gen += 1
        return self.aps[i], Lease(self.bufs[i], self.bufs[i].gen)


def default_sts():
    return [
        [dict(kind="P", blocks=[0, 1, 2, 3]), dict(kind="P", blocks=[4, 5, 6])],
        [dict(kind="P", blocks=[7, 8, 9, 10]), dict(kind="P", blocks=[11, 12, 13])],
        [dict(kind="P", blocks=[14, 15, 16, 17]), dict(kind="P", blocks=[18, 19]), dict(kind="S", blocks=["S"])],
    ]


class StopBuild(Exception):
    pass


def build(depth, sts, npb, first_out_gb, nslots=5, dbg=None, stop_after=None, variant=0):
    L = depth
    last_gb = npb - 1
    nc = bass.Bass("TRN2", target_bir_lowering=False)

    def din(name, shape):
        return nc.dram_tensor(name, list(shape), F32, kind="ExternalInput").ap()

    def dout(name, shape):
        return nc.dram_tensor(name, list(shape), F32, kind="ExternalOutput").ap()

    n_out_blocks = npb - first_out_gb
    xp = din("xp", [npb * 128, D])
    xs = din("xs", [128, D])
    ck_d = din("ck", [L, NSEQ, 128, 128])
    cv_d = din("cv", [L, NSEQ, 128, 128])
    sconv_d = din("sconv", [L, NSEQ * 2, 512])
    wp_d = din("wp", [L, 128, PCOLS])
    nw_d = din("nw", [128, L * 8])
    bg_d = din("bg", [128, L * 24])
    qw_d = din("qw", [128, L])
    kw_d = din("kw", [128, L])
    sk_d = din("sk4", [128, L * 4])
    cw_d = din("cw", [128, L * 12])
    vnw_d = din("vnw", [L, 128, 512])
    wsp_d = din("wsp", [L, 4, 128, 128])
    bspt_d = din("bspt", [L, 128, 512])
    bspts_d = din("bspts", [L, 128, 512])
    ident_d = din("ident", [128, 128])
    tri_d = din("tri", [128, 128])
    bdm_d = din("bdm", [128, 128])
    bd64_d = din("bd64", [128, 128])
    repi_d = din("repi", [8, 128])
    bias_d = din("biasall", [10, 128, 512])

    yp_o = dout("yp", [n_out_blocks * 128, D])
    ys_o = dout("ys", [128, D])
    pk_o = dout("pk", [L, 128, 128])
    pv_o = dout("pv", [L, 128, 128])
    pc_o = dout("pc", [L, 2, 512])
    sk_o = dout("sko", [L, NSEQ, 128, 128])
    sv_o = dout("svo", [L, NSEQ, 128, 128])
    sc_o = dout("sco", [L, NSEQ * 2, 512])
    scv_o = dout("scv", [L, 128, 512])
    dbg_o = None
    if dbg:
        dbg_o = dout("dbg", [128, dbg])

    with ExitStack() as ctx:
        def sb(name, shape, dt):
            return ctx.enter_context(nc.sbuf_tensor(name, list(shape), dt))

        STW = max(sum(len(t["blocks"]) for t in st) for st in sts) * 128

        xT = sb("xT", [128, NCH, STW], F32)
        xnT = sb("xnT", [128, NCH, STW], BF16)
        PA = sb("PA", [128, 4, STW], BF16)
        PB = sb("PB", [128, 4, STW], BF16)
        PC = sb("PC", [128, 4, STW], BF16)
        QT = PB
        MBR = sb("MBR", [128, NCH * STW], BF16)
        MB = MBR[:].rearrange("p (c n) -> p c n", c=NCH)
        YX = MBR[:].bitcast(F32).rearrange("p (c n) -> p c n", c=4)
        CE = sb("CE", [128, 2 + STW], F32)
        CES = sb("CES", [128, NSEQ, 10], F32)
        KTe = sb("KTe", [128, STW + 128], BF16)
        Ve = sb("Ve", [128, STW // 128 + 1, 128], BF16)
        KTb = sb("KTb", [128, L, 128], BF16)
        Vb = sb("Vb", [128, L, 128], BF16)
        CEb = sb("CEb", [128, L, 4, 2], F32)
        TMPS = [sb("tmp%d" % i, [128, 512], F32) for i in range(7)]
        BTS = [sb("bt%d" % i, [128, 512], BF16) for i in range(6)]
        NEGH = sb("NEGH", [128, 64], F32)
        R2S = [sb("r2s%d" % i, [128, 128], F32) for i in range(8)]
        SMALL = [sb("sm%d" % i, [128, 256], F32) for i in range(2)]
        IO = [sb("io%d" % i, [128, D], F32) for i in range(2)]
        KOs = sb("KOs", [128, 128], F32)
        VOs = KOs
        CKT = sb("CKT", [128, NSEQ, 128], BF16)
        STG = sb("STG", [96, 512], F32)
        SCo = STG[0:32, :]
        PCs = STG[32:34, :]
        SCs = STG[64:96, :]
        C32 = sb("C32", [128, 32], F32)
        SSS = [sb("SS%d" % i, [128, 4], F32) for i in range(4)]
        NRT = [sb("NRT%d" % i, [128, 8], F32) for i in range(3)]
        RPS = [sb("RP%d" % i, [128, 8], F32) for i in range(6)]
        IDENT = sb("IDENT", [128, 128], F32)
        IDENTB = sb("IDENTB", [128, 128], BF16)
        ONESB = sb("ONESB", [128, 128], BF16)
        BD64 = sb("BD64", [128, 128], BF16)
        TRI = sb("TRI", [128, 128], BF16)
        BDM = sb("BDM", [128, 128], BF16)
        REPI = sb("REPI", [8, 128], BF16)
        BIAS = sb("BIAS", [128, 8, 512], F32)
        WMT = sb("WMT", [128, L, 4, 128], BF16)
        WMTS = sb("WMTS", [128, L, 4, 128], BF16)
        WSPL = sb("WSPL", [128, 4, 128], BF16)
        VNW = sb("VNW", [128, 512], F32)
        BSPT = sb("BSPT", [128, 512], F32)
        BSPTS = sb("BSPTS", [128, 512], F32)
        NW = sb("NW", [128, L * 8], F32)
        BGH = sb("BGH", [128, L * 24], F32)
        QW = sb("QW", [128, L], F32)
        KW = sb("KW", [128, L], F32)
        SE4 = sb("SE4", [128, L * 4], F32)
        CW = sb("CW", [128, L * 12], F32)
        EPSC = sb("EPSC", [128, 1], F32)
        print("sbuf before ring:", nc.sbuf_bytes_remaining, "need", nslots * SLOT_ELEMS * 2)
        RING = [sb("ring%d" % i, [128, SLOT_ELEMS], BF16) for i in range(nslots)]
        PSB = [ctx.enter_context(nc.psum_tensor("ps%d" % i, [128, 512], F32)) for i in range(8)]

        block = ctx.enter_context(nc.Block())
        def emit_all(S):

            psum = Pool([p[:] for p in PSB], "ps", excl=True)
            psum_lo = Pool([p[:] for p in PSB[0:5]], "ps", bufs=psum.bufs[0:5])
            psum_hi = Pool([p[:] for p in PSB[5:8]], "ps", bufs=psum.bufs[5:8])
            tmps = Pool([t[:] for t in TMPS], "tmp")
            bts = Pool([t[:] for t in BTS], "bt")
            smalls = Pool([t[:] for t in SMALL], "sm")
            r2s = Pool([t[:] for t in R2S], "r2s")
            sss = Pool([t[:] for t in SSS], "ss")
            rps = Pool([t[:] for t in RPS], "rp")

            B = {}

            def buf(name):
                if name not in B:
                    B[name] = Buf(name)
                return B[name]

            ntile_max = max(len(st) for st in sts)
            b_xT = [[buf("xT%d_%d" % (t, c)) for c in range(NCH)] for t in range(ntile_max)]
            b_xnT = [buf("xnT%d" % t) for t in range(ntile_max)]
            b_PA = [buf("PA%d" % t) for t in range(ntile_max)]
            b_PB = [buf("PB%d" % t) for t in range(ntile_max)]
            b_PC = [buf("PC%d" % t) for t in range(ntile_max)]
            b_MB = buf("MBR")
            b_MBt = [buf("MBt%d" % t) for t in range(ntile_max)]
            b_CE = buf("CE")
            b_CES = buf("CES")
            b_KT = [buf("KT%d" % i) for i in range(STW // 128 + 1)]
            b_V = [buf("V%d" % i) for i in range(STW // 128 + 1)]
            b_KTb = [buf("KTb%d" % l) for l in range(L)]
            b_Vb = [buf("Vb%d" % l) for l in range(L)]
            b_CEb = [buf("CEb%d" % l) for l in range(L)]
            b_IO = [buf("IO0"), buf("IO1")]
            b_const = buf("const")
            b_ring = [buf("ring%d" % i) for i in range(nslots)]

            schedule = []
            for si in range(len(sts)):
                for l in range(L):
                    for (pn, pk_, pg) in PIECES:
                        schedule.append((l, pn))
            wstate = dict(next=0, cur=-1)

            def pump(limit):
                while wstate["next"] < min(limit, len(schedule)):
                    k = wstate["next"]
                    l, pn = schedule[k]
                    off, kk, g = PIECE_OFF[pn]
                    slot = k % nslots
                    S.dma("pool", "ring%d" % slot, RING[slot][:, 0:kk * g], wp_d[l, :, off:off + kk * g], writes=[b_ring[slot]])
                    wstate["next"] = k + 1

            def piece(pn):
                wstate["cur"] += 1
                k = wstate["cur"]
                assert schedule[k][1] == pn, (schedule[k], pn)
                pump(k + 1)
                off, kk, g = PIECE_OFF[pn]
                slot = k % nslots
                return RING[slot][:, 0:kk * g].rearrange("p (k g) -> p k g", k=kk), b_ring[slot]

            def piece_done():
                pump(wstate["cur"] + nslots + 1)

            pump(nslots)

            cq = "sp"
            for dst, src in [(IDENT, ident_d), (NW, nw_d), (BGH, bg_d), (QW, qw_d), (KW, kw_d), (SE4, sk_d), (CW, cw_d)]:
                S.dma(cq, "const", dst[:], src, writes=[b_const])
            S.dma(cq, "const", BIAS[:, 0:4, :], bias_d[0:4].rearrange("k p n -> p k n"), writes=[b_const])
            for dst, src in [(IDENTB, ident_d), (TRI, tri_d), (BDM, bdm_d), (BD64, bd64_d), (REPI, repi_d)]:
                S.dma("pool", "constp", dst[:], src, writes=[b_const])
            S.op("dve", lambda e: e.memset(ONESB[:], 1.0), writes=[b_const])
            S.op("dve", lambda e: e.memset(NEGH[:], -0.5), writes=[b_const])
            S.op("dve", lambda e: e.memset(EPSC[:], EPS), writes=[b_const])
            S.op("dve", lambda e: e.memset(KTb[:], 0.0), writes=b_KTb)
            S.op("dve", lambda e: e.memset(Vb[:], 0.0), writes=b_Vb)
            S.op("dve", lambda e: e.memset(CEb[:], 0.0), writes=b_CEb)
            S.op("dve", lambda e: e.tensor_scalar(BGH[:], BGH[:], 0.5, None, op0=ALU.mult), reads=[b_const], writes=[b_const])
            S.op("act", lambda e: e.activation(SE4[:], SE4[:], AF.Exp), reads=[b_const], writes=[b_const])
            for l in range(L):
                bw = buf("WSPL")
                S.dma("pool", "wspl", WSPL[:], wsp_d[l].rearrange("g t s -> t g s"), writes=[bw])
                pst, bp = psum.get()
                pstb = pst.bitcast(BF16)
                for g in range(4):
                    S.op("pe", lambda e, g=g: e.transpose(pstb[:, g * 128:(g + 1) * 128], WSPL[:, g, :], IDENTB[:]),
                         reads=[bw, b_const], writes=[bp])
                for g in range(4):
                    S.op("dve", lambda e, g=g, l=l: e.tensor_tensor(WMT[:, l, g, :], pstb[:, g * 128:(g + 1) * 128], TRI[:], op=ALU.mult),
                         reads=[bp, b_const], writes=[b_const])
                ps2, bp2 = psum.get()
                for g in range(4):
                    S.op("pe", lambda e, g=g, l=l: e.matmul(
                        ps2[:, g * 128:(g + 1) * 128].rearrange("p (n t) -> p n t", n=NSEQ), REPI[:],
                        WMT[0:8, l, g, 0:8].unsqueeze(1).to_broadcast([8, NSEQ, 8]), start=True, stop=True),
                        reads=[b_const], writes=[bp2])
                for g in range(4):
                    S.op("dve", lambda e, g=g, l=l: e.tensor_tensor(WMTS[:, l, g, :], ps2[:, g * 128:(g + 1) * 128], BDM[:], op=ALU.mult),
                         reads=[bp2, b_const], writes=[b_const])


            def mm_group(ps_ap, bp, pairs, reads, **kw):
                n = len(pairs)
                for i, (lh, rh) in enumerate(pairs):
                    S.op("pe", lambda e, lh=lh, rh=rh, i=i: e.matmul(ps_ap, lh, rh, start=(i == 0), stop=(i == n - 1), **kw),
                         reads=reads, writes=[bp])

            def inproj(ps_ap, bp, w_ap, bw, col0, ncols, tcols, t):
                mm_group(ps_ap, bp, [(w_ap[:, k, col0:col0 + ncols], xnT[:, k, tcols]) for k in range(NCH)],
                         reads=[bw, b_xnT[t]])

            def rstd_part1a(sq_list, sq_bufs, n, H, rhs_sel, scale, r=None, br=None):
                nb = n // 128
                pst, bpst = psum_hi.get()
                for b in range(nb):
                    mm_group(pst[:, b * H:(b + 1) * H], bpst, [(sq[:, b * 128:(b + 1) * 128], rhs_sel) for sq in sq_list],
                             reads=list(sq_bufs) + [b_const])
                if r is None:
                    r, br = rps.get()
                w = nb * H
                S.op("act", lambda e: e.activation(r[:, 0:w], pst[:, 0:w], AF.Identity, bias=EPSC[:, 0:1], scale=scale),
                     reads=[bpst, b_const], writes=[br])
                S.op("pool", lambda e: e.tensor_tensor(r[:, 0:w], r[:, 0:w], NEGH[:, 0:w], op=ALU.pow),
                     reads=[br, b_const], writes=[br])
                return r, br

            def rstd_part1b(r, br, n, H):
                r2l = []
                for b in range(n // 128):
                    r2, br2 = r2s.get()
                    S.op("dve", lambda e, b=b, r2=r2: e.tensor_copy(
                        r2.rearrange("p (h d) -> p h d", h=H), r[:, b * H:(b + 1) * H].unsqueeze(2).to_broadcast([128, H, 128 // H])),
                        reads=[br], writes=[br2])
                    r2l.append((r2, br2))
                return r2l

            def rstd_part1(sq_list, sq_bufs, n, H, rhs_sel, scale):
                r, br = rstd_part1a(sq_list, sq_bufs, n, H, rhs_sel, scale)
                return rstd_part1b(r, br, n, H)

            def rstd_part2(r2l, n, to_sbuf):
                pbc, bpbc = psum_hi.get()
                for b, (r2, br2) in enumerate(r2l):
                    S.op("pe", lambda e, b=b, r2=r2: e.transpose(pbc[:, b * 128:(b + 1) * 128], r2, IDENT[:]),
                         reads=[br2, b_const], writes=[bpbc])
                if not to_sbuf:
                    return pbc, bpbc, (pbc, bpbc)
                rs, brs = tmps.get()
                S.op("act", lambda e: e.copy(rs[:, 0:n], pbc[:, 0:n]), reads=[bpbc], writes=[brs])
                return rs, brs, (pbc, bpbc)

            def run_pipeline(items, order):
                ns = len(items[0])
                for step in range(len(items) + ns - 1):
                    for s_ in order:
                        k = step - s_
                        if 0 <= k < len(items):
                            items[k][s_]()

            def silu2(ps_ap, bp, n):
                th, bth = tmps.get()
                S.op("act", lambda e: e.activation(th[:, 0:n], ps_ap, AF.Tanh, scale=0.5), reads=[bp], writes=[bth])
                t2, bt2 = tmps.get()
                S.op("dve", lambda e: e.scalar_tensor_tensor(t2[:, 0:n], th[:, 0:n], 1.0, ps_ap, op0=ALU.add, op1=ALU.mult),
                     reads=[bth, bp], writes=[bt2])
                return t2, bt2

            def transpose_out(src_ap, bsrc, rows, ncols_src, dst_sb, bdst, dst_cols, ps_lease=None):
                ps, bp = ps_lease if ps_lease is not None else psum.get()
                S.op("pe", lambda e: e.transpose(ps[0:ncols_src, 0:rows], src_ap, IDENT[0:rows, 0:rows]),
                     reads=[bsrc, b_const], writes=[bp])
                S.op("act", lambda e: e.copy(dst_sb[0:ncols_src, dst_cols], ps[0:ncols_src, 0:rows]), reads=[bp], writes=[bdst])

            cur_si = [0]

            def stage(name):
                if stop_after == name or stop_after == "%d:%s" % (cur_si[0], name):
                    raise StopBuild()

            try:
              stage("setup")
              for si, st in enumerate(sts):
                  cur_si[0] = si
                  tiles = []
                  col = 0
                  for t, tl in enumerate(st):
                      n = 128 * len(tl["blocks"])
                      tiles.append(dict(kind=tl["kind"], c0=col, n=n, blocks=tl["blocks"], t=t))
                      col += n
                  nblk = col // 128
                  has_sample = any(tl["kind"] == "S" for tl in tiles)
                  n_prompt_blk = sum(len(tl["blocks"]) for tl in tiles if tl["kind"] == "P")
                  first_blk_is_out = None

                  io_i = 0
                  for tl in tiles:
                      for bi, gb in enumerate(tl["blocks"]):
                          c0 = tl["c0"] + bi * 128
                          src = xs if gb == "S" else xp[gb * 128:(gb + 1) * 128, :]
                          io, bio = IO[io_i % 2], b_IO[io_i % 2]
                          S.dma("sp", "io%d" % (io_i % 2), io[:], src, writes=[bio])
                          io_i += 1
                          for h in range(2):
                              ps, bp = psum.get()
                              for c in range(4):
                                  cc = h * 4 + c
                                  S.op("pe", lambda e, cc=cc, c=c, ps=ps, io=io: e.transpose(ps[:, c * 128:(c + 1) * 128], io[:, cc * 128:(cc + 1) * 128], IDENT[:]),
                                       reads=[bio, b_const], writes=[bp])
                              eng = "act" if h == 0 else "dve"
                              if eng == "act":
                                  S.op("act", lambda e, h=h, ps=ps, c0=c0: e.copy(xT[:, h * 4:(h + 1) * 4, c0:c0 + 128], ps.rearrange("p (c n) -> p c n", c=4)),
                                       reads=[bp], writes=b_xT[tl["t"]][h * 4:(h + 1) * 4])
                              else:
                                  S.op("dve", lambda e, h=h, ps=ps, c0=c0: e.tensor_copy(xT[:, h * 4:(h + 1) * 4, c0:c0 + 128], ps.rearrange("p (c n) -> p c n", c=4)),
                                       reads=[bp], writes=b_xT[tl["t"]][h * 4:(h + 1) * 4])

                  stage('load')
                  if si == 0:
                      for l_ in range(L):
                          S.dma("sp", "cachecopy", sk_o[l_, :, 0:128 - TS, :], ck_d[l_, :, TS:128, :])
                          S.dma("sp", "cachecopy", sv_o[l_, :, 0:128 - TS, :], cv_d[l_, :, TS:128, :])
                  if si == 0:
                      S.dma("sp", "biasx", BIAS[:, 4:6, :], bias_d[4:6].rearrange("k p n -> p k n"), writes=[buf("biasx")])
                  if has_sample:
                      S.dma("sp", "biasx", BIAS[:, 4:8, :], bias_d[6:10].rearrange("k p n -> p k n"), writes=[buf("biasx")])
                  b_biasx = buf("biasx")

                  tiles_full = tiles

                  def tiles_at(l_):
                      min_gb = first_out_gb - (L - l_)
                      res = []
                      for lo in (min_gb, min_gb + 1 if min_gb >= 0 else min_gb):
                          lst = []
                          for tl in tiles_full:
                              keep = [(i, gb) for i, gb in enumerate(tl["blocks"]) if gb == "S" or gb >= lo]
                              if not keep:
                                  continue
                              i0_ = keep[0][0]
                              lst.append(dict(kind=tl["kind"], c0=tl["c0"] + 128 * i0_, n=128 * len(keep), blocks=[gb for _, gb in keep], t=tl["t"]))
                          res.append(lst)
                      return res

                  norm_pre = {}

                  def norm_AB(l_, tl):
                      t, c0, n = tl["t"], tl["c0"], tl["n"]
                      tc_ = slice(c0, c0 + n)
                      for c in range(NCH):
                          S.op("act", lambda e, c=c: e.activation(xnT[:, c, tc_], xT[:, c, tc_], AF.Square),
                               reads=[b_xT[t][c]], writes=[b_xnT[t]])
                      norm_pre[(l_, t)] = rstd_part1a([xnT[:, c, tc_] for c in range(NCH)], [b_xnT[t]], n, 1, ONESB[:, 0:1], 1.0 / D,
                                                      r=NRT[t][:], br=buf("NRT%d" % t))

                  norm_done = set()

                  def norm_C(l_, tl):
                      t, c0, n = tl["t"], tl["c0"], tl["n"]
                      tc_ = slice(c0, c0 + n)
                      rr, brr = norm_pre.pop((l_, t))
                      r, br, _ = rstd_part2(rstd_part1b(rr, brr, n, 1), n, False)
                      for c in range(NCH):
                          S.op("dve", lambda e, c=c: e.scalar_tensor_tensor(
                              xnT[:, c, tc_], xT[:, c, tc_], NW[:, l_ * 8 + c:l_ * 8 + c + 1], r[:, 0:n], op0=ALU.mult, op1=ALU.mult),
                              reads=[b_xT[t][c], br, b_const], writes=[b_xnT[t]])
                      norm_done.add((l_, t))

                  for l in range(L):
                      tiles_kv, tiles = tiles_at(l)
                      b_lay = buf("laycst")
                      S.dma("sp", "laycst", VNW[:], vnw_d[l], writes=[b_lay])
                      S.dma("sp", "laycst", BSPT[:], bspt_d[l], writes=[b_lay])
                      if has_sample:
                          S.dma("sp", "laycst", BSPTS[:], bspts_d[l], writes=[b_lay])

                      S.op("pool", lambda e: e.tensor_copy(KTe[:, 0:128], KTb[:, l, :]), reads=[b_KTb[l]], writes=[b_KT[0]])
                      S.op("pool", lambda e: e.tensor_copy(Ve[:, 0, :], Vb[:, l, :]), reads=[b_Vb[l]], writes=[b_V[0]])
                      if has_sample:
                          S.dma("pool", "ckl", IO[0][:].bitcast(BF16)[:, 0:NSEQ * 128].rearrange("p (n f) -> p n f", n=NSEQ),
                                ck_d[l].rearrange("n j f -> j n f"), writes=[b_IO[0]])
                          S.dma("pool", "cvl", IO[1][:].bitcast(BF16)[:, 0:NSEQ * 128].rearrange("p (n f) -> p n f", n=NSEQ),
                                cv_d[l].rearrange("n j f -> j n f"), writes=[b_IO[1]])
                          CK = IO[0][:].bitcast(BF16)[:, 0:NSEQ * 128].rearrange("p (n f) -> p n f", n=NSEQ)
                          CV = IO[1][:].bitcast(BF16)[:, 0:NSEQ * 128].rearrange("p (n f) -> p n f", n=NSEQ)
                      wts = {}

                      def mk_norm(tl):
                          t = tl["t"]

                          def fA():
                              if (l, t) not in norm_pre and (l, t) not in norm_done:
                                  norm_AB(l, tl)

                          def fC():
                              if (l, t) not in norm_done:
                                  norm_C(l, tl)
                              norm_done.discard((l, t))
                          return [fA, (lambda: None), fC, (lambda: None)]

                      def mk_q(tl, g, first, last):
                          t, c0, n = tl["t"], tl["c0"], tl["n"]
                          tc_ = slice(c0, c0 + n)
                          st_ = {}

                          def fA():
                              wq, bwq = wts["q"]
                              ps, bp = psum_lo.get()
                              inproj(ps[:, 0:n], bp, wq, bwq, g * 128, 128, tc_, t)
                              sq, bsq = bts.get()
                              S.op("act", lambda e: e.activation(sq[:, 0:n], ps[:, 0:n], AF.Square), reads=[bp], writes=[bsq])
                              st_.update(ps=ps, bp=bp, sq=sq, bsq=bsq)

                          def fB():
                              st_["rr"] = rstd_part1a([st_["sq"][:, 0:n]], [st_["bsq"]], n, 2, BD64[:, 0:128:64], 1.0 / 64)

                          def fR():
                              st_["r2l"] = rstd_part1b(st_["rr"][0], st_["rr"][1], n, 2)

                          def fC():
                              r, br, _ = rstd_part2(st_["r2l"], n, True)
                              ps, bp = st_["ps"], st_["bp"]
                              S.op("dve", lambda e: e.scalar_tensor_tensor(
                                  QT[:, g, tc_], ps[:, 0:n], QW[:, l:l + 1], r[:, 0:n], op0=ALU.mult, op1=ALU.mult),
                                  reads=[bp, br, b_const], writes=[b_PB[t]])
                          return [fA, fB, fR, fC]

                      def mk_k(tl, first, last):
                          t, c0, n = tl["t"], tl["c0"], tl["n"]
                          tc_ = slice(c0, c0 + n)
                          st_ = {}

                          def fA():
                              wkv, bwkv = wts["kv"]
                              ps, bp = psum_lo.get()
                              inproj(ps[:, 0:n], bp, wkv, bwkv, 0, 128, tc_, t)
                              sq, bsq = bts.get()
                              S.op("act", lambda e: e.activation(sq[:, 0:n], ps[:, 0:n], AF.Square), reads=[bp], writes=[bsq])
                              st_.update(ps=ps, bp=bp, sq=sq, bsq=bsq)

                          def fB():
                              st_["rr"] = rstd_part1a([st_["sq"][:, 0:n]], [st_["bsq"]], n, 2, BD64[:, 0:128:64], 1.0 / 64)

                          def fR():
                              st_["r2l"] = rstd_part1b(st_["rr"][0], st_["rr"][1], n, 2)

                          def fC():
                              r, br, pfree = rstd_part2(st_["r2l"], n, True)
                              ps, bp = st_["ps"], st_["bp"]
                              kslots = [b_KT[1 + (c0 // 128) + i] for i in range(n // 128)]
                              S.op("dve", lambda e: e.scalar_tensor_tensor(
                                  KTe[:, 128 + c0:128 + c0 + n], ps[:, 0:n], KW[:, l:l + 1], r[:, 0:n], op0=ALU.mult, op1=ALU.mult),
                                  reads=[bp, br, b_const], writes=kslots)
                              for bi, gb in enumerate(tl["blocks"]):
                                  if (gb == "S") or (gb == last_gb):
                                      kf_, bkf = tmps.get()
                                      KF = kf_[:, 0:128]
                                      S.op("dve", lambda e, bi=bi: e.scalar_tensor_tensor(
                                          KF, ps[:, bi * 128:(bi + 1) * 128], KW[:, l:l + 1], r[:, bi * 128:(bi + 1) * 128], op0=ALU.mult, op1=ALU.mult),
                                          reads=[bp, br, b_const], writes=[bkf])
                                      bko = buf("KOs")
                                      transpose_out(KF, bkf, 128, 128, KOs, bko, slice(0, 128), ps_lease=pfree)
                                      if gb == "S":
                                          for nn in range(NSEQ):
                                              S.dma("sp", "kout", sk_o[l, nn, 128 - TS:128, :], KOs[nn * TS:(nn + 1) * TS, :], reads=[bko])
                                      else:
                                          S.dma("sp", "kout", pk_o[l], KOs[:], reads=[bko])
                          return [fA, fB, fR, fC]

                      def mk_v(tl, last):
                          t, c0, n = tl["t"], tl["c0"], tl["n"]

                          def fA():
                              wkv, bwkv = wts["kv"]
                              psv, bpv = psum_lo.get()
                              for bi, gb in enumerate(tl["blocks"]):
                                  bc = c0 + bi * 128
                                  mm_group(psv[:, bi * 128:(bi + 1) * 128], bpv, [(xnT[:, k, bc:bc + 128], wkv[:, k, 128:256]) for k in range(NCH)],
                                           reads=[bwkv, b_xnT[t]])
                              vs0 = 1 + c0 // 128
                              nb = n // 128
                              S.op("act", lambda e: e.copy(Ve[:, vs0:vs0 + nb, :], psv[:, 0:n].rearrange("p (b f) -> p b f", b=nb)),
                                   reads=[bpv], writes=[b_V[vs0 + i] for i in range(nb)])
                              for bi, gb in enumerate(tl["blocks"]):
                                  if (gb == "S") or (gb == last_gb):
                                      bvo = buf("KOs")
                                      S.op("act", lambda e, bi=bi: e.copy(VOs[:], psv[:, bi * 128:(bi + 1) * 128]), reads=[bpv], writes=[bvo])
                                      if gb == "S":
                                          for nn in range(NSEQ):
                                              S.dma("sp", "vout", sv_o[l, nn, 128 - TS:128, :], VOs[nn * TS:(nn + 1) * TS, :], reads=[bvo])
                                      else:
                                          S.dma("sp", "vout", pv_o[l], VOs[:], reads=[bvo])

                          return [fA, (lambda: None), (lambda: None), (lambda: None)]

                      wts["q"] = piece("q")
                      wts["kv"] = piece("kv")
                      items = [mk_norm(tl) for tl in tiles_kv]
                      full_t = {tl["t"]: tl for tl in tiles}
                      rest = []
                      for i_norm, tk in enumerate(tiles_kv):
                          if tk["t"] in full_t:
                              rest += [(i_norm, mk_q(full_t[tk["t"]], g, False, False)) for g in range(4)]
                          rest.append((i_norm, mk_k(tk, False, False)))
                          rest.append((i_norm, mk_v(tk, False)))
                      pad = 1
                      for p_, (i_norm, _) in enumerate(rest):
                          pad = max(pad, i_norm + 3 - (len(tiles_kv) + p_))
                      for _ in range(pad):
                          items.append([(lambda: None)] * 4)
                      items += [it for _, it in rest]
                      run_pipeline(items, [0, 1, 2, 3])
                      piece_done()
                      if has_sample:
                          b_CKT = buf("CKT")
                          for q4 in range(NSEQ // 4):
                              ps, bp = psum.get()
                              psb = ps.bitcast(BF16)
                              for i in range(4):
                                  nn = q4 * 4 + i
                                  S.op("pe", lambda e, nn=nn, i=i, psb=psb: e.transpose(psb[:, i * 128:(i + 1) * 128], CK[:, nn, :], IDENTB[:]),
                                       reads=[b_IO[0], b_const], writes=[bp])
                              S.op("act", lambda e, q4=q4, psb=psb: e.copy(CKT[:, q4 * 4:(q4 + 1) * 4, :], psb[:, 0:512].rearrange("p (n f) -> p n f", n=4)),
                                   reads=[bp], writes=[b_CKT])


                      stage('Akv')
                      def mk_att(tl, bi, gb, kv):
                          t, c0, n = tl["t"], tl["c0"], tl["n"]
                          bc = c0 + bi * 128
                          bslot = bc // 128
                          pr = slice(64 * kv, 64 * kv + 64)
                          st_ = {}

                          def score(lhsT, rhs_ap, reads, bias_idx, bias_buf, out_view=None):
                              ps, bp = psum.get()
                              S.op("pe", lambda e: e.matmul(ps[:].rearrange("p (g t) -> p g t", g=4), lhsT, rhs_ap, start=True, stop=True),
                                   reads=reads, writes=[bp])
                              return ps, bp

                          def softexp(ps, bp, bias_idx, bias_buf):
                              sbm, bsb = tmps.get()
                              S.op("act", lambda e: e.activation(sbm[:], ps[:], AF.Exp, scale=0.125), reads=[bp], writes=[bsb])
                              pt, bpt = bts.get()
                              S.op("pool", lambda e: e.tensor_tensor(pt[:], sbm[:], BIAS[:, bias_idx, :], op=ALU.mult),
                                   reads=[bsb, bias_buf], writes=[bpt])
                              return pt, bpt

                          def fA():
                              if gb == "S":
                                  ps, bp = psum.get()
                                  for nn in range(NSEQ):
                                      S.op("pe", lambda e, nn=nn: e.matmul(
                                          ps[:, nn * 32:(nn + 1) * 32].rearrange("p (g t) -> p g t", g=4),
                                          CKT[pr, nn, :], QT[pr, :, bc + nn * TS:bc + (nn + 1) * TS], start=True, stop=True),
                                          reads=[b_CKT, b_PB[t]], writes=[bp])
                                  st_["ptc"] = softexp(ps, bp, 4 + kv, b_biasx)
                                  ps, bp = score(KTe[pr, 128 + bc:128 + bc + 128], QT[pr, :, bc:bc + 128], [b_KT[bslot + 1], b_PB[t]], None, None)
                                  st_["ptn"] = softexp(ps, bp, 6 + kv, b_biasx)
                              else:
                                  first = (si == 0 and gb == first_out_gb)
                                  kbs = [(bslot, (4 + kv) if first else kv, b_biasx if first else b_const), (bslot + 1, 2 + kv, b_const)]
                                  pts = []
                                  for (slot, bidx, bb) in kbs:
                                      ps, bp = score(KTe[pr, slot * 128:(slot + 1) * 128], QT[pr, :, bc:bc + 128], [b_KT[slot], b_PB[t]], None, None)
                                      pt, bpt = softexp(ps, bp, bidx, bb)
                                      pts.append((pt, bpt, slot))
                                  st_["pts"] = pts

                          def fB():
                              pso, bpo = psum.get()
                              if gb == "S":
                                  ptc, bptc = st_["ptc"]
                                  ptn, bptn = st_["ptn"]
                                  ptn3 = ptn[:].rearrange("p (g t) -> p g t", g=4)
                                  ptc4 = ptc[:].rearrange("p (n g t) -> p n g t", n=NSEQ, g=4)
                                  for par in range(2):
                                      orow = slice(64 * par, 64 * par + 64)
                                      tp = dict(tile_position=(0, 64)) if par == 1 else {}
                                      for which in range(2):
                                          ocol = 256 * which
                                          oap = pso[orow, ocol:ocol + 256].rearrange("p (c q) -> p c q", c=2)
                                          lhs_new = Ve[:, bslot + 1, pr] if which == 0 else ONESB[:, 0:64]
                                          S.op("pe", lambda e: e.matmul(
                                              oap, lhs_new, ptn3[:, par::2, :], start=True, stop=False, skip_group_check=True, **tp),
                                              reads=[bptn, b_V[bslot + 1], b_const], writes=[bpo])
                                          for nn in range(NSEQ):
                                              lhs_c = CV[:, nn, pr] if which == 0 else ONESB[:, 0:64]
                                              S.op("pe", lambda e, nn=nn, lhs_c=lhs_c: e.matmul(
                                                  oap[:, :, nn * TS:(nn + 1) * TS], lhs_c, ptc4[:, nn, par::2, :],
                                                  start=False, stop=(nn == NSEQ - 1), skip_group_check=True, **tp),
                                                  reads=[bptc, b_IO[1], b_const], writes=[bpo])
                              else:
                                  pts = st_["pts"]
                                  for par in range(2):
                                      orow = slice(64 * par, 64 * par + 64)
                                      tp = dict(tile_position=(0, 64)) if par == 1 else {}
                                      for which in range(2):
                                          ocol = 256 * which
                                          oap = pso[orow, ocol:ocol + 256].rearrange("p (c q) -> p c q", c=2)
                                          for i, (pt, bpt, slot) in enumerate(pts):
                                              lhs = Ve[:, slot, pr] if which == 0 else ONESB[:, 0:64]
                                              pt3 = pt[:].rearrange("p (g t) -> p g t", g=4)
                                              S.op("pe", lambda e, lhs=lhs, pt3=pt3, i=i: e.matmul(
                                                  oap, lhs, pt3[:, par::2, :], start=(i == 0), stop=(i == 1), skip_group_check=True, **tp),
                                                  reads=[bpt, b_V[slot], b_const], writes=[bpo])
                              ds, bds = smalls.get()
                              for c2 in range(2):
                                  S.op("dve", lambda e, c2=c2: e.tensor_scalar(
                                      ds[:, c2 * 128:(c2 + 1) * 128], pso[:, 256 + c2 * 128:256 + (c2 + 1) * 128],
                                      SE4[:, l * 4 + kv * 2 + c2:l * 4 + kv * 2 + c2 + 1], None, op0=ALU.add),
                                      reads=[bpo, b_const], writes=[bds])
                              S.op("dve", lambda e: e.reciprocal(ds[:], ds[:]), reads=[bds], writes=[bds])
                              S.op("dve", lambda e: e.tensor_tensor(
                                  YX[:, kv * 2:(kv + 1) * 2, bc:bc + 128], pso[:, 0:256].rearrange("p (c q) -> p c q", c=2),
                                  ds[:].rearrange("p (c q) -> p c q", c=2), op=ALU.mult),
                                  reads=[bpo, bds], writes=[b_MB] + b_MBt)
                          return [fA, (lambda: None), fB]

                      items = [mk_att(tl, bi, gb, kv) for tl in tiles for bi, gb in enumerate(tl["blocks"]) for kv in range(2)]
                      run_pipeline(items, [0, 1, 2])

                      stage('Aattn')
                      wza, bwza = piece("za")
                      for tl in tiles:
                          t, c0, n = tl["t"], tl["c0"], tl["n"]
                          tc_ = slice(c0, c0 + n)
                          for c2 in range(4):
                              ps, bp = psum.get()
                              inproj(ps[:, 0:n], bp, wza, bwza, c2 * 128, 128, tc_, t)
                              t2, bt2 = silu2(ps[:, 0:n], bp, n)
                              S.op("pool", lambda e, t2=t2, c2=c2: e.tensor_tensor(PA[:, c2, tc_], t2[:, 0:n], YX[:, c2, tc_], op=ALU.mult),
                                   reads=[bt2, b_MB], writes=[b_PA[t]])
                      piece_done()
                      lp = n_prompt_blk
                      S.op("pool", lambda e: e.tensor_copy(KTb[:, l, :], KTe[:, lp * 128:(lp + 1) * 128]), reads=[b_KT[lp]], writes=[b_KTb[l]])
                      S.op("pool", lambda e: e.tensor_copy(Vb[:, l, :], Ve[:, lp, :]), reads=[b_V[lp]], writes=[b_Vb[l]])

                      stage('A')
                      if has_sample:
                          bsc = buf("SCin")
                          S.dma("sp", "scin", SCs[:], sconv_d[l], writes=[bsc])
                      for c in range(4):
                          wb, bwb = piece("b%d" % c)
                          S.op("dve", lambda e, c=c: e.tensor_copy(CE[:, 0:2], CEb[:, l, c, :]), reads=[b_CEb[l]], writes=[b_CE])
                          if has_sample:
                              ps, bp = psum.get()
                              S.op("pe", lambda e, ps=ps, c=c: e.transpose(ps[:, 0:32], SCs[:, c * 128:(c + 1) * 128], IDENT[64:96, 64:96]),
                                   reads=[bsc, b_const], writes=[bp])
                              S.op("act", lambda e, ps=ps: e.copy(CES[:, :, 0:2], ps[:, 0:32].rearrange("p (n r) -> p n r", r=2)),
                                   reads=[bp], writes=[b_CES])
                          for tl in tiles_kv:
                              t, c0, n = tl["t"], tl["c0"], tl["n"]
                              tc_ = slice(c0, c0 + n)
                              deferred = []
                              ps1, bp1 = psum.get()
                              inproj(ps1[:, 0:n], bp1, wb, bwb, 0, 128, tc_, t)
                              gc, bgc = tmps.get()
                              S.op("act", lambda e, ps1=ps1, gc=gc: e.copy(gc[:, 0:n], ps1[:, 0:n]), reads=[bp1], writes=[bgc])
                              ps2, bp2 = psum.get()
                              inproj(ps2[:, 0:n], bp2, wb, bwb, 128, 128, tc_, t)
                              co, bco = tmps.get()
                              if tl["kind"] == "P":
                                  S.op("dve", lambda e, ps2=ps2, gc=gc: e.tensor_tensor(CE[:, 2 + c0:2 + c0 + n], ps2[:, 0:n], gc[:, 0:n], op=ALU.mult),
                                       reads=[bp2, bgc], writes=[b_CE])
                                  for r_ in range(3):
                                      src = CE[:, c0 + r_:c0 + r_ + n]
                                      wcol = CW[:, l * 12 + r_ * 4 + c:l * 12 + r_ * 4 + c + 1]
                                      if r_ == 0:
                                          S.op("dve", lambda e, src=src, wcol=wcol, co=co: e.tensor_scalar(co[:, 0:n], src, wcol, None, op0=ALU.mult),
                                               reads=[b_CE, b_const], writes=[bco])
                                      else:
                                          S.op("dve", lambda e, src=src, wcol=wcol, co=co: e.scalar_tensor_tensor(
                                              co[:, 0:n], src, wcol, co[:, 0:n], op0=ALU.mult, op1=ALU.add),
                                              reads=[b_CE, b_const, bco], writes=[bco])
                                  if last_gb in tl["blocks"]:
                                      e0 = 2 + c0 + n - 2
                                      bpc = buf("PCs")
                                      deferred.append(lambda e0=e0, bpc=bpc, c=c: transpose_out(
                                          CE[:, e0:e0 + 2], b_CE, 128, 2, PCs, bpc, slice(c * 128, (c + 1) * 128)))
                              else:
                                  ces_in = CES[:, :, 2:10]
                                  S.op("dve", lambda e, ps2=ps2, gc=gc: e.tensor_tensor(
                                      ces_in, ps2[:, 0:n].rearrange("p (n t) -> p n t", t=TS), gc[:, 0:n].rearrange("p (n t) -> p n t", t=TS), op=ALU.mult),
                                      reads=[bp2, bgc], writes=[b_CES])
                                  co3 = co[:, 0:n].rearrange("p (n t) -> p n t", t=TS)
                                  for r_ in range(3):
                                      src = CES[:, :, r_:r_ + TS]
                                      wcol = CW[:, l * 12 + r_ * 4 + c:l * 12 + r_ * 4 + c + 1]
                                      if r_ == 0:
                                          S.op("dve", lambda e, src=src, wcol=wcol, co3=co3: e.tensor_scalar(co3, src, wcol, None, op0=ALU.mult),
                                               reads=[b_CES, b_const], writes=[bco])
                                      else:
                                          S.op("dve", lambda e, src=src, wcol=wcol, co3=co3: e.scalar_tensor_tensor(
                                              co3, src, wcol, co3, op0=ALU.mult, op1=ALU.add),
                                              reads=[b_CES, b_const, bco], writes=[bco])
                                  bc32 = buf("C32")
                                  S.op("dve", lambda e: e.tensor_copy(C32[:].rearrange("p (n r) -> p n r", r=2), CES[:, :, 8:10]),
                                       reads=[b_CES], writes=[bc32])
                                  bscs = buf("SCs2")
                                  deferred.append(lambda bc32=bc32, bscs=bscs, c=c: transpose_out(
                                      C32[:], bc32, 128, 32, SCo, bscs, slice(c * 128, (c + 1) * 128)))
                              ps3, bp3 = psum.get()
                              inproj(ps3[:, 0:n], bp3, wb, bwb, 256, 128, tc_, t)
                              S.op("dve", lambda e, ps3=ps3, co=co: e.tensor_tensor(co[:, 0:n], ps3[:, 0:n], co[:, 0:n], op=ALU.mult),
                                   reads=[bp3, bco], writes=[bco])
                              ps4, bp4 = psum.get()
                              inproj(ps4[:, 0:n], bp4, wb, bwb, 384, 128, tc_, t)
                              for fn_ in deferred:
                                  fn_()
                              t2, bt2 = silu2(ps4[:, 0:n], bp4, n)
                              S.op("pool", lambda e, t2=t2, co=co, c=c: e.tensor_tensor(PB[:, c, tc_], t2[:, 0:n], co[:, 0:n], op=ALU.mult),
                                   reads=[bt2, bco], writes=[b_PB[t]])
                          e0 = 2 + n_prompt_blk * 128 - 2
                          S.op("dve", lambda e, c=c, e0=e0: e.tensor_copy(CEb[:, l, c, :], CE[:, e0:e0 + 2]), reads=[b_CE], writes=[b_CEb[l]])
                          piece_done()
                      if last_gb in [gb for tl in tiles_kv for gb in tl["blocks"]]:
                          S.dma("sp", "pcout", pc_o[l], PCs[:], reads=[buf("PCs")])
                      if has_sample:
                          S.dma("sp", "scout", sc_o[l], SCo[:], reads=[buf("SCs2")])

                      stage('B')
                      wvc, bwvc = piece("vc")

                      def mk_vc(tl, bi, gb):
                          t, c0, n = tl["t"], tl["c0"], tl["n"]
                          bc = c0 + bi * 128
                          st_ = {}

                          def fA():
                              psv, bpv = psum.get()
                              mm_group(psv[:], bpv, [(xnT[:, k, bc:bc + 128], wvc[:, k, :]) for k in range(NCH)], reads=[bwvc, b_xnT[t]])
                              junk, bj = tmps.get()
                              SS, bss = sss.get()
                              S.op("act", lambda e: e.activation(junk[:], psv[:], AF.Square, accum_out=SS[:, 0:1]),
                                   reads=[bpv], writes=[bj, bss])
                              S.op("dve", lambda e: e.tensor_scalar(SS[:, 1:2], SS[:, 0:1], 1.0 / 512, EPS, op0=ALU.mult, op1=ALU.add),
                                   reads=[bss], writes=[bss])
                              S.op("pool", lambda e: e.tensor_tensor(SS[:, 2:3], SS[:, 1:2], NEGH[:, 0:1], op=ALU.pow), reads=[bss, b_const], writes=[bss])
                              st_.update(psv=psv, bpv=bpv, SS=SS, bss=bss)

                          def fB():
                              psv, bpv, SS, bss = st_["psv"], st_["bpv"], st_["SS"], st_["bss"]
                              vcn, bvcn = bts.get()
                              S.op("dve", lambda e: e.scalar_tensor_tensor(
                                  vcn[:], psv[:], SS[:, 2:3], VNW[:], op0=ALU.mult, op1=ALU.mult), reads=[bpv, bss, b_lay], writes=[bvcn])
                              if gb == "S":
                                  vf, bvf = tmps.get()
                                  S.op("dve", lambda e: e.scalar_tensor_tensor(
                                      vf[:], psv[:], SS[:, 2:3], VNW[:], op0=ALU.mult, op1=ALU.mult), reads=[bpv, bss, b_lay], writes=[bvf])
                                  S.dma("sp", "scvout", scv_o[l], vf[:], reads=[bvf])
                              st_.update(vcn=vcn, bvcn=bvcn)

                          def fC():
                              vcn, bvcn = st_["vcn"], st_["bvcn"]
                              pss, bps = psum.get()
                              wm = WMTS if gb == "S" else WMT
                              for g in range(4):
                                  S.op("pe", lambda e, g=g: e.matmul(
                                      pss[:, g * 128:(g + 1) * 128], vcn[:, g * 128:(g + 1) * 128], wm[:, l, g, :], start=True, stop=True),
                                      reads=[bvcn, b_const], writes=[bps])
                              bt_ = BSPTS if gb == "S" else BSPT
                              S.op("dve", lambda e: e.tensor_tensor(
                                  YX[:, :, bc:bc + 128], pss[:].rearrange("p (g t) -> p g t", g=4), bt_[:].rearrange("p (g t) -> p g t", g=4), op=ALU.add),
                                  reads=[bps, b_lay], writes=[b_MB] + b_MBt)
                          return [fA, fB, fC]

                      items = [mk_vc(tl, bi, gb) for tl in tiles for bi, gb in enumerate(tl["blocks"])]
                      run_pipeline(items, [1, 0, 2])
                      piece_done()
                      wu, bwu = piece("u")
                      for tl in tiles:
                          t, c0, n = tl["t"], tl["c0"], tl["n"]
                          tc_ = slice(c0, c0 + n)
                          for g in range(4):
                              ps, bp = psum.get()
                              inproj(ps[:, 0:n], bp, wu, bwu, g * 128, 128, tc_, t)
                              S.op("dve", lambda e, ps=ps, g=g: e.tensor_tensor(YX[:, g, tc_], ps[:, 0:n], YX[:, g, tc_], op=ALU.mult),
                                   reads=[bp, b_MB], writes=[b_MB] + b_MBt)
                      piece_done()
                      wzc, bwzc = piece("zc")
                      for tl in tiles:
                          t, c0, n = tl["t"], tl["c0"], tl["n"]
                          tc_ = slice(c0, c0 + n)
                          for g in range(4):
                              ps, bp = psum.get()
                              inproj(ps[:, 0:n], bp, wzc, bwzc, g * 128, 128, tc_, t)
                              t2, bt2 = silu2(ps[:, 0:n], bp, n)
                              S.op("pool", lambda e, t2=t2, g=g: e.tensor_tensor(PC[:, g, tc_], t2[:, 0:n], YX[:, g, tc_], op=ALU.mult),
                                   reads=[bt2, b_MB], writes=[b_PC[t]])
                      piece_done()

                      stage('C')
                      Ps = [(PA, b_PA), (PB, b_PB), (PC, b_PC)]
                      for j in range(8):
                          wg, bwg = piece("g%d" % j)
                          for tl in tiles:
                              t, c0, n = tl["t"], tl["c0"], tl["n"]
                              tc_ = slice(c0, c0 + n)
                              acc = None
                              for i in range(3):
                                  Pi, bPi = Ps[i]
                                  psa, bpa = psum.get()
                                  mm_group(psa[:, 0:n], bpa, [(wg[:, 8 + k, i * 128:(i + 1) * 128], Pi[:, k, tc_]) for k in range(4)],
                                           reads=[bwg, bPi[t]])
                                  psg, bpg = psum.get()
                                  inproj(psg[:, 0:n], bpg, wg, bwg, i * 128, 128, tc_, t)
                                  tg, btg = tmps.get()
                                  bcol = l * 24 + i * 8 + j
                                  S.op("act", lambda e, psg=psg, tg=tg, bcol=bcol: e.activation(
                                      tg[:, 0:n], psg[:, 0:n], AF.Tanh, bias=BGH[:, bcol:bcol + 1], scale=0.5), reads=[bpg, b_const], writes=[btg])
                                  S.op("dve", lambda e, psa=psa, tg=tg: e.scalar_tensor_tensor(
                                      tg[:, 0:n], tg[:, 0:n], 1.0, psa[:, 0:n], op0=ALU.add, op1=ALU.mult), reads=[btg, bpa], writes=[btg])
                                  if i == 0:
                                      acc, bacc = tg, btg
                                  elif i == 1:
                                      S.op("pool", lambda e, acc=acc, tg=tg: e.tensor_tensor(acc[:, 0:n], acc[:, 0:n], tg[:, 0:n], op=ALU.add),
                                           reads=[bacc, btg], writes=[bacc])
                                  else:
                                      S.op("pool", lambda e, acc=acc, tg=tg, j=j: e.tensor_tensor(MB[:, j, tc_], acc[:, 0:n], tg[:, 0:n], op=ALU.add),
                                           reads=[bacc, btg], writes=[b_MB, b_MBt[t]])
                          piece_done()

                      stage('G')
                      for h in range(2):
                          wo, bwo = piece("o%d" % h)
                          for tl in tiles:
                              t, c0, n = tl["t"], tl["c0"], tl["n"]
                              tc_ = slice(c0, c0 + n)
                              for e4 in range(4):
                                  ec = h * 4 + e4
                                  ps, bp = psum.get()
                                  mm_group(ps[:, 0:n], bp, [(wo[:, k, e4 * 128:(e4 + 1) * 128], MB[:, k, tc_]) for k in range(NCH)],
                                           reads=[bwo, b_MBt[t]])
                                  S.op("dve", lambda e, ps=ps, ec=ec: e.scalar_tensor_tensor(
                                      xT[:, ec, tc_], ps[:, 0:n], 0.25, xT[:, ec, tc_], op0=ALU.mult, op1=ALU.add),
                                      reads=[bp, b_xT[t][ec]], writes=[b_xT[t][ec]])
                                  if h == 1 and l + 1 < L and e4 == 0:
                                      for tn in tiles_at(l + 1)[0]:
                                          if (l + 1, tn["t"]) in norm_pre:
                                              norm_C(l + 1, tn)
                              if h == 1 and l + 1 < L:
                                  for tn in tiles_at(l + 1)[0]:
                                      if tn["t"] == t:
                                          norm_AB(l + 1, tn)
                          piece_done()

                  stage('O')
                  io_i = 0
                  for tl in tiles_full:
                      for bi, gb in enumerate(tl["blocks"]):
                          if gb != "S" and gb < first_out_gb:
                              continue
                          c0 = tl["c0"] + bi * 128
                          io, bio = IO[io_i % 2], b_IO[io_i % 2]
                          for h in range(2):
                              ps, bp = psum.get()
                              for c in range(4):
                                  cc = h * 4 + c
                                  S.op("pe", lambda e, cc=cc, c=c, ps=ps, c0=c0: e.transpose(ps[:, c * 128:(c + 1) * 128], xT[:, cc, c0:c0 + 128], IDENT[:]),
                                       reads=[b_xT[tl["t"]][cc], b_const], writes=[bp])
                              if h == 0:
                                  S.op("act", lambda e, ps=ps, io=io: e.copy(io[:, 0:512], ps[:]), reads=[bp], writes=[bio])
                              else:
                                  S.op("dve", lambda e, ps=ps, io=io: e.tensor_copy(io[:, 512:1024], ps[:]), reads=[bp], writes=[bio])
                          dst = ys_o if gb == "S" else yp_o[(gb - first_out_gb) * 128:(gb - first_out_gb + 1) * 128, :]
                          S.dma("sp", "io%d" % (io_i % 2), dst, io[:], reads=[bio])
                          io_i += 1

            except StopBuild:
                pass
            if stop_after is None:
                assert wstate["cur"] == len(schedule) - 1
            S.wait_all_dma("sp")
            for e in ("pe", "act", "dve", "pool"):
                pass
            build.stats = dict(nops=dict(S.nops), nwaits=S.nwaits, nsig=dict(S.sig))

        dry = Sched(nc, ctx, signal=None)
        emit_all(dry)
        emit_all(Sched(nc, ctx, signal=dry.signal))
    return nc


def _piece_cols():
    cols = {}
    qperm = np.empty(512, np.int64)
    for g in range(4):
        for kv in range(2):
            qperm[g * 128 + kv * 64:g * 128 + kv * 64 + 64] = O_Q + (kv * 4 + g) * 64 + np.arange(64)
    cols["q"] = qperm
    cols["kv"] = np.concatenate([O_K + np.arange(128), O_V + np.arange(128)])
    cols["za"] = O_ZA + np.arange(512)
    for c in range(4):
        r = np.arange(128) + c * 128
        cols["b%d" % c] = np.concatenate([O_GC + r, O_HB + r, O_GB + r, O_ZB + r])
    cols["vc"] = O_VC + np.arange(512)
    cols["u"] = O_U + np.arange(512)
    cols["zc"] = O_ZC + np.arange(512)
    for j in range(8):
        r = np.arange(128) + j * 128
        cols["g%d" % j] = np.concatenate([O_G + r, O_G + 1024 + r, O_G + 2048 + r])
    return cols


def pack_weights(w_in, w_out_a, w_out_b, w_out_c, w_o):
    L = w_in.shape[0]
    cols = _piece_cols()
    wp = np.empty((L, 128, PCOLS), np.float32)
    for l in range(L):
        for (pn, kk, g) in PIECES:
            off = PIECE_OFF[pn][0]
            if pn.startswith("g"):
                j = int(pn[1:])
                a = w_in[l][:, cols[pn]].reshape(8, 128, g).transpose(1, 0, 2)
                wo = np.concatenate([w_out_a[l][:, j * 128:(j + 1) * 128], w_out_b[l][:, j * 128:(j + 1) * 128],
                                     w_out_c[l][:, j * 128:(j + 1) * 128]], axis=1)
                b = wo.reshape(4, 128, g).transpose(1, 0, 2)
                blk = np.concatenate([a, b], axis=1)
            elif pn.startswith("o"):
                h = int(pn[1:])
                blk = w_o[l][:, h * 512:(h + 1) * 512].reshape(8, 128, g).transpose(1, 0, 2)
            else:
                blk = w_in[l][:, cols[pn]].reshape(8, 128, g).transpose(1, 0, 2)
            wp[l, :, off:off + kk * g] = blk.reshape(128, kk * g)
    return wp


def attn_bias_tiles(first_masked):
    h = np.arange(1, 9, dtype=np.float64)
    slopes = np.exp2(-8.0 * h / 8).reshape(2, 4)
    out = np.empty((10, 128, 512), np.float32)
    j = np.arange(128)[:, None]
    i = np.arange(128)[None, :]
    for kv in range(2):
        for g in range(4):
            s = slopes[kv, g]
            dp = i + 128 - j
            out[0 + kv][:, g * 128:(g + 1) * 128] = np.where(dp < 128, -s * dp, NEG)
            dc = i - j
            out[2 + kv][:, g * 128:(g + 1) * 128] = np.where(dc >= 0, -s * dc, NEG)
        out[4 + kv] = NEG if first_masked else out[0 + kv]
        t = np.arange(TS)[None, :]
        for g in range(4):
            s = slopes[kv, g]
            d = t + 128 - j
            tile_ = np.where(j > t, -s * d, NEG)
            for n in range(NSEQ):
                out[6 + kv][:, n * 32 + g * 8:n * 32 + g * 8 + 8] = tile_
        rn = (np.arange(128) // TS)[:, None]
        rs = (np.arange(128) % TS)[:, None]
        cn = (np.arange(128) // TS)[None, :]
        ct = (np.arange(128) % TS)[None, :]
        for g in range(4):
            s = slopes[kv, g]
            ok = (rn == cn) & (rs <= ct)
            out[8 + kv][:, g * 128:(g + 1) * 128] = np.where(ok, -s * (ct - rs), NEG)
    return np.exp(out.astype(np.float64)).astype(np.float32)


def host_common(inputs, L):
    f = np.float32
    norm_w, b_gate = inputs["norm_w"], inputs["b_gate"]
    c = {}
    c["wp"] = pack_weights(inputs["w_in"], inputs["w_out_a"], inputs["w_out_b"], inputs["w_out_c"], inputs["w_o"])
    c["nw"] = np.ascontiguousarray(norm_w.reshape(L, 8, 128).transpose(2, 0, 1).reshape(128, L * 8)).astype(f)
    c["bg"] = np.ascontiguousarray(b_gate.reshape(L, 24, 128).transpose(2, 0, 1).reshape(128, L * 24)).astype(f)
    c["qw"] = np.ascontiguousarray(np.tile(inputs["q_norm_w"], (1, 2)).T).astype(f)
    c["kw"] = np.ascontiguousarray(np.tile(inputs["k_norm_w"], (1, 2)).T).astype(f)
    sk = np.empty((128, L * 4), f)
    for l in range(L):
        for kv in range(2):
            for c2 in range(2):
                for par in range(2):
                    sk[par * 64:(par + 1) * 64, l * 4 + kv * 2 + c2] = inputs["sinks"][l, kv * 4 + 2 * c2 + par]
    c["sk4"] = sk
    c["cw"] = np.ascontiguousarray(inputs["conv_w"].reshape(L, 3, 4, 128).transpose(3, 0, 1, 2).reshape(128, L * 12)).astype(f)
    c["vnw"] = np.ascontiguousarray(np.broadcast_to(inputs["v_norm_w"][:, None, :], (L, 128, 512))).astype(f)
    c["wsp"] = np.ascontiguousarray(inputs["w_spatial"]).astype(f)
    bs = inputs["b_spatial"]
    c["bspt"] = np.ascontiguousarray(np.broadcast_to(bs.reshape(L, 1, 512), (L, 128, 512))).astype(f)
    bss = np.tile(bs[:, :, :TS], (1, 1, NSEQ))
    c["bspts"] = np.ascontiguousarray(np.broadcast_to(bss.reshape(L, 1, 512), (L, 128, 512))).astype(f)
    c["ident"] = np.eye(128, dtype=f)
    s_ = np.arange(128)
    c["tri"] = (s_[:, None] <= s_[None, :]).astype(f)
    c["bdm"] = ((s_[:, None] // TS) == (s_[None, :] // TS)).astype(f)
    c["bd64"] = ((s_[:, None] // 64) == (s_[None, :] // 64)).astype(f)
    c["repi"] = (np.arange(8)[:, None] == (s_[None, :] % TS)).astype(f)
    return c


def kernel(**inputs):
    L = 4
    n_cores = 8
    x_prompt = np.asarray(inputs["x_prompt"], np.float32)
    x_sample = np.asarray(inputs["x_sample"], np.float32)
    inputs = {k: np.asarray(v) for k, v in inputs.items()}
    B_, S_, _ = x_prompt.shape
    npb, halo = 20, 4
    cores_per_b = n_cores // B_
    tok_per_core = S_ // cores_per_b
    common = host_common(inputs, L)
    bias_mid = attn_bias_tiles(False)
    bias_first = attn_bias_tiles(True)
    in_maps = []
    for c in range(n_cores):
        b, q = divmod(c, cores_per_b)
        t0 = q * tok_per_core
        xp = np.zeros((npb * 128, D), np.float32)
        if q == 0:
            xp[halo * 128:] = x_prompt[b, t0:t0 + tok_per_core]
        else:
            xp[:] = x_prompt[b, t0 - halo * 128:t0 + tok_per_core]
        n0 = c * NSEQ
        m = dict(common)
        m["xp"] = xp
        m["xs"] = np.ascontiguousarray(x_sample[n0:n0 + NSEQ].reshape(NSEQ * TS, D))
        m["ck"] = np.ascontiguousarray(inputs["cache_k"][:, n0:n0 + NSEQ].reshape(L, NSEQ, 128, 128))
        m["cv"] = np.ascontiguousarray(inputs["cache_v"][:, n0:n0 + NSEQ].reshape(L, NSEQ, 128, 128))
        m["sconv"] = np.ascontiguousarray(inputs["state_conv"][:, n0:n0 + NSEQ].reshape(L, NSEQ * 2, 512))
        m["biasall"] = bias_first if q == 0 else bias_mid
        in_maps.append(m)
    nc = build(L, default_sts(), npb, halo)
    res = run_bass_kernel_spmd(nc, in_maps, core_ids=list(range(n_cores)))
    R = res.results
    yp = np.concatenate([R[c]["yp"] for c in range(n_cores)], axis=0).reshape(B_, S_, D)
    ys = np.concatenate([R[c]["ys"] for c in range(n_cores)], axis=0).reshape(n_cores * NSEQ, TS, D)
    last = [b * cores_per_b + cores_per_b - 1 for b in range(B_)]
    pk = np.stack([R[c]["pk"] for c in last], axis=1).reshape(L, B_, 128, 2, 64)
    pv = np.stack([R[c]["pv"] for c in last], axis=1).reshape(L, B_, 128, 2, 64)
    pc = np.stack([R[c]["pc"] for c in last], axis=1).reshape(L, B_, 2, 512)
    sk = np.concatenate([R[c]["sko"] for c in range(n_cores)], axis=1).reshape(L, n_cores * NSEQ, 128, 2, 64)
    sv = np.concatenate([R[c]["svo"] for c in range(n_cores)], axis=1).reshape(L, n_cores * NSEQ, 128, 2, 64)
    sc = np.concatenate([R[c]["sco"].reshape(L, NSEQ, 2, 512) for c in range(n_cores)], axis=1)
    scv = np.concatenate([R[c]["scv"].reshape(L, NSEQ, TS, 512) for c in range(n_cores)], axis=1)
    f = np.float32
    return (yp.astype(f), ys.astype(f), pk.astype(f), pv.astype(f), pc.astype(f),
            sk.astype(f), sv.astype(f), sc.astype(f), scv.astype(f))
```

```python
import numpy as np
from contextlib import ExitStack

import concourse.bass as bass
import concourse.mybir as mybir
from concourse.bass_utils import run_bass_kernel_spmd

F32 = mybir.dt.float32
BF16 = mybir.dt.bfloat16
AF = mybir.ActivationFunctionType
ALU = mybir.AluOpType

D = 1024
NCH = 8
NSEQ = 16
TS = 8
EPS = 1e-6
NEG = -30000.0

O_Q, O_K, O_V, O_ZA = 0, 512, 640, 768
O_GB, O_GC, O_HB, O_ZB = 1280, 1792, 2304, 2816
O_U, O_VC, O_ZC, O_G = 3328, 3840, 4352, 4864

PIECES = ([("q", 8, 512), ("kv", 8, 256), ("za", 8, 512)]
          + [("b%d" % c, 8, 512) for c in range(4)]
          + [("vc", 8, 512), ("u", 8, 512), ("zc", 8, 512)]
          + [("g%d" % j, 12, 384) for j in range(8)]
          + [("o0", 8, 512), ("o1", 8, 512)])
PIECE_OFF = {}
_o = 0
for _n, _k, _g in PIECES:
    PIECE_OFF[_n] = (_o, _k, _g)
    _o += _k * _g
PCOLS = _o
SLOT_ELEMS = 12 * 384


class Buf:
    __slots__ = ("name", "writer", "readers", "gen", "excl")

    def __init__(self, name, excl=False):
        self.name = name
        self.writer = None
        self.readers = {}
        self.gen = 0
        self.excl = excl


class _Dummy:
    pass


class Sched:
    ENG = ("pe", "act", "dve", "pool", "sp")

    def __init__(self, nc, ctx, signal=None):
        self.nc = nc
        self.dry = signal is None
        self.signal = {e: set() for e in self.ENG} if self.dry else signal
        self.eng = {"pe": nc.tensor, "act": nc.scalar, "dve": nc.vector, "pool": nc.gpsimd, "sp": nc.sync}
        self.sem = {}
        self.raw = {e: 0 for e in self.ENG}
        self.sig = {e: 0 for e in self.ENG}
        for e in self.ENG:
            self.sem[e] = _Dummy() if self.dry else ctx.enter_context(nc.semaphore("s_" + e))
        self.known = {e: {} for e in self.ENG}
        self.ctx = ctx
        self.dma_sems = {}
        self.nwaits = 0
        self.nops = {e: 0 for e in self.ENG}

    def _waits(self, e, reads, writes):
        evs = {}

        def add(ev):
            if ev is None:
                return
            s, v, src, raw = ev
            if src == "pe" and e == "pe":
                return
            k = id(s)
            if k not in evs or evs[k][1] < v:
                evs[k] = ev

        for b in reads:
            add(b.writer)
            if b.excl:
                for ev in b.readers.values():
                    if ev[2] != e:
                        add(ev)
        for b in writes:
            add(b.writer)
            for ev in b.readers.values():
                add(ev)
        kn = self.known[e]
        for k, (s, v, src, raw) in evs.items():
            if kn.get(k, 0) >= v:
                continue
            if self.dry:
                if src != "dma":
                    self.signal[src].add(raw)
            else:
                assert src == "dma" or raw in self.signal[src], "wait on a non-signalling instruction"
                self.eng[e].wait_ge(s, v)
            kn[k] = v
            self.nwaits += 1

    @staticmethod
    def _commit(ev, reads, writes):
        k = id(ev[0])
        for b in writes:
            b.writer = ev
            b.readers = {}
        for b in reads:
            old = b.readers.get(k)
            if old is None or old[1] < ev[1]:
                b.readers[k] = ev

    def op(self, e, fn, reads=(), writes=()):
        reads = _unlease(reads)
        writes = _unlease(writes)
        self._waits(e, reads, writes)
        self.raw[e] += 1
        self.nops[e] += 1
        raw = self.raw[e]
        if self.dry:
            ev = (self.sem[e], raw, e, raw)
        else:
            ins = fn(self.eng[e])
            if raw in self.signal[e]:
                self.sig[e] += 1
                ins.then_inc(self.sem[e], 1)
            ev = (self.sem[e], self.sig[e], e, raw)
        self._commit(ev, reads, writes)
        return ev

    def dma(self, q, semname, out, in_, reads=(), writes=(), **kw):
        if semname not in self.dma_sems:
            self.dma_sems[semname] = [_Dummy() if self.dry else self.ctx.enter_context(self.nc.semaphore("d_" + semname)), 0]
        ent = self.dma_sems[semname]
        reads = _unlease(reads)
        writes = _unlease(writes)
        self._waits(q, reads, writes)
        ent[1] += 16
        if not self.dry:
            ins = self.eng[q].dma_start(out=out, in_=in_, **kw)
            ins.then_inc(ent[0], 16)
        ev = (ent[0], ent[1], "dma", None)
        self._commit(ev, reads, writes)
        return ev

    def wait_all_dma(self, e):
        if self.dry:
            return
        for name, (s, v) in self.dma_sems.items():
            if v > 0 and self.known[e].get(id(s), 0) < v:
                self.eng[e].wait_ge(s, v)
                self.known[e][id(s)] = v


class Lease:
    __slots__ = ("buf", "gen")

    def __init__(self, buf, gen):
        self.buf = buf
        self.gen = gen


def _unlease(bs):
    out = []
    for b in bs:
        if isinstance(b, Lease):
            assert b.gen == b.buf.gen, "stale lease on %s" % b.buf.name
            b = b.buf
        out.append(b)
    return out


class Pool:
    def __init__(self, aps, name, excl=False, bufs=None):
        self.aps = aps
        self.bufs = bufs if bufs is not None else [Buf("%s%d" % (name, i), excl) for i in range(len(aps))]
        self.i = 0

    def get(self):
        i = self.i
        self.i = (i + 1) % len(self.aps)
        self.bufs[i].gen += 1
        return self.aps[i], Lease(self.bufs[i], self.bufs[i].gen)


def default_sts():
    return [
        [dict(kind="P", blocks=[0, 1, 2, 3]), dict(kind="P", blocks=[4, 5, 6])],
        [dict(kind="P", blocks=[7, 8, 9, 10]), dict(kind="P", blocks=[11, 12, 13])],
        [dict(kind="P", blocks=[14, 15, 16, 17]), dict(kind="P", blocks=[18, 19]), dict(kind="S", blocks=["S"])],
    ]


class StopBuild(Exception):
    pass


def build(depth, sts, npb, first_out_gb, nslots=5, dbg=None, stop_after=None, variant=0):
    L = depth
    last_gb = npb - 1
    nc = bass.Bass("TRN2", target_bir_lowering=False)

    def din(name, shape):
        return nc.dram_tensor(name, list(shape), F32, kind="ExternalInput").ap()

    def dout(name, shape):
        return nc.dram_tensor(name, list(shape), F32, kind="ExternalOutput").ap()

    n_out_blocks = npb - first_out_gb
    xp = din("xp", [npb * 128, D])
    xs = din("xs", [128, D])
    ck_d = din("ck", [L, NSEQ, 128, 128])
    cv_d = din("cv", [L, NSEQ, 128, 128])
    sconv_d = din("sconv", [L, NSEQ * 2, 512])
    wp_d = din("wp", [L, 128, PCOLS])
    nw_d = din("nw", [128, L * 8])
    bg_d = din("bg", [128, L * 24])
    qw_d = din("qw", [128, L])
    kw_d = din("kw", [128, L])
    sk_d = din("sk4", [128, L * 4])
    cw_d = din("cw", [128, L * 12])
    vnw_d = din("vnw", [L, 128, 512])
    wsp_d = din("wsp", [L, 4, 128, 128])
    bspt_d = din("bspt", [L, 128, 512])
    bspts_d = din("bspts", [L, 128, 512])
    ident_d = din("ident", [128, 128])
    tri_d = din("tri", [128, 128])
    bdm_d = din("bdm", [128, 128])
    bd64_d = din("bd64", [128, 128])
    repi_d = din("repi", [8, 128])
    bias_d = din("biasall", [10, 128, 512])

    yp_o = dout("yp", [n_out_blocks * 128, D])
    ys_o = dout("ys", [128, D])
    pk_o = dout("pk", [L, 128, 128])
    pv_o = dout("pv", [L, 128, 128])
    pc_o = dout("pc", [L, 2, 512])
    sk_o = dout("sko", [L, NSEQ, 128, 128])
    sv_o = dout("svo", [L, NSEQ, 128, 128])
    sc_o = dout("sco", [L, NSEQ * 2, 512])
    scv_o = dout("scv", [L, 128, 512])
    dbg_o = None
    if dbg:
        dbg_o = dout("dbg", [128, dbg])

    with ExitStack() as ctx:
        def sb(name, shape, dt):
            return ctx.enter_context(nc.sbuf_tensor(name, list(shape), dt))

        STW = max(sum(len(t["blocks"]) for t in st) for st in sts) * 128

        xT = sb("xT", [128, NCH, STW], F32)
        xnT = sb("xnT", [128, NCH, STW], BF16)
        PA = sb("PA", [128, 4, STW], BF16)
        PB = sb("PB", [128, 4, STW], BF16)
        PC = sb("PC", [128, 4, STW], BF16)
        QT = PB
        MBR = sb("MBR", [128, NCH * STW], BF16)
        MB = MBR[:].rearrange("p (c n) -> p c n", c=NCH)
        YX = MBR[:].bitcast(F32).rearrange("p (c n) -> p c n", c=4)
        CE = sb("CE", [128, 2 + STW], F32)
        CES = sb("CES", [128, NSEQ, 10], F32)
        KTe = sb("KTe", [128, STW + 128], BF16)
        Ve = sb("Ve", [128, STW // 128 + 1, 128], BF16)
        KTb = sb("KTb", [128, L, 128], BF16)
        Vb = sb("Vb", [128, L, 128], BF16)
        CEb = sb("CEb", [128, L, 4, 2], F32)
        TMPS = [sb("tmp%d" % i, [128, 512], F32) for i in range(7)]
        BTS = [sb("bt%d" % i, [128, 512], BF16) for i in range(6)]
        NEGH = sb("NEGH", [128, 64], F32)
        R2S = [sb("r2s%d" % i, [128, 128], F32) for i in range(8)]
        SMALL = [sb("sm%d" % i, [128, 256], F32) for i in range(2)]
        IO = [sb("io%d" % i, [128, D], F32) for i in range(2)]
        KOs = sb("KOs", [128, 128], F32)
        VOs = KOs
        CKT = sb("CKT", [128, NSEQ, 128], BF16)
        STG = sb("STG", [96, 512], F32)
        SCo = STG[0:32, :]
        PCs = STG[32:34, :]
        SCs = STG[64:96, :]
        C32 = sb("C32", [128, 32], F32)
        SSS = [sb("SS%d" % i, [128, 4], F32) for i in range(4)]
        NRT = [sb("NRT%d" % i, [128, 8], F32) for i in range(3)]
        RPS = [sb("RP%d" % i, [128, 8], F32) for i in range(6)]
        IDENT = sb("IDENT", [128, 128], F32)
        IDENTB = sb("IDENTB", [128, 128], BF16)
        ONESB = sb("ONESB", [128, 128], BF16)
        BD64 = sb("BD64", [128, 128], BF16)
        TRI = sb("TRI", [128, 128], BF16)
        BDM = sb("BDM", [128, 128], BF16)
        REPI = sb("REPI", [8, 128], BF16)
        BIAS = sb("BIAS", [128, 8, 512], F32)
        WMT = sb("WMT", [128, L, 4, 128], BF16)
        WMTS = sb("WMTS", [128, L, 4, 128], BF16)
        WSPL = sb("WSPL", [128, 4, 128], BF16)
        VNW = sb("VNW", [128, 512], F32)
        BSPT = sb("BSPT", [128, 512], F32)
        BSPTS = sb("BSPTS", [128, 512], F32)
        NW = sb("NW", [128, L * 8], F32)
        BGH = sb("BGH", [128, L * 24], F32)
        QW = sb("QW", [128, L], F32)
        KW = sb("KW", [128, L], F32)
        SE4 = sb("SE4", [128, L * 4], F32)
        CW = sb("CW", [128, L * 12], F32)
        EPSC = sb("EPSC", [128, 1], F32)
        print("sbuf before ring:", nc.sbuf_bytes_remaining, "need", nslots * SLOT_ELEMS * 2)
        RING = [sb("ring%d" % i, [128, SLOT_ELEMS], BF16) for i in range(nslots)]
        PSB = [ctx.enter_context(nc.psum_tensor("ps%d" % i, [128, 512], F32)) for i in range(8)]

        block = ctx.enter_context(nc.Block())
        def emit_all(S):

            psum = Pool([p[:] for p in PSB], "ps", excl=True)
            psum_lo = Pool([p[:] for p in PSB[0:5]], "ps", bufs=psum.bufs[0:5])
            psum_hi = Pool([p[:] for p in PSB[5:8]], "ps", bufs=psum.bufs[5:8])
            tmps = Pool([t[:] for t in TMPS], "tmp")
            bts = Pool([t[:] for t in BTS], "bt")
            smalls = Pool([t[:] for t in SMALL], "sm")
            r2s = Pool([t[:] for t in R2S], "r2s")
            sss = Pool([t[:] for t in SSS], "ss")
            rps = Pool([t[:] for t in RPS], "rp")

            B = {}

            def buf(name):
                if name not in B:
                    B[name] = Buf(name)
                return B[name]

            ntile_max = max(len(st) for st in sts)
            b_xT = [[buf("xT%d_%d" % (t, c)) for c in range(NCH)] for t in range(ntile_max)]
            b_xnT = [buf("xnT%d" % t) for t in range(ntile_max)]
            b_PA = [buf("PA%d" % t) for t in range(ntile_max)]
            b_PB = [buf("PB%d" % t) for t in range(ntile_max)]
            b_PC = [buf("PC%d" % t) for t in range(ntile_max)]
            b_MB = buf("MBR")
            b_MBt = [buf("MBt%d" % t) for t in range(ntile_max)]
            b_CE = buf("CE")
            b_CES = buf("CES")
            b_KT = [buf("KT%d" % i) for i in range(STW // 128 + 1)]
            b_V = [buf("V%d" % i) for i in range(STW // 128 + 1)]
            b_KTb = [buf("KTb%d" % l) for l in range(L)]
            b_Vb = [buf("Vb%d" % l) for l in range(L)]
            b_CEb = [buf("CEb%d" % l) for l in range(L)]
            b_IO = [buf("IO0"), buf("IO1")]
            b_const = buf("const")
            b_ring = [buf("ring%d" % i) for i in range(nslots)]

            schedule = []
            for si in range(len(sts)):
                for l in range(L):
                    for (pn, pk_, pg) in PIECES:
                        schedule.append((l, pn))
            wstate = dict(next=0, cur=-1)

            def pump(limit):
                while wstate["next"] < min(limit, len(schedule)):
                    k = wstate["next"]
                    l, pn = schedule[k]
                    off, kk, g = PIECE_OFF[pn]
                    slot = k % nslots
                    S.dma("pool", "ring%d" % slot, RING[slot][:, 0:kk * g], wp_d[l, :, off:off + kk * g], writes=[b_ring[slot]])
                    wstate["next"] = k + 1

            def piece(pn):
                wstate["cur"] += 1
                k = wstate["cur"]
                assert schedule[k][1] == pn, (schedule[k], pn)
                pump(k + 1)
                off, kk, g = PIECE_OFF[pn]
                slot = k % nslots
                return RING[slot][:, 0:kk * g].rearrange("p (k g) -> p k g", k=kk), b_ring[slot]

            def piece_done():
                pump(wstate["cur"] + nslots + 1)

            pump(nslots)

            cq = "sp"
            for dst, src in [(IDENT, ident_d), (NW, nw_d), (BGH, bg_d), (QW, qw_d), (KW, kw_d), (SE4, sk_d), (CW, cw_d)]:
                S.dma(cq, "const", dst[:], src, writes=[b_const])
            S.dma(cq, "const", BIAS[:, 0:4, :], bias_d[0:4].rearrange("k p n -> p k n"), writes=[b_const])
            for dst, src in [(IDENTB, ident_d), (TRI, tri_d), (BDM, bdm_d), (BD64, bd64_d), (REPI, repi_d)]:
                S.dma("pool", "constp", dst[:], src, writes=[b_const])
            S.op("dve", lambda e: e.memset(ONESB[:], 1.0), writes=[b_const])
            S.op("dve", lambda e: e.memset(NEGH[:], -0.5), writes=[b_const])
            S.op("dve", lambda e: e.memset(EPSC[:], EPS), writes=[b_const])
            S.op("dve", lambda e: e.memset(KTb[:], 0.0), writes=b_KTb)
            S.op("dve", lambda e: e.memset(Vb[:], 0.0), writes=b_Vb)
            S.op("dve", lambda e: e.memset(CEb[:], 0.0), writes=b_CEb)
            S.op("dve", lambda e: e.tensor_scalar(BGH[:], BGH[:], 0.5, None, op0=ALU.mult), reads=[b_const], writes=[b_const])
            S.op("act", lambda e: e.activation(SE4[:], SE4[:], AF.Exp), reads=[b_const], writes=[b_const])
            for l in range(L):
                bw = buf("WSPL")
                S.dma("pool", "wspl", WSPL[:], wsp_d[l].rearrange("g t s -> t g s"), writes=[bw])
                pst, bp = psum.get()
                pstb = pst.bitcast(BF16)
                for g in range(4):
                    S.op("pe", lambda e, g=g: e.transpose(pstb[:, g * 128:(g + 1) * 128], WSPL[:, g, :], IDENTB[:]),
                         reads=[bw, b_const], writes=[bp])
                for g in range(4):
                    S.op("dve", lambda e, g=g, l=l: e.tensor_tensor(WMT[:, l, g, :], pstb[:, g * 128:(g + 1) * 128], TRI[:], op=ALU.mult),
                         reads=[bp, b_const], writes=[b_const])
                ps2, bp2 = psum.get()
                for g in range(4):
                    S.op("pe", lambda e, g=g, l=l: e.matmul(
                        ps2[:, g * 128:(g + 1) * 128].rearrange("p (n t) -> p n t", n=NSEQ), REPI[:],
                        WMT[0:8, l, g, 0:8].unsqueeze(1).to_broadcast([8, NSEQ, 8]), start=True, stop=True),
                        reads=[b_const], writes=[bp2])
                for g in range(4):
                    S.op("dve", lambda e, g=g, l=l: e.tensor_tensor(WMTS[:, l, g, :], ps2[:, g * 128:(g + 1) * 128], BDM[:], op=ALU.mult),
                         reads=[bp2, b_const], writes=[b_const])


            def mm_group(ps_ap, bp, pairs, reads, **kw):
                n = len(pairs)
                for i, (lh, rh) in enumerate(pairs):
                    S.op("pe", lambda e, lh=lh, rh=rh, i=i: e.matmul(ps_ap, lh, rh, start=(i == 0), stop=(i == n - 1), **kw),
                         reads=reads, writes=[bp])

            def inproj(ps_ap, bp, w_ap, bw, col0, ncols, tcols, t):
                mm_group(ps_ap, bp, [(w_ap[:, k, col0:col0 + ncols], xnT[:, k, tcols]) for k in range(NCH)],
                         reads=[bw, b_xnT[t]])

            def rstd_part1a(sq_list, sq_bufs, n, H, rhs_sel, scale, r=None, br=None):
                nb = n // 128
                pst, bpst = psum_hi.get()
                for b in range(nb):
                    mm_group(pst[:, b * H:(b + 1) * H], bpst, [(sq[:, b * 128:(b + 1) * 128], rhs_sel) for sq in sq_list],
                             reads=list(sq_bufs) + [b_const])
                if r is None:
                    r, br = rps.get()
                w = nb * H
                S.op("act", lambda e: e.activation(r[:, 0:w], pst[:, 0:w], AF.Identity, bias=EPSC[:, 0:1], scale=scale),
                     reads=[bpst, b_const], writes=[br])
                S.op("pool", lambda e: e.tensor_tensor(r[:, 0:w], r[:, 0:w], NEGH[:, 0:w], op=ALU.pow),
                     reads=[br, b_const], writes=[br])
                return r, br

            def rstd_part1b(r, br, n, H):
                r2l = []
                for b in range(n // 128):
                    r2, br2 = r2s.get()
                    S.op("dve", lambda e, b=b, r2=r2: e.tensor_copy(
                        r2.rearrange("p (h d) -> p h d", h=H), r[:, b * H:(b + 1) * H].unsqueeze(2).to_broadcast([128, H, 128 // H])),
                        reads=[br], writes=[br2])
                    r2l.append((r2, br2))
                return r2l

            def rstd_part1(sq_list, sq_bufs, n, H, rhs_sel, scale):
                r, br = rstd_part1a(sq_list, sq_bufs, n, H, rhs_sel, scale)
                return rstd_part1b(r, br, n, H)

            def rstd_part2(r2l, n, to_sbuf):
                pbc, bpbc = psum_hi.get()
                for b, (r2, br2) in enumerate(r2l):
                    S.op("pe", lambda e, b=b, r2=r2: e.transpose(pbc[:, b * 128:(b + 1) * 128], r2, IDENT[:]),
                         reads=[br2, b_const], writes=[bpbc])
                if not to_sbuf:
                    return pbc, bpbc, (pbc, bpbc)
                rs, brs = tmps.get()
                S.op("act", lambda e: e.copy(rs[:, 0:n], pbc[:, 0:n]), reads=[bpbc], writes=[brs])
                return rs, brs, (pbc, bpbc)

            def run_pipeline(items, order):
                ns = len(items[0])
                for step in range(len(items) + ns - 1):
                    for s_ in order:
                        k = step - s_
                        if 0 <= k < len(items):
                            items[k][s_]()

            def silu2(ps_ap, bp, n):
                th, bth = tmps.get()
                S.op("act", lambda e: e.activation(th[:, 0:n], ps_ap, AF.Tanh, scale=0.5), reads=[bp], writes=[bth])
                t2, bt2 = tmps.get()
                S.op("dve", lambda e: e.scalar_tensor_tensor(t2[:, 0:n], th[:, 0:n], 1.0, ps_ap, op0=ALU.add, op1=ALU.mult),
                     reads=[bth, bp], writes=[bt2])
                return t2, bt2

            def transpose_out(src_ap, bsrc, rows, ncols_src, dst_sb, bdst, dst_cols, ps_lease=None):
                ps, bp = ps_lease if ps_lease is not None else psum.get()
                S.op("pe", lambda e: e.transpose(ps[0:ncols_src, 0:rows], src_ap, IDENT[0:rows, 0:rows]),
                     reads=[bsrc, b_const], writes=[bp])
                S.op("act", lambda e: e.copy(dst_sb[0:ncols_src, dst_cols], ps[0:ncols_src, 0:rows]), reads=[bp], writes=[bdst])

            cur_si = [0]

            def stage(name):
                if stop_after == name or stop_after == "%d:%s" % (cur_si[0], name):
                    raise StopBuild()

            try:
              stage("setup")
              for si, st in enumerate(sts):
                  cur_si[0] = si
                  tiles = []
                  col = 0
                  for t, tl in enumerate(st):
                      n = 128 * len(tl["blocks"])
                      tiles.append(dict(kind=tl["kind"], c0=col, n=n, blocks=tl["blocks"], t=t))
                      col += n
                  nblk = col // 128
                  has_sample = any(tl["kind"] == "S" for tl in tiles)
                  n_prompt_blk = sum(len(tl["blocks"]) for tl in tiles if tl["kind"] == "P")
                  first_blk_is_out = None

                  io_i = 0
                  for tl in tiles:
                      for bi, gb in enumerate(tl["blocks"]):
                          c0 = tl["c0"] + bi * 128
                          src = xs if gb == "S" else xp[gb * 128:(gb + 1) * 128, :]
                          io, bio = IO[io_i % 2], b_IO[io_i % 2]
                          S.dma("sp", "io%d" % (io_i % 2), io[:], src, writes=[bio])
                          io_i += 1
                          for h in range(2):
                              ps, bp = psum.get()
                              for c in range(4):
                                  cc = h * 4 + c
                                  S.op("pe", lambda e, cc=cc, c=c, ps=ps, io=io: e.transpose(ps[:, c * 128:(c + 1) * 128], io[:, cc * 128:(cc + 1) * 128], IDENT[:]),
                                       reads=[bio, b_const], writes=[bp])
                              eng = "act" if h == 0 else "dve"
                              if eng == "act":
                                  S.op("act", lambda e, h=h, ps=ps, c0=c0: e.copy(xT[:, h * 4:(h + 1) * 4, c0:c0 + 128], ps.rearrange("p (c n) -> p c n", c=4)),
                                       reads=[bp], writes=b_xT[tl["t"]][h * 4:(h + 1) * 4])
                              else:
                                  S.op("dve", lambda e, h=h, ps=ps, c0=c0: e.tensor_copy(xT[:, h * 4:(h + 1) * 4, c0:c0 + 128], ps.rearrange("p (c n) -> p c n", c=4)),
                                       reads=[bp], writes=b_xT[tl["t"]][h * 4:(h + 1) * 4])

                  stage('load')
                  if si == 0:
                      for l_ in range(L):
                          S.dma("sp", "cachecopy", sk_o[l_, :, 0:128 - TS, :], ck_d[l_, :, TS:128, :])
                          S.dma("sp", "cachecopy", sv_o[l_, :, 0:128 - TS, :], cv_d[l_, :, TS:128, :])
                  if si == 0:
                      S.dma("sp", "biasx", BIAS[:, 4:6, :], bias_d[4:6].rearrange("k p n -> p k n"), writes=[buf("biasx")])
                  if has_sample:
                      S.dma("sp", "biasx", BIAS[:, 4:8, :], bias_d[6:10].rearrange("k p n -> p k n"), writes=[buf("biasx")])
                  b_biasx = buf("biasx")

                  tiles_full = tiles

                  def tiles_at(l_):
                      min_gb = first_out_gb - (L - l_)
                      res = []
                      for lo in (min_gb, min_gb + 1 if min_gb >= 0 else min_gb):
                          lst = []
                          for tl in tiles_full:
                              keep = [(i, gb) for i, gb in enumerate(tl["blocks"]) if gb == "S" or gb >= lo]
                              if not keep:
                                  continue
                              i0_ = keep[0][0]
                              lst.append(dict(kind=tl["kind"], c0=tl["c0"] + 128 * i0_, n=128 * len(keep), blocks=[gb for _, gb in keep], t=tl["t"]))
                          res.append(lst)
                      return res

                  norm_pre = {}

                  def norm_AB(l_, tl):
                      t, c0, n = tl["t"], tl["c0"], tl["n"]
                      tc_ = slice(c0, c0 + n)
                      for c in range(NCH):
                          S.op("act", lambda e, c=c: e.activation(xnT[:, c, tc_], xT[:, c, tc_], AF.Square),
                               reads=[b_xT[t][c]], writes=[b_xnT[t]])
                      norm_pre[(l_, t)] = rstd_part1a([xnT[:, c, tc_] for c in range(NCH)], [b_xnT[t]], n, 1, ONESB[:, 0:1], 1.0 / D,
                                                      r=NRT[t][:], br=buf("NRT%d" % t))

                  stored = set()
                  store_i = [0]

                  def emit_store(tl):
                      stored.add(tl["t"])
                      for bi, gb in enumerate(tl["blocks"]):
                          if gb != "S" and gb < first_out_gb:
                              continue
                          c0 = tl["c0"] + bi * 128
                          io_i = store_i[0]
                          io, bio = IO[io_i % 2], b_IO[io_i % 2]
                          for h in range(2):
                              ps, bp = psum.get()
                              for c in range(4):
                                  cc = h * 4 + c
                                  S.op("pe", lambda e, cc=cc, c=c, ps=ps, c0=c0: e.transpose(ps[:, c * 128:(c + 1) * 128], xT[:, cc, c0:c0 + 128], IDENT[:]),
                                       reads=[b_xT[tl["t"]][cc], b_const], writes=[bp])
                              if h == 0:
                                  S.op("act", lambda e, ps=ps, io=io: e.copy(io[:, 0:512], ps[:]), reads=[bp], writes=[bio])
                              else:
                                  S.op("dve", lambda e, ps=ps, io=io: e.tensor_copy(io[:, 512:1024], ps[:]), reads=[bp], writes=[bio])
                          dst = ys_o if gb == "S" else yp_o[(gb - first_out_gb) * 128:(gb - first_out_gb + 1) * 128, :]
                          S.dma("sp", "io%d" % (io_i % 2), dst, io[:], reads=[bio])
                          store_i[0] += 1

                  norm_done = set()

                  def norm_C(l_, tl):
                      t, c0, n = tl["t"], tl["c0"], tl["n"]
                      tc_ = slice(c0, c0 + n)
                      rr, brr = norm_pre.pop((l_, t))
                      r, br, _ = rstd_part2(rstd_part1b(rr, brr, n, 1), n, False)
                      for c in range(NCH):
                          S.op("dve", lambda e, c=c: e.scalar_tensor_tensor(
                              xnT[:, c, tc_], xT[:, c, tc_], NW[:, l_ * 8 + c:l_ * 8 + c + 1], r[:, 0:n], op0=ALU.mult, op1=ALU.mult),
                              reads=[b_xT[t][c], br, b_const], writes=[b_xnT[t]])
                      norm_done.add((l_, t))

                  for l in range(L):
                      tiles_kv, tiles = tiles_at(l)
                      b_lay = buf("laycst")
                      S.dma("sp", "laycst", VNW[:], vnw_d[l], writes=[b_lay])
                      S.dma("sp", "laycst", BSPT[:], bspt_d[l], writes=[b_lay])
                      if has_sample:
                          S.dma("sp", "laycst", BSPTS[:], bspts_d[l], writes=[b_lay])

                      S.op("pool", lambda e: e.tensor_copy(KTe[:, 0:128], KTb[:, l, :]), reads=[b_KTb[l]], writes=[b_KT[0]])
                      S.op("pool", lambda e: e.tensor_copy(Ve[:, 0, :], Vb[:, l, :]), reads=[b_Vb[l]], writes=[b_V[0]])
                      if has_sample:
                          S.dma("pool", "ckl", IO[0][:].bitcast(BF16)[:, 0:NSEQ * 128].rearrange("p (n f) -> p n f", n=NSEQ),
                                ck_d[l].rearrange("n j f -> j n f"), writes=[b_IO[0]])
                          S.dma("pool", "cvl", IO[1][:].bitcast(BF16)[:, 0:NSEQ * 128].rearrange("p (n f) -> p n f", n=NSEQ),
                                cv_d[l].rearrange("n j f -> j n f"), writes=[b_IO[1]])
                          CK = IO[0][:].bitcast(BF16)[:, 0:NSEQ * 128].rearrange("p (n f) -> p n f", n=NSEQ)
                          CV = IO[1][:].bitcast(BF16)[:, 0:NSEQ * 128].rearrange("p (n f) -> p n f", n=NSEQ)
                      wts = {}

                      def mk_norm(tl):
                          t = tl["t"]

                          def fA():
                              if (l, t) not in norm_pre and (l, t) not in norm_done:
                                  norm_AB(l, tl)

                          def fC():
                              if (l, t) not in norm_done:
                                  norm_C(l, tl)
                              norm_done.discard((l, t))
                          return [fA, (lambda: None), fC, (lambda: None)]

                      def mk_q(tl, g, first, last):
                          t, c0, n = tl["t"], tl["c0"], tl["n"]
                          tc_ = slice(c0, c0 + n)
                          st_ = {}

                          def fA():
                              wq, bwq = wts["q"]
                              ps, bp = psum_lo.get()
                              inproj(ps[:, 0:n], bp, wq, bwq, g * 128, 128, tc_, t)
                              sq, bsq = bts.get()
                              S.op("act", lambda e: e.activation(sq[:, 0:n], ps[:, 0:n], AF.Square), reads=[bp], writes=[bsq])
                              st_.update(ps=ps, bp=bp, sq=sq, bsq=bsq)

                          def fB():
                              st_["rr"] = rstd_part1a([st_["sq"][:, 0:n]], [st_["bsq"]], n, 2, BD64[:, 0:128:64], 1.0 / 64)

                          def fR():
                              st_["r2l"] = rstd_part1b(st_["rr"][0], st_["rr"][1], n, 2)

                          def fC():
                              r, br, _ = rstd_part2(st_["r2l"], n, True)
                              ps, bp = st_["ps"], st_["bp"]
                              S.op("dve", lambda e: e.scalar_tensor_tensor(
                                  QT[:, g, tc_], ps[:, 0:n], QW[:, l:l + 1], r[:, 0:n], op0=ALU.mult, op1=ALU.mult),
                                  reads=[bp, br, b_const], writes=[b_PB[t]])
                          return [fA, fB, fR, fC]

                      def mk_k(tl, first, last):
                          t, c0, n = tl["t"], tl["c0"], tl["n"]
                          tc_ = slice(c0, c0 + n)
                          st_ = {}

                          def fA():
                              wkv, bwkv = wts["kv"]
                              ps, bp = psum_lo.get()
                              inproj(ps[:, 0:n], bp, wkv, bwkv, 0, 128, tc_, t)
                              sq, bsq = bts.get()
                              S.op("act", lambda e: e.activation(sq[:, 0:n], ps[:, 0:n], AF.Square), reads=[bp], writes=[bsq])
                              st_.update(ps=ps, bp=bp, sq=sq, bsq=bsq)

                          def fB():
                              st_["rr"] = rstd_part1a([st_["sq"][:, 0:n]], [st_["bsq"]], n, 2, BD64[:, 0:128:64], 1.0 / 64)

                          def fR():
                              st_["r2l"] = rstd_part1b(st_["rr"][0], st_["rr"][1], n, 2)

                          def fC():
                              r, br, pfree = rstd_part2(st_["r2l"], n, True)
                              ps, bp = st_["ps"], st_["bp"]
                              kslots = [b_KT[1 + (c0 // 128) + i] for i in range(n // 128)]
                              S.op("dve", lambda e: e.scalar_tensor_tensor(
                                  KTe[:, 128 + c0:128 + c0 + n], ps[:, 0:n], KW[:, l:l + 1], r[:, 0:n], op0=ALU.mult, op1=ALU.mult),
                                  reads=[bp, br, b_const], writes=kslots)
                              for bi, gb in enumerate(tl["blocks"]):
                                  if (gb == "S") or (gb == last_gb):
                                      kf_, bkf = tmps.get()
                                      KF = kf_[:, 0:128]
                                      S.op("dve", lambda e, bi=bi: e.scalar_tensor_tensor(
                                          KF, ps[:, bi * 128:(bi + 1) * 128], KW[:, l:l + 1], r[:, bi * 128:(bi + 1) * 128], op0=ALU.mult, op1=ALU.mult),
                                          reads=[bp, br, b_const], writes=[bkf])
                                      bko = buf("KOs")
                                      transpose_out(KF, bkf, 128, 128, KOs, bko, slice(0, 128), ps_lease=pfree)
                                      if gb == "S":
                                          for nn in range(NSEQ):
                                              S.dma("sp", "kout", sk_o[l, nn, 128 - TS:128, :], KOs[nn * TS:(nn + 1) * TS, :], reads=[bko])
                                      else:
                                          S.dma("sp", "kout", pk_o[l], KOs[:], reads=[bko])
                          return [fA, fB, fR, fC]

                      def mk_v(tl, last):
                          t, c0, n = tl["t"], tl["c0"], tl["n"]

                          def fA():
                              wkv, bwkv = wts["kv"]
                              psv, bpv = psum_lo.get()
                              for bi, gb in enumerate(tl["blocks"]):
                                  bc = c0 + bi * 128
                                  mm_group(psv[:, bi * 128:(bi + 1) * 128], bpv, [(xnT[:, k, bc:bc + 128], wkv[:, k, 128:256]) for k in range(NCH)],
                                           reads=[bwkv, b_xnT[t]])
                              vs0 = 1 + c0 // 128
                              nb = n // 128
                              S.op("act", lambda e: e.copy(Ve[:, vs0:vs0 + nb, :], psv[:, 0:n].rearrange("p (b f) -> p b f", b=nb)),
                                   reads=[bpv], writes=[b_V[vs0 + i] for i in range(nb)])
                              for bi, gb in enumerate(tl["blocks"]):
                                  if (gb == "S") or (gb == last_gb):
                                      bvo = buf("KOs")
                                      S.op("act", lambda e, bi=bi: e.copy(VOs[:], psv[:, bi * 128:(bi + 1) * 128]), reads=[bpv], writes=[bvo])
                                      if gb == "S":
                                          for nn in range(NSEQ):
                                              S.dma("sp", "vout", sv_o[l, nn, 128 - TS:128, :], VOs[nn * TS:(nn + 1) * TS, :], reads=[bvo])
                                      else:
                                          S.dma("sp", "vout", pv_o[l], VOs[:], reads=[bvo])

                          return [fA, (lambda: None), (lambda: None), (lambda: None)]

                      wts["q"] = piece("q")
                      wts["kv"] = piece("kv")
                      items = [mk_norm(tl) for tl in tiles_kv]
                      full_t = {tl["t"]: tl for tl in tiles}
                      rest = []
                      for i_norm, tk in enumerate(tiles_kv):
                          if tk["t"] in full_t:
                              rest += [(i_norm, mk_q(full_t[tk["t"]], g, False, False)) for g in range(4)]
                          rest.append((i_norm, mk_k(tk, False, False)))
                          rest.append((i_norm, mk_v(tk, False)))
                      pad = 1
                      for p_, (i_norm, _) in enumerate(rest):
                          pad = max(pad, i_norm + 3 - (len(tiles_kv) + p_))
                      for _ in range(pad):
                          items.append([(lambda: None)] * 4)
                      items += [it for _, it in rest]
                      run_pipeline(items, [0, 1, 2, 3])
                      piece_done()
                      if has_sample:
                          b_CKT = buf("CKT")
                          for q4 in range(NSEQ // 4):
                              ps, bp = psum.get()
                              psb = ps.bitcast(BF16)
                              for i in range(4):
                                  nn = q4 * 4 + i
                                  S.op("pe", lambda e, nn=nn, i=i, psb=psb: e.transpose(psb[:, i * 128:(i + 1) * 128], CK[:, nn, :], IDENTB[:]),
                                       reads=[b_IO[0], b_const], writes=[bp])
                              S.op("act", lambda e, q4=q4, psb=psb: e.copy(CKT[:, q4 * 4:(q4 + 1) * 4, :], psb[:, 0:512].rearrange("p (n f) -> p n f", n=4)),
                                   reads=[bp], writes=[b_CKT])


                      stage('Akv')
                      def mk_att(tl, bi, gb, kv):
                          t, c0, n = tl["t"], tl["c0"], tl["n"]
                          bc = c0 + bi * 128
                          bslot = bc // 128
                          pr = slice(64 * kv, 64 * kv + 64)
                          st_ = {}

                          def score(lhsT, rhs_ap, reads, bias_idx, bias_buf, out_view=None):
                              ps, bp = psum.get()
                              S.op("pe", lambda e: e.matmul(ps[:].rearrange("p (g t) -> p g t", g=4), lhsT, rhs_ap, start=True, stop=True),
                                   reads=reads, writes=[bp])
                              return ps, bp

                          def softexp(ps, bp, bias_idx, bias_buf):
                              sbm, bsb = tmps.get()
                              S.op("act", lambda e: e.activation(sbm[:], ps[:], AF.Exp, scale=0.125), reads=[bp], writes=[bsb])
                              pt, bpt = bts.get()
                              S.op("pool", lambda e: e.tensor_tensor(pt[:], sbm[:], BIAS[:, bias_idx, :], op=ALU.mult),
                                   reads=[bsb, bias_buf], writes=[bpt])
                              return pt, bpt

                          def fA():
                              if gb == "S":
                                  ps, bp = psum.get()
                                  for nn in range(NSEQ):
                                      S.op("pe", lambda e, nn=nn: e.matmul(
                                          ps[:, nn * 32:(nn + 1) * 32].rearrange("p (g t) -> p g t", g=4),
                                          CKT[pr, nn, :], QT[pr, :, bc + nn * TS:bc + (nn + 1) * TS], start=True, stop=True),
                                          reads=[b_CKT, b_PB[t]], writes=[bp])
                                  st_["ptc"] = softexp(ps, bp, 4 + kv, b_biasx)
                                  ps, bp = score(KTe[pr, 128 + bc:128 + bc + 128], QT[pr, :, bc:bc + 128], [b_KT[bslot + 1], b_PB[t]], None, None)
                                  st_["ptn"] = softexp(ps, bp, 6 + kv, b_biasx)
                              else:
                                  first = (si == 0 and gb == first_out_gb)
                                  kbs = [(bslot, (4 + kv) if first else kv, b_biasx if first else b_const), (bslot + 1, 2 + kv, b_const)]
                                  pts = []
                                  for (slot, bidx, bb) in kbs:
                                      ps, bp = score(KTe[pr, slot * 128:(slot + 1) * 128], QT[pr, :, bc:bc + 128], [b_KT[slot], b_PB[t]], None, None)
                                      pt, bpt = softexp(ps, bp, bidx, bb)
                                      pts.append((pt, bpt, slot))
                                  st_["pts"] = pts

                          def fB():
                              pso, bpo = psum.get()
                              if gb == "S":
                                  ptc, bptc = st_["ptc"]
                                  ptn, bptn = st_["ptn"]
                                  ptn3 = ptn[:].rearrange("p (g t) -> p g t", g=4)
                                  ptc4 = ptc[:].rearrange("p (n g t) -> p n g t", n=NSEQ, g=4)
                                  for par in range(2):
                                      orow = slice(64 * par, 64 * par + 64)
                                      tp = dict(tile_position=(0, 64)) if par == 1 else {}
                                      for which in range(2):
                                          ocol = 256 * which
                                          oap = pso[orow, ocol:ocol + 256].rearrange("p (c q) -> p c q", c=2)
                                          lhs_new = Ve[:, bslot + 1, pr] if which == 0 else ONESB[:, 0:64]
                                          S.op("pe", lambda e: e.matmul(
                                              oap, lhs_new, ptn3[:, par::2, :], start=True, stop=False, skip_group_check=True, **tp),
                                              reads=[bptn, b_V[bslot + 1], b_const], writes=[bpo])
                                          for nn in range(NSEQ):
                                              lhs_c = CV[:, nn, pr] if which == 0 else ONESB[:, 0:64]
                                              S.op("pe", lambda e, nn=nn, lhs_c=lhs_c: e.matmul(
                                                  oap[:, :, nn * TS:(nn + 1) * TS], lhs_c, ptc4[:, nn, par::2, :],
                                                  start=False, stop=(nn == NSEQ - 1), skip_group_check=True, **tp),
                                                  reads=[bptc, b_IO[1], b_const], writes=[bpo])
                              else:
                                  pts = st_["pts"]
                                  for par in range(2):
                                      orow = slice(64 * par, 64 * par + 64)
                                      tp = dict(tile_position=(0, 64)) if par == 1 else {}
                                      for which in range(2):
                                          ocol = 256 * which
                                          oap = pso[orow, ocol:ocol + 256].rearrange("p (c q) -> p c q", c=2)
                                          for i, (pt, bpt, slot) in enumerate(pts):
                                              lhs = Ve[:, slot, pr] if which == 0 else ONESB[:, 0:64]
                                              pt3 = pt[:].rearrange("p (g t) -> p g t", g=4)
                                              S.op("pe", lambda e, lhs=lhs, pt3=pt3, i=i: e.matmul(
                                                  oap, lhs, pt3[:, par::2, :], start=(i == 0), stop=(i == 1), skip_group_check=True, **tp),
                                                  reads=[bpt, b_V[slot], b_const], writes=[bpo])
                              ds, bds = smalls.get()
                              for c2 in range(2):
                                  S.op("dve", lambda e, c2=c2: e.tensor_scalar(
                                      ds[:, c2 * 128:(c2 + 1) * 128], pso[:, 256 + c2 * 128:256 + (c2 + 1) * 128],
                                      SE4[:, l * 4 + kv * 2 + c2:l * 4 + kv * 2 + c2 + 1], None, op0=ALU.add),
                                      reads=[bpo, b_const], writes=[bds])
                              S.op("dve", lambda e: e.reciprocal(ds[:], ds[:]), reads=[bds], writes=[bds])
                              S.op("dve", lambda e: e.tensor_tensor(
                                  YX[:, kv * 2:(kv + 1) * 2, bc:bc + 128], pso[:, 0:256].rearrange("p (c q) -> p c q", c=2),
                                  ds[:].rearrange("p (c q) -> p c q", c=2), op=ALU.mult),
                                  reads=[bpo, bds], writes=[b_MB] + b_MBt)
                          return [fA, (lambda: None), fB]

                      items = [mk_att(tl, bi, gb, kv) for tl in tiles for bi, gb in enumerate(tl["blocks"]) for kv in range(2)]
                      run_pipeline(items, [0, 1, 2])

                      stage('Aattn')
                      wza, bwza = piece("za")
                      for tl in tiles:
                          t, c0, n = tl["t"], tl["c0"], tl["n"]
                          tc_ = slice(c0, c0 + n)
                          for c2 in range(4):
                              ps, bp = psum.get()
                              inproj(ps[:, 0:n], bp, wza, bwza, c2 * 128, 128, tc_, t)
                              t2, bt2 = silu2(ps[:, 0:n], bp, n)
                              S.op("pool", lambda e, t2=t2, c2=c2: e.tensor_tensor(PA[:, c2, tc_], t2[:, 0:n], YX[:, c2, tc_], op=ALU.mult),
                                   reads=[bt2, b_MB], writes=[b_PA[t]])
                      piece_done()
                      lp = n_prompt_blk
                      S.op("pool", lambda e: e.tensor_copy(KTb[:, l, :], KTe[:, lp * 128:(lp + 1) * 128]), reads=[b_KT[lp]], writes=[b_KTb[l]])
                      S.op("pool", lambda e: e.tensor_copy(Vb[:, l, :], Ve[:, lp, :]), reads=[b_V[lp]], writes=[b_Vb[l]])

                      stage('A')
                      if has_sample:
                          bsc = buf("SCin")
                          S.dma("sp", "scin", SCs[:], sconv_d[l], writes=[bsc])
                      for c in range(4):
                          wb, bwb = piece("b%d" % c)
                          S.op("dve", lambda e, c=c: e.tensor_copy(CE[:, 0:2], CEb[:, l, c, :]), reads=[b_CEb[l]], writes=[b_CE])
                          if has_sample:
                              ps, bp = psum.get()
                              S.op("pe", lambda e, ps=ps, c=c: e.transpose(ps[:, 0:32], SCs[:, c * 128:(c + 1) * 128], IDENT[64:96, 64:96]),
                                   reads=[bsc, b_const], writes=[bp])
                              S.op("act", lambda e, ps=ps: e.copy(CES[:, :, 0:2], ps[:, 0:32].rearrange("p (n r) -> p n r", r=2)),
                                   reads=[bp], writes=[b_CES])
                          for tl in tiles_kv:
                              t, c0, n = tl["t"], tl["c0"], tl["n"]
                              tc_ = slice(c0, c0 + n)
                              deferred = []
                              ps1, bp1 = psum.get()
                              inproj(ps1[:, 0:n], bp1, wb, bwb, 0, 128, tc_, t)
                              gc, bgc = tmps.get()
                              S.op("act", lambda e, ps1=ps1, gc=gc: e.copy(gc[:, 0:n], ps1[:, 0:n]), reads=[bp1], writes=[bgc])
                              ps2, bp2 = psum.get()
                              inproj(ps2[:, 0:n], bp2, wb, bwb, 128, 128, tc_, t)
                              co, bco = tmps.get()
                              if tl["kind"] == "P":
                                  S.op("dve", lambda e, ps2=ps2, gc=gc: e.tensor_tensor(CE[:, 2 + c0:2 + c0 + n], ps2[:, 0:n], gc[:, 0:n], op=ALU.mult),
                                       reads=[bp2, bgc], writes=[b_CE])
                                  for r_ in range(3):
                                      src = CE[:, c0 + r_:c0 + r_ + n]
                                      wcol = CW[:, l * 12 + r_ * 4 + c:l * 12 + r_ * 4 + c + 1]
                                      if r_ == 0:
                                          S.op("dve", lambda e, src=src, wcol=wcol, co=co: e.tensor_scalar(co[:, 0:n], src, wcol, None, op0=ALU.mult),
                                               reads=[b_CE, b_const], writes=[bco])
                                      else:
                                          S.op("dve", lambda e, src=src, wcol=wcol, co=co: e.scalar_tensor_tensor(
                                              co[:, 0:n], src, wcol, co[:, 0:n], op0=ALU.mult, op1=ALU.add),
                                              reads=[b_CE, b_const, bco], writes=[bco])
                                  if last_gb in tl["blocks"]:
                                      e0 = 2 + c0 + n - 2
                                      bpc = buf("PCs")
                                      deferred.append(lambda e0=e0, bpc=bpc, c=c: transpose_out(
                                          CE[:, e0:e0 + 2], b_CE, 128, 2, PCs, bpc, slice(c * 128, (c + 1) * 128)))
                              else:
                                  ces_in = CES[:, :, 2:10]
                                  S.op("dve", lambda e, ps2=ps2, gc=gc: e.tensor_tensor(
                                      ces_in, ps2[:, 0:n].rearrange("p (n t) -> p n t", t=TS), gc[:, 0:n].rearrange("p (n t) -> p n t", t=TS), op=ALU.mult),
                                      reads=[bp2, bgc], writes=[b_CES])
                                  co3 = co[:, 0:n].rearrange("p (n t) -> p n t", t=TS)
                                  for r_ in range(3):
                                      src = CES[:, :, r_:r_ + TS]
                                      wcol = CW[:, l * 12 + r_ * 4 + c:l * 12 + r_ * 4 + c + 1]
                                      if r_ == 0:
                                          S.op("dve", lambda e, src=src, wcol=wcol, co3=co3: e.tensor_scalar(co3, src, wcol, None, op0=ALU.mult),
                                               reads=[b_CES, b_const], writes=[bco])
                                      else:
                                          S.op("dve", lambda e, src=src, wcol=wcol, co3=co3: e.scalar_tensor_tensor(
                                              co3, src, wcol, co3, op0=ALU.mult, op1=ALU.add),
                                              reads=[b_CES, b_const, bco], writes=[bco])
                                  bc32 = buf("C32")
                                  S.op("dve", lambda e: e.tensor_copy(C32[:].rearrange("p (n r) -> p n r", r=2), CES[:, :, 8:10]),
                                       reads=[b_CES], writes=[bc32])
                                  bscs = buf("SCs2")
                                  deferred.append(lambda bc32=bc32, bscs=bscs, c=c: transpose_out(
                                      C32[:], bc32, 128, 32, SCo, bscs, slice(c * 128, (c + 1) * 128)))
                              ps3, bp3 = psum.get()
                              inproj(ps3[:, 0:n], bp3, wb, bwb, 256, 128, tc_, t)
                              S.op("dve", lambda e, ps3=ps3, co=co: e.tensor_tensor(co[:, 0:n], ps3[:, 0:n], co[:, 0:n], op=ALU.mult),
                                   reads=[bp3, bco], writes=[bco])
                              ps4, bp4 = psum.get()
                              inproj(ps4[:, 0:n], bp4, wb, bwb, 384, 128, tc_, t)
                              for fn_ in deferred:
                                  fn_()
                              t2, bt2 = silu2(ps4[:, 0:n], bp4, n)
                              S.op("pool", lambda e, t2=t2, co=co, c=c: e.tensor_tensor(PB[:, c, tc_], t2[:, 0:n], co[:, 0:n], op=ALU.mult),
                                   reads=[bt2, bco], writes=[b_PB[t]])
                          e0 = 2 + n_prompt_blk * 128 - 2
                          S.op("dve", lambda e, c=c, e0=e0: e.tensor_copy(CEb[:, l, c, :], CE[:, e0:e0 + 2]), reads=[b_CE], writes=[b_CEb[l]])
                          piece_done()
                      if last_gb in [gb for tl in tiles_kv for gb in tl["blocks"]]:
                          S.dma("sp", "pcout", pc_o[l], PCs[:], reads=[buf("PCs")])
                      if has_sample:
                          S.dma("sp", "scout", sc_o[l], SCo[:], reads=[buf("SCs2")])

                      stage('B')
                      wvc, bwvc = piece("vc")

                      def mk_vc(tl, bi, gb):
                          t, c0, n = tl["t"], tl["c0"], tl["n"]
                          bc = c0 + bi * 128
                          st_ = {}

                          def fA():
                              psv, bpv = psum.get()
                              mm_group(psv[:], bpv, [(xnT[:, k, bc:bc + 128], wvc[:, k, :]) for k in range(NCH)], reads=[bwvc, b_xnT[t]])
                              junk, bj = tmps.get()
                              SS, bss = sss.get()
                              S.op("act", lambda e: e.activation(junk[:], psv[:], AF.Square, accum_out=SS[:, 0:1]),
                                   reads=[bpv], writes=[bj, bss])
                              S.op("dve", lambda e: e.tensor_scalar(SS[:, 1:2], SS[:, 0:1], 1.0 / 512, EPS, op0=ALU.mult, op1=ALU.add),
                                   reads=[bss], writes=[bss])
                              S.op("pool", lambda e: e.tensor_tensor(SS[:, 2:3], SS[:, 1:2], NEGH[:, 0:1], op=ALU.pow), reads=[bss, b_const], writes=[bss])
                              st_.update(psv=psv, bpv=bpv, SS=SS, bss=bss)

                          def fB():
                              psv, bpv, SS, bss = st_["psv"], st_["bpv"], st_["SS"], st_["bss"]
                              vcn, bvcn = bts.get()
                              S.op("dve", lambda e: e.scalar_tensor_tensor(
                                  vcn[:], psv[:], SS[:, 2:3], VNW[:], op0=ALU.mult, op1=ALU.mult), reads=[bpv, bss, b_lay], writes=[bvcn])
                              if gb == "S":
                                  vf, bvf = tmps.get()
                                  S.op("dve", lambda e: e.scalar_tensor_tensor(
                                      vf[:], psv[:], SS[:, 2:3], VNW[:], op0=ALU.mult, op1=ALU.mult), reads=[bpv, bss, b_lay], writes=[bvf])
                                  S.dma("sp", "scvout", scv_o[l], vf[:], reads=[bvf])
                              st_.update(vcn=vcn, bvcn=bvcn)

                          def fC():
                              vcn, bvcn = st_["vcn"], st_["bvcn"]
                              pss, bps = psum.get()
                              wm = WMTS if gb == "S" else WMT
                              for g in range(4):
                                  S.op("pe", lambda e, g=g: e.matmul(
                                      pss[:, g * 128:(g + 1) * 128], vcn[:, g * 128:(g + 1) * 128], wm[:, l, g, :], start=True, stop=True),
                                      reads=[bvcn, b_const], writes=[bps])
                              bt_ = BSPTS if gb == "S" else BSPT
                              S.op("dve", lambda e: e.tensor_tensor(
                                  YX[:, :, bc:bc + 128], pss[:].rearrange("p (g t) -> p g t", g=4), bt_[:].rearrange("p (g t) -> p g t", g=4), op=ALU.add),
                                  reads=[bps, b_lay], writes=[b_MB] + b_MBt)
                          return [fA, fB, fC]

                      items = [mk_vc(tl, bi, gb) for tl in tiles for bi, gb in enumerate(tl["blocks"])]
                      run_pipeline(items, [1, 0, 2])
                      piece_done()
                      wu, bwu = piece("u")
                      for tl in tiles:
                          t, c0, n = tl["t"], tl["c0"], tl["n"]
                          tc_ = slice(c0, c0 + n)
                          for g in range(4):
                              ps, bp = psum.get()
                              inproj(ps[:, 0:n], bp, wu, bwu, g * 128, 128, tc_, t)
                              S.op("dve", lambda e, ps=ps, g=g: e.tensor_tensor(YX[:, g, tc_], ps[:, 0:n], YX[:, g, tc_], op=ALU.mult),
                                   reads=[bp, b_MB], writes=[b_MB] + b_MBt)
                      piece_done()
                      wzc, bwzc = piece("zc")
                      for tl in tiles:
                          t, c0, n = tl["t"], tl["c0"], tl["n"]
                          tc_ = slice(c0, c0 + n)
                          for g in range(4):
                              ps, bp = psum.get()
                              inproj(ps[:, 0:n], bp, wzc, bwzc, g * 128, 128, tc_, t)
                              t2, bt2 = silu2(ps[:, 0:n], bp, n)
                              S.op("pool", lambda e, t2=t2, g=g: e.tensor_tensor(PC[:, g, tc_], t2[:, 0:n], YX[:, g, tc_], op=ALU.mult),
                                   reads=[bt2, b_MB], writes=[b_PC[t]])
                      piece_done()

                      stage('C')
                      Ps = [(PA, b_PA), (PB, b_PB), (PC, b_PC)]
                      for j in range(8):
                          wg, bwg = piece("g%d" % j)
                          for tl in tiles:
                              t, c0, n = tl["t"], tl["c0"], tl["n"]
                              tc_ = slice(c0, c0 + n)
                              acc = None
                              for i in range(3):
                                  Pi, bPi = Ps[i]
                                  psa, bpa = psum.get()
                                  mm_group(psa[:, 0:n], bpa, [(wg[:, 8 + k, i * 128:(i + 1) * 128], Pi[:, k, tc_]) for k in range(4)],
                                           reads=[bwg, bPi[t]])
                                  psg, bpg = psum.get()
                                  inproj(psg[:, 0:n], bpg, wg, bwg, i * 128, 128, tc_, t)
                                  tg, btg = tmps.get()
                                  bcol = l * 24 + i * 8 + j
                                  S.op("act", lambda e, psg=psg, tg=tg, bcol=bcol: e.activation(
                                      tg[:, 0:n], psg[:, 0:n], AF.Tanh, bias=BGH[:, bcol:bcol + 1], scale=0.5), reads=[bpg, b_const], writes=[btg])
                                  S.op("dve", lambda e, psa=psa, tg=tg: e.scalar_tensor_tensor(
                                      tg[:, 0:n], tg[:, 0:n], 1.0, psa[:, 0:n], op0=ALU.add, op1=ALU.mult), reads=[btg, bpa], writes=[btg])
                                  if i == 0:
                                      acc, bacc = tg, btg
                                  elif i == 1:
                                      S.op("pool", lambda e, acc=acc, tg=tg: e.tensor_tensor(acc[:, 0:n], acc[:, 0:n], tg[:, 0:n], op=ALU.add),
                                           reads=[bacc, btg], writes=[bacc])
                                  else:
                                      S.op("pool", lambda e, acc=acc, tg=tg, j=j: e.tensor_tensor(MB[:, j, tc_], acc[:, 0:n], tg[:, 0:n], op=ALU.add),
                                           reads=[bacc, btg], writes=[b_MB, b_MBt[t]])
                          piece_done()

                      stage('G')
                      for h in range(2):
                          wo, bwo = piece("o%d" % h)
                          for tl in tiles:
                              t, c0, n = tl["t"], tl["c0"], tl["n"]
                              tc_ = slice(c0, c0 + n)
                              for e4 in range(4):
                                  ec = h * 4 + e4
                                  ps, bp = psum.get()
                                  mm_group(ps[:, 0:n], bp, [(wo[:, k, e4 * 128:(e4 + 1) * 128], MB[:, k, tc_]) for k in range(NCH)],
                                           reads=[bwo, b_MBt[t]])
                                  S.op("dve", lambda e, ps=ps, ec=ec: e.scalar_tensor_tensor(
                                      xT[:, ec, tc_], ps[:, 0:n], 0.25, xT[:, ec, tc_], op0=ALU.mult, op1=ALU.add),
                                      reads=[bp, b_xT[t][ec]], writes=[b_xT[t][ec]])
                                  if h == 1 and l + 1 == L and e4 == 0:
                                      for tp_ in tiles_full:
                                          if tp_["t"] < t and tp_["t"] not in stored:
                                              emit_store(tp_)
                                  if h == 1 and l + 1 < L and e4 == 0:
                                      for tn in tiles_at(l + 1)[0]:
                                          if (l + 1, tn["t"]) in norm_pre:
                                              norm_C(l + 1, tn)
                              if h == 1 and l + 1 < L:
                                  for tn in tiles_at(l + 1)[0]:
                                      if tn["t"] == t:
                                          norm_AB(l + 1, tn)
                          piece_done()

                  stage('O')
                  for tl in tiles_full:
                      if tl["t"] not in stored:
                          emit_store(tl)

            except StopBuild:
                pass
            if stop_after is None:
                assert wstate["cur"] == len(schedule) - 1
            S.wait_all_dma("sp")
            for e in ("pe", "act", "dve", "pool"):
                pass
            build.stats = dict(nops=dict(S.nops), nwaits=S.nwaits, nsig=dict(S.sig))

        dry = Sched(nc, ctx, signal=None)
        emit_all(dry)
        emit_all(Sched(nc, ctx, signal=dry.signal))
    return nc


def _piece_cols():
    cols = {}
    qperm = np.empty(512, np.int64)
    for g in range(4):
        for kv in range(2):
            qperm[g * 128 + kv * 64:g * 128 + kv * 64 + 64] = O_Q + (kv * 4 + g) * 64 + np.arange(64)
    cols["q"] = qperm
    cols["kv"] = np.concatenate([O_K + np.arange(128), O_V + np.arange(128)])
    cols["za"] = O_ZA + np.arange(512)
    for c in range(4):
        r = np.arange(128) + c * 128
        cols["b%d" % c] = np.concatenate([O_GC + r, O_HB + r, O_GB + r, O_ZB + r])
    cols["vc"] = O_VC + np.arange(512)
    cols["u"] = O_U + np.arange(512)
    cols["zc"] = O_ZC + np.arange(512)
    for j in range(8):
        r = np.arange(128) + j * 128
        cols["g%d" % j] = np.concatenate([O_G + r, O_G + 1024 + r, O_G + 2048 + r])
    return cols


def pack_weights(w_in, w_out_a, w_out_b, w_out_c, w_o):
    L = w_in.shape[0]
    cols = _piece_cols()
    wp = np.empty((L, 128, PCOLS), np.float32)
    for l in range(L):
        for (pn, kk, g) in PIECES:
            off = PIECE_OFF[pn][0]
            if pn.startswith("g"):
                j = int(pn[1:])
                a = w_in[l][:, cols[pn]].reshape(8, 128, g).transpose(1, 0, 2)
                wo = np.concatenate([w_out_a[l][:, j * 128:(j + 1) * 128], w_out_b[l][:, j * 128:(j + 1) * 128],
                                     w_out_c[l][:, j * 128:(j + 1) * 128]], axis=1)
                b = wo.reshape(4, 128, g).transpose(1, 0, 2)
                blk = np.concatenate([a, b], axis=1)
            elif pn.startswith("o"):
                h = int(pn[1:])
                blk = w_o[l][:, h * 512:(h + 1) * 512].reshape(8, 128, g).transpose(1, 0, 2)
            else:
                blk = w_in[l][:, cols[pn]].reshape(8, 128, g).transpose(1, 0, 2)
            wp[l, :, off:off + kk * g] = blk.reshape(128, kk * g)
    return wp


def attn_bias_tiles(first_masked):
    h = np.arange(1, 9, dtype=np.float64)
    slopes = np.exp2(-8.0 * h / 8).reshape(2, 4)
    out = np.empty((10, 128, 512), np.float32)
    j = np.arange(128)[:, None]
    i = np.arange(128)[None, :]
    for kv in range(2):
        for g in range(4):
            s = slopes[kv, g]
            dp = i + 128 - j
            out[0 + kv][:, g * 128:(g + 1) * 128] = np.where(dp < 128, -s * dp, NEG)
            dc = i - j
            out[2 + kv][:, g * 128:(g + 1) * 128] = np.where(dc >= 0, -s * dc, NEG)
        out[4 + kv] = NEG if first_masked else out[0 + kv]
        t = np.arange(TS)[None, :]
        for g in range(4):
            s = slopes[kv, g]
            d = t + 128 - j
            tile_ = np.where(j > t, -s * d, NEG)
            for n in range(NSEQ):
                out[6 + kv][:, n * 32 + g * 8:n * 32 + g * 8 + 8] = tile_
        rn = (np.arange(128) // TS)[:, None]
        rs = (np.arange(128) % TS)[:, None]
        cn = (np.arange(128) // TS)[None, :]
        ct = (np.arange(128) % TS)[None, :]
        for g in range(4):
            s = slopes[kv, g]
            ok = (rn == cn) & (rs <= ct)
            out[8 + kv][:, g * 128:(g + 1) * 128] = np.where(ok, -s * (ct - rs), NEG)
    return np.exp(out.astype(np.float64)).astype(np.float32)


def host_common(inputs, L):
    f = np.float32
    norm_w, b_gate = inputs["norm_w"], inputs["b_gate"]
    c = {}
    c["wp"] = pack_weights(inputs["w_in"], inputs["w_out_a"], inputs["w_out_b"], inputs["w_out_c"], inputs["w_o"])
    c["nw"] = np.ascontiguousarray(norm_w.reshape(L, 8, 128).transpose(2, 0, 1).reshape(128, L * 8)).astype(f)
    c["bg"] = np.ascontiguousarray(b_gate.reshape(L, 24, 128).transpose(2, 0, 1).reshape(128, L * 24)).astype(f)
    c["qw"] = np.ascontiguousarray(np.tile(inputs["q_norm_w"], (1, 2)).T).astype(f)
    c["kw"] = np.ascontiguousarray(np.tile(inputs["k_norm_w"], (1, 2)).T).astype(f)
    sk = np.empty((128, L * 4), f)
    for l in range(L):
        for kv in range(2):
            for c2 in range(2):
                for par in range(2):
                    sk[par * 64:(par + 1) * 64, l * 4 + kv * 2 + c2] = inputs["sinks"][l, kv * 4 + 2 * c2 + par]
    c["sk4"] = sk
    c["cw"] = np.ascontiguousarray(inputs["conv_w"].reshape(L, 3, 4, 128).transpose(3, 0, 1, 2).reshape(128, L * 12)).astype(f)
    c["vnw"] = np.ascontiguousarray(np.broadcast_to(inputs["v_norm_w"][:, None, :], (L, 128, 512))).astype(f)
    c["wsp"] = np.ascontiguousarray(inputs["w_spatial"]).astype(f)
    bs = inputs["b_spatial"]
    c["bspt"] = np.ascontiguousarray(np.broadcast_to(bs.reshape(L, 1, 512), (L, 128, 512))).astype(f)
    bss = np.tile(bs[:, :, :TS], (1, 1, NSEQ))
    c["bspts"] = np.ascontiguousarray(np.broadcast_to(bss.reshape(L, 1, 512), (L, 128, 512))).astype(f)
    c["ident"] = np.eye(128, dtype=f)
    s_ = np.arange(128)
    c["tri"] = (s_[:, None] <= s_[None, :]).astype(f)
    c["bdm"] = ((s_[:, None] // TS) == (s_[None, :] // TS)).astype(f)
    c["bd64"] = ((s_[:, None] // 64) == (s_[None, :] // 64)).astype(f)
    c["repi"] = (np.arange(8)[:, None] == (s_[None, :] % TS)).astype(f)
    return c


def kernel(**inputs):
    L = 4
    n_cores = 8
    x_prompt = np.asarray(inputs["x_prompt"], np.float32)
    x_sample = np.asarray(inputs["x_sample"], np.float32)
    inputs = {k: np.asarray(v) for k, v in inputs.items()}
    B_, S_, _ = x_prompt.shape
    npb, halo = 20, 4
    cores_per_b = n_cores // B_
    tok_per_core = S_ // cores_per_b
    common = host_common(inputs, L)
    bias_mid = attn_bias_tiles(False)
    bias_first = attn_bias_tiles(True)
    in_maps = []
    for c in range(n_cores):
        b, q = divmod(c, cores_per_b)
        t0 = q * tok_per_core
        xp = np.zeros((npb * 128, D), np.float32)
        if q == 0:
            xp[halo * 128:] = x_prompt[b, t0:t0 + tok_per_core]
        else:
            xp[:] = x_prompt[b, t0 - halo * 128:t0 + tok_per_core]
        n0 = c * NSEQ
        m = dict(common)
        m["xp"] = xp
        m["xs"] = np.ascontiguousarray(x_sample[n0:n0 + NSEQ].reshape(NSEQ * TS, D))
        m["ck"] = np.ascontiguousarray(inputs["cache_k"][:, n0:n0 + NSEQ].reshape(L, NSEQ, 128, 128))
        m["cv"] = np.ascontiguousarray(inputs["cache_v"][:, n0:n0 + NSEQ].reshape(L, NSEQ, 128, 128))
        m["sconv"] = np.ascontiguousarray(inputs["state_conv"][:, n0:n0 + NSEQ].reshape(L, NSEQ * 2, 512))
        m["biasall"] = bias_first if q == 0 else bias_mid
        in_maps.append(m)
    nc = build(L, default_sts(), npb, halo)
    res = run_bass_kernel_spmd(nc, in_maps, core_ids=list(range(n_cores)))
    R = res.results
    yp = np.concatenate([R[c]["yp"] for c in range(n_cores)], axis=0).reshape(B_, S_, D)
    ys = np.concatenate([R[c]["ys"] for c in range(n_cores)], axis=0).reshape(n_cores * NSEQ, TS, D)
    last = [b * cores_per_b + cores_per_b - 1 for b in range(B_)]
    pk = np.stack([R[c]["pk"] for c in last], axis=1).reshape(L, B_, 128, 2, 64)
    pv = np.stack([R[c]["pv"] for c in last], axis=1).reshape(L, B_, 128, 2, 64)
    pc = np.stack([R[c]["pc"] for c in last], axis=1).reshape(L, B_, 2, 512)
    sk = np.concatenate([R[c]["sko"] for c in range(n_cores)], axis=1).reshape(L, n_cores * NSEQ, 128, 2, 64)
    sv = np.concatenate([R[c]["svo"] for c in range(n_cores)], axis=1).reshape(L, n_cores * NSEQ, 128, 2, 64)
    sc = np.concatenate([R[c]["sco"].reshape(L, NSEQ, 2, 512) for c in range(n_cores)], axis=1)
    scv = np.concatenate([R[c]["scv"].reshape(L, NSEQ, TS, 512) for c in range(n_cores)], axis=1)
    f = np.float32
    return (yp.astype(f), ys.astype(f), pk.astype(f), pv.astype(f), pc.astype(f),
            sk.astype(f), sv.astype(f), sc.astype(f), scv.astype(f))
```

```python
import numpy as np
from contextlib import ExitStack

import concourse.bass as bass
import concourse.mybir as mybir
from concourse.bass_utils import run_bass_kernel_spmd

F32 = mybir.dt.float32
BF16 = mybir.dt.bfloat16
AF = mybir.ActivationFunctionType
ALU = mybir.AluOpType

D = 1024
NCH = 8
NSEQ = 16
TS = 8
EPS = 1e-6
NEG = -30000.0

O_Q, O_K, O_V, O_ZA = 0, 512, 640, 768
O_GB, O_GC, O_HB, O_ZB = 1280, 1792, 2304, 2816
O_U, O_VC, O_ZC, O_G = 3328, 3840, 4352, 4864

PIECES = ([("q", 8, 512), ("kv", 8, 256), ("za", 8, 512)]
          + [("b%d" % c, 8, 512) for c in range(4)]
          + [("vc", 8, 512), ("u", 8, 512), ("zc", 8, 512)]
          + [("g%d" % j, 12, 384) for j in range(8)]
          + [("o0", 8, 512), ("o1", 8, 512)])
PIECE_OFF = {}
_o = 0
for _n, _k, _g in PIECES:
    PIECE_OFF[_n] = (_o, _k, _g)
    _o += _k * _g
PCOLS = _o
SLOT_ELEMS = 12 * 384


class Buf:
    __slots__ = ("name", "writer", "readers", "gen", "excl")

    def __init__(self, name, excl=False):
        self.name = name
        self.writer = None
        self.readers = {}
        self.gen = 0
        self.excl = excl


class _Dummy:
    pass


class Sched:
    ENG = ("pe", "act", "dve", "pool", "sp")

    def __init__(self, nc, ctx, signal=None):
        self.nc = nc
        self.dry = signal is None
        self.signal = {e: set() for e in self.ENG} if self.dry else signal
        self.eng = {"pe": nc.tensor, "act": nc.scalar, "dve": nc.vector, "pool": nc.gpsimd, "sp": nc.sync}
        self.sem = {}
        self.raw = {e: 0 for e in self.ENG}
        self.sig = {e: 0 for e in self.ENG}
        for e in self.ENG:
            self.sem[e] = _Dummy() if self.dry else ctx.enter_context(nc.semaphore("s_" + e))
        self.known = {e: {} for e in self.ENG}
        self.ctx = ctx
        self.dma_sems = {}
        self.nwaits = 0
        self.nops = {e: 0 for e in self.ENG}

    def _waits(self, e, reads, writes):
        evs = {}

        def add(ev):
            if ev is None:
                return
            s, v, src, raw = ev
            if src == "pe" and e == "pe":
                return
            k = id(s)
            if k not in evs or evs[k][1] < v:
                evs[k] = ev

        for b in reads:
            add(b.writer)
            if b.excl:
                for ev in b.readers.values():
                    if ev[2] != e:
                        add(ev)
        for b in writes:
            add(b.writer)
            for ev in b.readers.values():
                add(ev)
        kn = self.known[e]
        for k, (s, v, src, raw) in evs.items():
            if kn.get(k, 0) >= v:
                continue
            if self.dry:
                if src != "dma":
                    self.signal[src].add(raw)
            else:
                assert src == "dma" or raw in self.signal[src], "wait on a non-signalling instruction"
                self.eng[e].wait_ge(s, v)
            kn[k] = v
            self.nwaits += 1

    @staticmethod
    def _commit(ev, reads, writes):
        k = id(ev[0])
        for b in writes:
            b.writer = ev
            b.readers = {}
        for b in reads:
            old = b.readers.get(k)
            if old is None or old[1] < ev[1]:
                b.readers[k] = ev

    def op(self, e, fn, reads=(), writes=()):
        reads = _unlease(reads)
        writes = _unlease(writes)
        self._waits(e, reads, writes)
        self.raw[e] += 1
        self.nops[e] += 1
        raw = self.raw[e]
        if self.dry:
            ev = (self.sem[e], raw, e, raw)
        else:
            ins = fn(self.eng[e])
            if raw in self.signal[e]:
                self.sig[e] += 1
                ins.then_inc(self.sem[e], 1)
            ev = (self.sem[e], self.sig[e], e, raw)
        self._commit(ev, reads, writes)
        return ev

    def dma(self, q, semname, out, in_, reads=(), writes=(), **kw):
        if semname not in self.dma_sems:
            self.dma_sems[semname] = [_Dummy() if self.dry else self.ctx.enter_context(self.nc.semaphore("d_" + semname)), 0]
        ent = self.dma_sems[semname]
        reads = _unlease(reads)
        writes = _unlease(writes)
        self._waits(q, reads, writes)
        ent[1] += 16
        if not self.dry:
            ins = self.eng[q].dma_start(out=out, in_=in_, **kw)
            ins.then_inc(ent[0], 16)
        ev = (ent[0], ent[1], "dma", None)
        self._commit(ev, reads, writes)
        return ev

    def wait_all_dma(self, e):
        if self.dry:
            return
        for name, (s, v) in self.dma_sems.items():
            if v > 0 and self.known[e].get(id(s), 0) < v:
                self.eng[e].wait_ge(s, v)
                self.known[e][id(s)] = v


class Lease:
    __slots__ = ("buf", "gen")

    def __init__(self, buf, gen):
        self.buf = buf
        self.gen = gen


def _unlease(bs):
    out = []
    for b in bs:
        if isinstance(b, Lease):
            assert b.gen == b.buf.gen, "stale lease on %s" % b.buf.name
            b = b.buf
        out.append(b)
    return out


class Pool:
    def __init__(self, aps, name, excl=False, bufs=None):
        self.aps = aps
        self.bufs = bufs if bufs is not None else [Buf("%s%d" % (name, i), excl) for i in range(len(aps))]
        self.i = 0

    def get(self):
        i = self.i
        self.i = (i + 1) % len(self.aps)
        self.bufs[i].gen += 1
        return self.aps[i], Lease(self.bufs[i], self.bufs[i].gen)


def default_sts():
    return [
        [dict(kind="P", blocks=[0, 1, 2, 3]), dict(kind="P", blocks=[4, 5, 6])],
        [dict(kind="P", blocks=[7, 8, 9, 10]), dict(kind="P", blocks=[11, 12, 13])],
        [dict(kind="P", blocks=[14, 15, 16, 17]), dict(kind="P", blocks=[18, 19]), dict(kind="S", blocks=["S"])],
    ]


class StopBuild(Exception):
    pass


def build(depth, sts, npb, first_out_gb, nslots=5, dbg=None, stop_after=None, variant=0):
    L = depth
    last_gb = npb - 1
    nc = bass.Bass("TRN2", target_bir_lowering=False)

    def din(name, shape):
        return nc.dram_tensor(name, list(shape), F32, kind="ExternalInput").ap()

    def dout(name, shape):
        return nc.dram_tensor(name, list(shape), F32, kind="ExternalOutput").ap()

    n_out_blocks = npb - first_out_gb
    xp = din("xp", [npb * 128, D])
    xs = din("xs", [128, D])
    ck_d = din("ck", [L, NSEQ, 128, 128])
    cv_d = din("cv", [L, NSEQ, 128, 128])
    sconv_d = din("sconv", [L, NSEQ * 2, 512])
    wp_d = din("wp", [L, 128, PCOLS])
    nw_d = din("nw", [128, L * 8])
    bg_d = din("bg", [128, L * 24])
    qw_d = din("qw", [128, L])
    kw_d = din("kw", [128, L])
    sk_d = din("sk4", [128, L * 4])
    cw_d = din("cw", [128, L * 12])
    vnw_d = din("vnw", [L, 128, 512])
    wsp_d = din("wsp", [L, 4, 128, 128])
    bspt_d = din("bspt", [L, 128, 512])
    bspts_d = din("bspts", [L, 128, 512])
    ident_d = din("ident", [128, 128])
    tri_d = din("tri", [128, 128])
    bdm_d = din("bdm", [128, 128])
    bd64_d = din("bd64", [128, 128])
    repi_d = din("repi", [8, 128])
    bias_d = din("biasall", [10, 128, 512])

    yp_o = dout("yp", [n_out_blocks * 128, D])
    ys_o = dout("ys", [128, D])
    pk_o = dout("pk", [L, 128, 128])
    pv_o = dout("pv", [L, 128, 128])
    pc_o = dout("pc", [L, 2, 512])
    sk_o = dout("sko", [L, NSEQ, 128, 128])
    sv_o = dout("svo", [L, NSEQ, 128, 128])
    sc_o = dout("sco", [L, NSEQ * 2, 512])
    scv_o = dout("scv", [L, 128, 512])
    dbg_o = None
    if dbg:
        dbg_o = dout("dbg", [128, dbg])

    with ExitStack() as ctx:
        def sb(name, shape, dt):
            return ctx.enter_context(nc.sbuf_tensor(name, list(shape), dt))

        STW = max(sum(len(t["blocks"]) for t in st) for st in sts) * 128

        xT = sb("xT", [128, NCH, STW], F32)
        xnT = sb("xnT", [128, NCH, STW], BF16)
        PA = sb("PA", [128, 4, STW], BF16)
        PB = sb("PB", [128, 4, STW], BF16)
        PC = sb("PC", [128, 4, STW], BF16)
        QT = PB
        MBR = sb("MBR", [128, NCH * STW], BF16)
        MB = MBR[:].rearrange("p (c n) -> p c n", c=NCH)
        YX = MBR[:].bitcast(F32).rearrange("p (c n) -> p c n", c=4)
        CE = sb("CE", [128, 2 + STW], F32)
        CES = sb("CES", [128, NSEQ, 10], F32)
        KTe = sb("KTe", [128, STW + 128], BF16)
        Ve = sb("Ve", [128, STW // 128 + 1, 128], BF16)
        KTb = sb("KTb", [128, L, 128], BF16)
        Vb = sb("Vb", [128, L, 128], BF16)
        CEb = sb("CEb", [128, L, 4, 2], F32)
        TMPS = [sb("tmp%d" % i, [128, 512], F32) for i in range(7)]
        BTS = [sb("bt%d" % i, [128, 512], BF16) for i in range(6)]
        NEGH = sb("NEGH", [128, 64], F32)
        R2S = [sb("r2s%d" % i, [128, 128], F32) for i in range(8)]
        SMALL = [sb("sm%d" % i, [128, 256], F32) for i in range(2)]
        IO = [sb("io%d" % i, [128, D], F32) for i in range(2)]
        KOs = sb("KOs", [128, 128], F32)
        VOs = KOs
        CKT = sb("CKT", [128, NSEQ, 128], BF16)
        STG = sb("STG", [96, 512], F32)
        SCo = STG[0:32, :]
        PCs = STG[32:34, :]
        SCs = STG[64:96, :]
        C32 = sb("C32", [128, 32], F32)
        SSS = [sb("SS%d" % i, [128, 4], F32) for i in range(4)]
        NRT = [sb("NRT%d" % i, [128, 8], F32) for i in range(3)]
        RPS = [sb("RP%d" % i, [128, 8], F32) for i in range(6)]
        IDENT = sb("IDENT", [128, 128], F32)
        IDENTB = sb("IDENTB", [128, 128], BF16)
        ONESB = sb("ONESB", [128, 128], BF16)
        BD64 = sb("BD64", [128, 128], BF16)
        TRI = sb("TRI", [128, 128], BF16)
        BDM = sb("BDM", [128, 128], BF16)
        REPI = sb("REPI", [8, 128], BF16)
        BIAS = sb("BIAS", [128, 8, 512], F32)
        WMT = sb("WMT", [128, L, 4, 128], BF16)
        WMTS = sb("WMTS", [128, L, 4, 128], BF16)
        WSPL = sb("WSPL", [128, 4, 128], BF16)
        VNW = sb("VNW", [128, 512], F32)
        BSPT = sb("BSPT", [128, 512], F32)
        BSPTS = sb("BSPTS", [128, 512], F32)
        NW = sb("NW", [128, L * 8], F32)
        BGH = sb("BGH", [128, L * 24], F32)
        QW = sb("QW", [128, L], F32)
        KW = sb("KW", [128, L], F32)
        SE4 = sb("SE4", [128, L * 4], F32)
        CW = sb("CW", [128, L * 12], F32)
        EPSC = sb("EPSC", [128, 1], F32)
        print("sbuf before ring:", nc.sbuf_bytes_remaining, "need", nslots * SLOT_ELEMS * 2)
        RING = [sb("ring%d" % i, [128, SLOT_ELEMS], BF16) for i in range(nslots)]
        PSB = [ctx.enter_context(nc.psum_tensor("ps%d" % i, [128, 512], F32)) for i in range(8)]

        block = ctx.enter_context(nc.Block())
        def emit_all(S):

            psum = Pool([p[:] for p in PSB], "ps", excl=True)
            psum_lo = Pool([p[:] for p in PSB[0:5]], "ps", bufs=psum.bufs[0:5])
            psum_hi = Pool([p[:] for p in PSB[5:8]], "ps", bufs=psum.bufs[5:8])
            tmps = Pool([t[:] for t in TMPS], "tmp")
            bts = Pool([t[:] for t in BTS], "bt")
            smalls = Pool([t[:] for t in SMALL], "sm")
            r2s = Pool([t[:] for t in R2S], "r2s")
            sss = Pool([t[:] for t in SSS], "ss")
            rps = Pool([t[:] for t in RPS], "rp")

            B = {}

            def buf(name):
                if name not in B:
                    B[name] = Buf(name)
                return B[name]

            ntile_max = max(len(st) for st in sts)
            b_xT = [[buf("xT%d_%d" % (t, c)) for c in range(NCH)] for t in range(ntile_max)]
            b_xnT = [buf("xnT%d" % t) for t in range(ntile_max)]
            b_PA = [buf("PA%d" % t) for t in range(ntile_max)]
            b_PB = [buf("PB%d" % t) for t in range(ntile_max)]
            b_PC = [buf("PC%d" % t) for t in range(ntile_max)]
            b_MB = buf("MBR")
            b_MBt = [buf("MBt%d" % t) for t in range(ntile_max)]
            b_CE = buf("CE")
            b_CES = buf("CES")
            b_KT = [buf("KT%d" % i) for i in range(STW // 128 + 1)]
            b_V = [buf("V%d" % i) for i in range(STW // 128 + 1)]
            b_KTb = [buf("KTb%d" % l) for l in range(L)]
            b_Vb = [buf("Vb%d" % l) for l in range(L)]
            b_CEb = [buf("CEb%d" % l) for l in range(L)]
            b_IO = [buf("IO0"), buf("IO1")]
            b_const = buf("const")
            b_ring = [buf("ring%d" % i) for i in range(nslots)]

            schedule = []
            for si in range(len(sts)):
                for l in range(L):
                    for (pn, pk_, pg) in PIECES:
                        schedule.append((l, pn))
            wstate = dict(next=0, cur=-1)

            def pump(limit):
                while wstate["next"] < min(limit, len(schedule)):
                    k = wstate["next"]
                    l, pn = schedule[k]
                    off, kk, g = PIECE_OFF[pn]
                    slot = k % nslots
                    S.dma("pool", "ring%d" % slot, RING[slot][:, 0:kk * g], wp_d[l, :, off:off + kk * g], writes=[b_ring[slot]])
                    wstate["next"] = k + 1

            def piece(pn):
                wstate["cur"] += 1
                k = wstate["cur"]
                assert schedule[k][1] == pn, (schedule[k], pn)
                pump(k + 1)
                off, kk, g = PIECE_OFF[pn]
                slot = k % nslots
                return RING[slot][:, 0:kk * g].rearrange("p (k g) -> p k g", k=kk), b_ring[slot]

            def piece_done():
                pump(wstate["cur"] + nslots + 1)

            pump(nslots)

            cq = "sp"
            for dst, src in [(IDENT, ident_d), (NW, nw_d), (BGH, bg_d), (QW, qw_d), (KW, kw_d), (SE4, sk_d), (CW, cw_d)]:
                S.dma(cq, "const", dst[:], src, writes=[b_const])
            S.dma(cq, "const", BIAS[:, 0:4, :], bias_d[0:4].rearrange("k p n -> p k n"), writes=[b_const])
            for dst, src in [(IDENTB, ident_d), (TRI, tri_d), (BDM, bdm_d), (BD64, bd64_d), (REPI, repi_d)]:
                S.dma("pool", "constp", dst[:], src, writes=[b_const])
            S.op("dve", lambda e: e.memset(ONESB[:], 1.0), writes=[b_const])
            S.op("dve", lambda e: e.memset(NEGH[:], -0.5), writes=[b_const])
            S.op("dve", lambda e: e.memset(EPSC[:], EPS), writes=[b_const])
            S.op("dve", lambda e: e.memset(KTb[:], 0.0), writes=b_KTb)
            S.op("dve", lambda e: e.memset(Vb[:], 0.0), writes=b_Vb)
            S.op("dve", lambda e: e.memset(CEb[:], 0.0), writes=b_CEb)
            S.op("dve", lambda e: e.tensor_scalar(BGH[:], BGH[:], 0.5, None, op0=ALU.mult), reads=[b_const], writes=[b_const])
            S.op("act", lambda e: e.activation(SE4[:], SE4[:], AF.Exp), reads=[b_const], writes=[b_const])
            for l in range(L):
                bw = buf("WSPL")
                S.dma("pool", "wspl", WSPL[:], wsp_d[l].rearrange("g t s -> t g s"), writes=[bw])
                pst, bp = psum.get()
                pstb = pst.bitcast(BF16)
                for g in range(4):
                    S.op("pe", lambda e, g=g: e.transpose(pstb[:, g * 128:(g + 1) * 128], WSPL[:, g, :], IDENTB[:]),
                         reads=[bw, b_const], writes=[bp])
                for g in range(4):
                    S.op("dve", lambda e, g=g, l=l: e.tensor_tensor(WMT[:, l, g, :], pstb[:, g * 128:(g + 1) * 128], TRI[:], op=ALU.mult),
                         reads=[bp, b_const], writes=[b_const])
                ps2, bp2 = psum.get()
                for g in range(4):
                    S.op("pe", lambda e, g=g, l=l: e.matmul(
                        ps2[:, g * 128:(g + 1) * 128].rearrange("p (n t) -> p n t", n=NSEQ), REPI[:],
                        WMT[0:8, l, g, 0:8].unsqueeze(1).to_broadcast([8, NSEQ, 8]), start=True, stop=True),
                        reads=[b_const], writes=[bp2])
                for g in range(4):
                    S.op("dve", lambda e, g=g, l=l: e.tensor_tensor(WMTS[:, l, g, :], ps2[:, g * 128:(g + 1) * 128], BDM[:], op=ALU.mult),
                         reads=[bp2, b_const], writes=[b_const])


            def mm_group(ps_ap, bp, pairs, reads, **kw):
                n = len(pairs)
                for i, (lh, rh) in enumerate(pairs):
                    S.op("pe", lambda e, lh=lh, rh=rh, i=i: e.matmul(ps_ap, lh, rh, start=(i == 0), stop=(i == n - 1), **kw),
                         reads=reads, writes=[bp])

            def inproj(ps_ap, bp, w_ap, bw, col0, ncols, tcols, t):
                mm_group(ps_ap, bp, [(w_ap[:, k, col0:col0 + ncols], xnT[:, k, tcols]) for k in range(NCH)],
                         reads=[bw, b_xnT[t]])

            def rstd_part1a(sq_list, sq_bufs, n, H, rhs_sel, scale, r=None, br=None):
                nb = n // 128
                pst, bpst = psum_hi.get()
                for b in range(nb):
                    mm_group(pst[:, b * H:(b + 1) * H], bpst, [(sq[:, b * 128:(b + 1) * 128], rhs_sel) for sq in sq_list],
                             reads=list(sq_bufs) + [b_const])
                if r is None:
                    r, br = rps.get()
                w = nb * H
                S.op("act", lambda e: e.activation(r[:, 0:w], pst[:, 0:w], AF.Identity, bias=EPSC[:, 0:1], scale=scale),
                     reads=[bpst, b_const], writes=[br])
                S.op("pool", lambda e: e.tensor_tensor(r[:, 0:w], r[:, 0:w], NEGH[:, 0:w], op=ALU.pow),
                     reads=[br, b_const], writes=[br])
                return r, br

            def rstd_part1b(r, br, n, H):
                r2l = []
                for b in range(n // 128):
                    r2, br2 = r2s.get()
                    S.op("dve", lambda e, b=b, r2=r2: e.tensor_copy(
                        r2.rearrange("p (h d) -> p h d", h=H), r[:, b * H:(b + 1) * H].unsqueeze(2).to_broadcast([128, H, 128 // H])),
                        reads=[br], writes=[br2])
                    r2l.append((r2, br2))
                return r2l

            def rstd_part1(sq_list, sq_bufs, n, H, rhs_sel, scale):
                r, br = rstd_part1a(sq_list, sq_bufs, n, H, rhs_sel, scale)
                return rstd_part1b(r, br, n, H)

            def rstd_part2(r2l, n, to_sbuf):
                pbc, bpbc = psum_hi.get()
                for b, (r2, br2) in enumerate(r2l):
                    S.op("pe", lambda e, b=b, r2=r2: e.transpose(pbc[:, b * 128:(b + 1) * 128], r2, IDENT[:]),
                         reads=[br2, b_const], writes=[bpbc])
                if not to_sbuf:
                    return pbc, bpbc, (pbc, bpbc)
                rs, brs = tmps.get()
                S.op("act", lambda e: e.copy(rs[:, 0:n], pbc[:, 0:n]), reads=[bpbc], writes=[brs])
                return rs, brs, (pbc, bpbc)

            def run_pipeline(items, order):
                ns = len(items[0])
                for step in range(len(items) + ns - 1):
                    for s_ in order:
                        k = step - s_
                        if 0 <= k < len(items):
                            items[k][s_]()

            def silu2(ps_ap, bp, n):
                th, bth = tmps.get()
                S.op("act", lambda e: e.activation(th[:, 0:n], ps_ap, AF.Tanh, scale=0.5), reads=[bp], writes=[bth])
                t2, bt2 = tmps.get()
                S.op("dve", lambda e: e.scalar_tensor_tensor(t2[:, 0:n], th[:, 0:n], 1.0, ps_ap, op0=ALU.add, op1=ALU.mult),
                     reads=[bth, bp], writes=[bt2])
                return t2, bt2

            def transpose_out(src_ap, bsrc, rows, ncols_src, dst_sb, bdst, dst_cols, ps_lease=None):
                ps, bp = ps_lease if ps_lease is not None else psum.get()
                S.op("pe", lambda e: e.transpose(ps[0:ncols_src, 0:rows], src_ap, IDENT[0:rows, 0:rows]),
                     reads=[bsrc, b_const], writes=[bp])
                S.op("act", lambda e: e.copy(dst_sb[0:ncols_src, dst_cols], ps[0:ncols_src, 0:rows]), reads=[bp], writes=[bdst])

            cur_si = [0]

            def stage(name):
                if stop_after == name or stop_after == "%d:%s" % (cur_si[0], name):
                    raise StopBuild()

            try:
              stage("setup")
              for si, st in enumerate(sts):
                  cur_si[0] = si
                  tiles = []
                  col = 0
                  for t, tl in enumerate(st):
                      n = 128 * len(tl["blocks"])
                      tiles.append(dict(kind=tl["kind"], c0=col, n=n, blocks=tl["blocks"], t=t))
                      col += n
                  nblk = col // 128
                  has_sample = any(tl["kind"] == "S" for tl in tiles)
                  n_prompt_blk = sum(len(tl["blocks"]) for tl in tiles if tl["kind"] == "P")
                  first_blk_is_out = None

                  io_i = 0
                  for tl in tiles:
                      for bi, gb in enumerate(tl["blocks"]):
                          c0 = tl["c0"] + bi * 128
                          src = xs if gb == "S" else xp[gb * 128:(gb + 1) * 128, :]
                          io, bio = IO[io_i % 2], b_IO[io_i % 2]
                          S.dma("sp", "io%d" % (io_i % 2), io[:], src, writes=[bio])
                          io_i += 1
                          for h in range(2):
                              ps, bp = psum.get()
                              for c in range(4):
                                  cc = h * 4 + c
                                  S.op("pe", lambda e, cc=cc, c=c, ps=ps, io=io: e.transpose(ps[:, c * 128:(c + 1) * 128], io[:, cc * 128:(cc + 1) * 128], IDENT[:]),
                                       reads=[bio, b_const], writes=[bp])
                              eng = "act" if h == 0 else "dve"
                              if eng == "act":
                                  S.op("act", lambda e, h=h, ps=ps, c0=c0: e.copy(xT[:, h * 4:(h + 1) * 4, c0:c0 + 128], ps.rearrange("p (c n) -> p c n", c=4)),
                                       reads=[bp], writes=b_xT[tl["t"]][h * 4:(h + 1) * 4])
                              else:
                                  S.op("dve", lambda e, h=h, ps=ps, c0=c0: e.tensor_copy(xT[:, h * 4:(h + 1) * 4, c0:c0 + 128], ps.rearrange("p (c n) -> p c n", c=4)),
                                       reads=[bp], writes=b_xT[tl["t"]][h * 4:(h + 1) * 4])

                  stage('load')
                  if si == 0:
                      for l_ in range(L):
                          S.dma("sp", "cachecopy", sk_o[l_, :, 0:128 - TS, :], ck_d[l_, :, TS:128, :])
                          S.dma("sp", "cachecopy", sv_o[l_, :, 0:128 - TS, :], cv_d[l_, :, TS:128, :])
                  if si == 0:
                      S.dma("sp", "biasx", BIAS[:, 4:6, :], bias_d[4:6].rearrange("k p n -> p k n"), writes=[buf("biasx")])
                  if has_sample:
                      S.dma("sp", "biasx", BIAS[:, 4:8, :], bias_d[6:10].rearrange("k p n -> p k n"), writes=[buf("biasx")])
                  b_biasx = buf("biasx")

                  tiles_full = tiles

                  def tiles_at(l_):
                      min_gb = first_out_gb - (L - l_)
                      res = []
                      for lo in (min_gb, min_gb + 1 if min_gb >= 0 else min_gb):
                          lst = []
                          for tl in tiles_full:
                              keep = [(i, gb) for i, gb in enumerate(tl["blocks"]) if gb == "S" or gb >= lo]
                              if not keep:
                                  continue
                              i0_ = keep[0][0]
                              lst.append(dict(kind=tl["kind"], c0=tl["c0"] + 128 * i0_, n=128 * len(keep), blocks=[gb for _, gb in keep], t=tl["t"]))
                          res.append(lst)
                      return res

                  norm_pre = {}

                  def norm_AB(l_, tl):
                      t, c0, n = tl["t"], tl["c0"], tl["n"]
                      tc_ = slice(c0, c0 + n)
                      for c in range(NCH):
                          S.op("act", lambda e, c=c: e.activation(xnT[:, c, tc_], xT[:, c, tc_], AF.Square),
                               reads=[b_xT[t][c]], writes=[b_xnT[t]])
                      norm_pre[(l_, t)] = rstd_part1a([xnT[:, c, tc_] for c in range(NCH)], [b_xnT[t]], n, 1, ONESB[:, 0:1], 1.0 / D,
                                                      r=NRT[t][:], br=buf("NRT%d" % t))

                  stored = set()
                  store_i = [0]

                  def emit_store(tl):
                      stored.add(tl["t"])
                      for bi, gb in enumerate(tl["blocks"]):
                          if gb != "S" and gb < first_out_gb:
                              continue
                          c0 = tl["c0"] + bi * 128
                          io_i = store_i[0]
                          io, bio = IO[io_i % 2], b_IO[io_i % 2]
                          for h in range(2):
                              ps, bp = psum.get()
                              for c in range(4):
                                  cc = h * 4 + c
                                  S.op("pe", lambda e, cc=cc, c=c, ps=ps, c0=c0: e.transpose(ps[:, c * 128:(c + 1) * 128], xT[:, cc, c0:c0 + 128], IDENT[:]),
                                       reads=[b_xT[tl["t"]][cc], b_const], writes=[bp])
                              if h == 0:
                                  S.op("act", lambda e, ps=ps, io=io: e.copy(io[:, 0:512], ps[:]), reads=[bp], writes=[bio])
                              else:
                                  S.op("dve", lambda e, ps=ps, io=io: e.tensor_copy(io[:, 512:1024], ps[:]), reads=[bp], writes=[bio])
                          dst = ys_o if gb == "S" else yp_o[(gb - first_out_gb) * 128:(gb - first_out_gb + 1) * 128, :]
                          S.dma("sp", "io%d" % (io_i % 2), dst, io[:], reads=[bio])
                          store_i[0] += 1

                  norm_done = set()

                  def norm_C(l_, tl):
                      t, c0, n = tl["t"], tl["c0"], tl["n"]
                      tc_ = slice(c0, c0 + n)
                      rr, brr = norm_pre.pop((l_, t))
                      r, br, _ = rstd_part2(rstd_part1b(rr, brr, n, 1), n, False)
                      for c in range(NCH):
                          S.op("dve", lambda e, c=c: e.scalar_tensor_tensor(
                              xnT[:, c, tc_], xT[:, c, tc_], NW[:, l_ * 8 + c:l_ * 8 + c + 1], r[:, 0:n], op0=ALU.mult, op1=ALU.mult),
                              reads=[b_xT[t][c], br, b_const], writes=[b_xnT[t]])
                      norm_done.add((l_, t))

                  for l in range(L):
                      tiles_kv, tiles = tiles_at(l)
                      b_lay = buf("laycst")
                      S.dma("sp", "laycst", VNW[:], vnw_d[l], writes=[b_lay])
                      S.dma("sp", "laycst", BSPT[:], bspt_d[l], writes=[b_lay])
                      if has_sample:
                          S.dma("sp", "laycst", BSPTS[:], bspts_d[l], writes=[b_lay])

                      S.op("pool", lambda e: e.tensor_copy(KTe[:, 0:128], KTb[:, l, :]), reads=[b_KTb[l]], writes=[b_KT[0]])
                      S.op("pool", lambda e: e.tensor_copy(Ve[:, 0, :], Vb[:, l, :]), reads=[b_Vb[l]], writes=[b_V[0]])
                      if has_sample:
                          S.dma("pool", "ckl", IO[0][:].bitcast(BF16)[:, 0:NSEQ * 128].rearrange("p (n f) -> p n f", n=NSEQ),
                                ck_d[l].rearrange("n j f -> j n f"), writes=[b_IO[0]])
                          S.dma("pool", "cvl", IO[1][:].bitcast(BF16)[:, 0:NSEQ * 128].rearrange("p (n f) -> p n f", n=NSEQ),
                                cv_d[l].rearrange("n j f -> j n f"), writes=[b_IO[1]])
                          CK = IO[0][:].bitcast(BF16)[:, 0:NSEQ * 128].rearrange("p (n f) -> p n f", n=NSEQ)
                          CV = IO[1][:].bitcast(BF16)[:, 0:NSEQ * 128].rearrange("p (n f) -> p n f", n=NSEQ)
                      wts = {}

                      def mk_norm(tl):
                          t = tl["t"]

                          def fA():
                              if (l, t) not in norm_pre and (l, t) not in norm_done:
                                  norm_AB(l, tl)

                          def fC():
                              if (l, t) not in norm_done:
                                  norm_C(l, tl)
                              norm_done.discard((l, t))
                          return [fA, (lambda: None), fC, (lambda: None)]

                      def mk_q(tl, g, first, last):
                          t, c0, n = tl["t"], tl["c0"], tl["n"]
                          tc_ = slice(c0, c0 + n)
                          st_ = {}

                          def fA():
                              wq, bwq = wts["q"]
                              ps, bp = psum_lo.get()
                              inproj(ps[:, 0:n], bp, wq, bwq, g * 128, 128, tc_, t)
                              sq, bsq = bts.get()
                              S.op("act", lambda e: e.activation(sq[:, 0:n], ps[:, 0:n], AF.Square), reads=[bp], writes=[bsq])
                              st_.update(ps=ps, bp=bp, sq=sq, bsq=bsq)

                          def fB():
                              st_["rr"] = rstd_part1a([st_["sq"][:, 0:n]], [st_["bsq"]], n, 2, BD64[:, 0:128:64], 1.0 / 64)

                          def fR():
                              st_["r2l"] = rstd_part1b(st_["rr"][0], st_["rr"][1], n, 2)

                          def fC():
                              r, br, _ = rstd_part2(st_["r2l"], n, True)
                              ps, bp = st_["ps"], st_["bp"]
                              S.op("dve", lambda e: e.scalar_tensor_tensor(
                                  QT[:, g, tc_], ps[:, 0:n], QW[:, l:l + 1], r[:, 0:n], op0=ALU.mult, op1=ALU.mult),
                                  reads=[bp, br, b_const], writes=[b_PB[t]])
                          return [fA, fB, fR, fC]

                      def mk_k(tl, first, last):
                          t, c0, n = tl["t"], tl["c0"], tl["n"]
                          tc_ = slice(c0, c0 + n)
                          st_ = {}

                          def fA():
                              wkv, bwkv = wts["kv"]
                              ps, bp = psum_lo.get()
                              inproj(ps[:, 0:n], bp, wkv, bwkv, 0, 128, tc_, t)
                              sq, bsq = bts.get()
                              S.op("act", lambda e: e.activation(sq[:, 0:n], ps[:, 0:n], AF.Square), reads=[bp], writes=[bsq])
                              st_.update(ps=ps, bp=bp, sq=sq, bsq=bsq)

                          def fB():
                              st_["rr"] = rstd_part1a([st_["sq"][:, 0:n]], [st_["bsq"]], n, 2, BD64[:, 0:128:64], 1.0 / 64)

                          def fR():
                              st_["r2l"] = rstd_part1b(st_["rr"][0], st_["rr"][1], n, 2)

                          def fC():
                              r, br, pfree = rstd_part2(st_["r2l"], n, True)
                              ps, bp = st_["ps"], st_["bp"]
                              kslots = [b_KT[1 + (c0 // 128) + i] for i in range(n // 128)]
                              S.op("dve", lambda e: e.scalar_tensor_tensor(
                                  KTe[:, 128 + c0:128 + c0 + n], ps[:, 0:n], KW[:, l:l + 1], r[:, 0:n], op0=ALU.mult, op1=ALU.mult),
                                  reads=[bp, br, b_const], writes=kslots)
                              for bi, gb in enumerate(tl["blocks"]):
                                  if (gb == "S") or (gb == last_gb):
                                      kf_, bkf = tmps.get()
                                      KF = kf_[:, 0:128]
                                      S.op("dve", lambda e, bi=bi: e.scalar_tensor_tensor(
                                          KF, ps[:, bi * 128:(bi + 1) * 128], KW[:, l:l + 1], r[:, bi * 128:(bi + 1) * 128], op0=ALU.mult, op1=ALU.mult),
                                          reads=[bp, br, b_const], writes=[bkf])
                                      bko = buf("KOs")
                                      transpose_out(KF, bkf, 128, 128, KOs, bko, slice(0, 128), ps_lease=pfree)
                                      if gb == "S":
                                          for nn in range(NSEQ):
                                              S.dma("sp", "kout", sk_o[l, nn, 128 - TS:128, :], KOs[nn * TS:(nn + 1) * TS, :], reads=[bko])
                                      else:
                                          S.dma("sp", "kout", pk_o[l], KOs[:], reads=[bko])
                          return [fA, fB, fR, fC]

                      def mk_v(tl, last):
                          t, c0, n = tl["t"], tl["c0"], tl["n"]

                          def fA():
                              wkv, bwkv = wts["kv"]
                              psv, bpv = psum_lo.get()
                              for bi, gb in enumerate(tl["blocks"]):
                                  bc = c0 + bi * 128
                                  mm_group(psv[:, bi * 128:(bi + 1) * 128], bpv, [(xnT[:, k, bc:bc + 128], wkv[:, k, 128:256]) for k in range(NCH)],
                                           reads=[bwkv, b_xnT[t]])
                              vs0 = 1 + c0 // 128
                              nb = n // 128
                              S.op("act", lambda e: e.copy(Ve[:, vs0:vs0 + nb, :], psv[:, 0:n].rearrange("p (b f) -> p b f", b=nb)),
                                   reads=[bpv], writes=[b_V[vs0 + i] for i in range(nb)])
                              for bi, gb in enumerate(tl["blocks"]):
                                  if (gb == "S") or (gb == last_gb):
                                      bvo = buf("KOs")
                                      S.op("act", lambda e, bi=bi: e.copy(VOs[:], psv[:, bi * 128:(bi + 1) * 128]), reads=[bpv], writes=[bvo])
                                      if gb == "S":
                                          for nn in range(NSEQ):
                                              S.dma("sp", "vout", sv_o[l, nn, 128 - TS:128, :], VOs[nn * TS:(nn + 1) * TS, :], reads=[bvo])
                                      else:
                                          S.dma("sp", "vout", pv_o[l], VOs[:], reads=[bvo])

                          return [fA, (lambda: None), (lambda: None), (lambda: None)]

                      wts["q"] = piece("q")
                      wts["kv"] = piece("kv")
                      items = [mk_norm(tl) for tl in tiles_kv]
                      full_t = {tl["t"]: tl for tl in tiles}
                      rest = []
                      for i_norm, tk in enumerate(tiles_kv):
                          if tk["t"] in full_t:
                              rest += [(i_norm, mk_q(full_t[tk["t"]], g, False, False)) for g in range(4)]
                          rest.append((i_norm, mk_k(tk, False, False)))
                          rest.append((i_norm, mk_v(tk, False)))
                      pad = 1
                      for p_, (i_norm, _) in enumerate(rest):
                          pad = max(pad, i_norm + 3 - (len(tiles_kv) + p_))
                      for _ in range(pad):
                          items.append([(lambda: None)] * 4)
                      items += [it for _, it in rest]
                      run_pipeline(items, [0, 1, 2, 3])
                      piece_done()
                      if has_sample:
                          b_CKT = buf("CKT")
                          for q4 in range(NSEQ // 4):
                              ps, bp = psum.get()
                              psb = ps.bitcast(BF16)
                              for i in range(4):
                                  nn = q4 * 4 + i
                                  S.op("pe", lambda e, nn=nn, i=i, psb=psb: e.transpose(psb[:, i * 128:(i + 1) * 128], CK[:, nn, :], IDENTB[:]),
                                       reads=[b_IO[0], b_const], writes=[bp])
                              S.op("act", lambda e, q4=q4, psb=psb: e.copy(CKT[:, q4 * 4:(q4 + 1) * 4, :], psb[:, 0:512].rearrange("p (n f) -> p n f", n=4)),
                                   reads=[bp], writes=[b_CKT])


                      stage('Akv')
                      def mk_att(tl, bi, gb, kv):
                          t, c0, n = tl["t"], tl["c0"], tl["n"]
                          bc = c0 + bi * 128
                          bslot = bc // 128
                          pr = slice(64 * kv, 64 * kv + 64)
                          st_ = {}

                          def score(lhsT, rhs_ap, reads, bias_idx, bias_buf, out_view=None):
                              ps, bp = psum.get()
                              S.op("pe", lambda e: e.matmul(ps[:].rearrange("p (g t) -> p g t", g=4), lhsT, rhs_ap, start=True, stop=True),
                                   reads=reads, writes=[bp])
                              return ps, bp

                          def softexp(ps, bp, bias_idx, bias_buf):
                              sbm, bsb = tmps.get()
                              S.op("act", lambda e: e.activation(sbm[:], ps[:], AF.Exp, scale=0.125), reads=[bp], writes=[bsb])
                              pt, bpt = bts.get()
                              S.op("pool", lambda e: e.tensor_tensor(pt[:], sbm[:], BIAS[:, bias_idx, :], op=ALU.mult),
                                   reads=[bsb, bias_buf], writes=[bpt])
                              return pt, bpt

                          def fA():
                              if gb == "S":
                                  ps, bp = psum.get()
                                  for nn in range(NSEQ):
                                      S.op("pe", lambda e, nn=nn: e.matmul(
                                          ps[:, nn * 32:(nn + 1) * 32].rearrange("p (g t) -> p g t", g=4),
                                          CKT[pr, nn, :], QT[pr, :, bc + nn * TS:bc + (nn + 1) * TS], start=True, stop=True),
                                          reads=[b_CKT, b_PB[t]], writes=[bp])
                                  st_["ptc"] = softexp(ps, bp, 4 + kv, b_biasx)
                                  ps, bp = score(KTe[pr, 128 + bc:128 + bc + 128], QT[pr, :, bc:bc + 128], [b_KT[bslot + 1], b_PB[t]], None, None)
                                  st_["ptn"] = softexp(ps, bp, 6 + kv, b_biasx)
                              else:
                                  first = (si == 0 and gb == first_out_gb)
                                  kbs = [(bslot, (4 + kv) if first else kv, b_biasx if first else b_const), (bslot + 1, 2 + kv, b_const)]
                                  pts = []
                                  for (slot, bidx, bb) in kbs:
                                      ps, bp = score(KTe[pr, slot * 128:(slot + 1) * 128], QT[pr, :, bc:bc + 128], [b_KT[slot], b_PB[t]], None, None)
                                      pt, bpt = softexp(ps, bp, bidx, bb)
                                      pts.append((pt, bpt, slot))
                                  st_["pts"] = pts

                          def fB():
                              pso, bpo = psum.get()
                              if gb == "S":
                                  ptc, bptc = st_["ptc"]
                                  ptn, bptn = st_["ptn"]
                                  ptn3 = ptn[:].rearrange("p (g t) -> p g t", g=4)
                                  ptc4 = ptc[:].rearrange("p (n g t) -> p n g t", n=NSEQ, g=4)
                                  for par in range(2):
                                      orow = slice(64 * par, 64 * par + 64)
                                      tp = dict(tile_position=(0, 64)) if par == 1 else {}
                                      for which in range(2):
                                          ocol = 256 * which
                                          oap = pso[orow, ocol:ocol + 256].rearrange("p (c q) -> p c q", c=2)
                                          lhs_new = Ve[:, bslot + 1, pr] if which == 0 else ONESB[:, 0:64]
                                          S.op("pe", lambda e: e.matmul(
                                              oap, lhs_new, ptn3[:, par::2, :], start=True, stop=False, skip_group_check=True, **tp),
                                              reads=[bptn, b_V[bslot + 1], b_const], writes=[bpo])
                                          for nn in range(NSEQ):
                                              lhs_c = CV[:, nn, pr] if which == 0 else ONESB[:, 0:64]
                                              S.op("pe", lambda e, nn=nn, lhs_c=lhs_c: e.matmul(
                                                  oap[:, :, nn * TS:(nn + 1) * TS], lhs_c, ptc4[:, nn, par::2, :],
                                                  start=False, stop=(nn == NSEQ - 1), skip_group_check=True, **tp),
                                                  reads=[bptc, b_IO[1], b_const], writes=[bpo])
                              else:
                                  pts = st_["pts"]
                                  for par in range(2):
                                      orow = slice(64 * par, 64 * par + 64)
                                      tp = dict(tile_position=(0, 64)) if par == 1 else {}
                                      for which in range(2):
                                          ocol = 256 * which
                                          oap = pso[orow, ocol:ocol + 256].rearrange("p (c q) -> p c q", c=2)
                                          for i, (pt, bpt, slot) in enumerate(pts):
                                              lhs = Ve[:, slot, pr] if which == 0 else ONESB[:, 0:64]
                                              pt3 = pt[:].rearrange("p (g t) -> p g t", g=4)
                                              S.op("pe", lambda e, lhs=lhs, pt3=pt3, i=i: e.matmul(
                                                  oap, lhs, pt3[:, par::2, :], start=(i == 0), stop=(i == 1), skip_group_check=True, **tp),
                                                  reads=[bpt, b_V[slot], b_const], writes=[bpo])
                              ds, bds = smalls.get()
                              for c2 in range(2):
                                  S.op("act", lambda e, c2=c2: e.activation(
                                      ds[:, c2 * 128:(c2 + 1) * 128], pso[:, 256 + c2 * 128:256 + (c2 + 1) * 128], AF.Identity,
                                      bias=SE4[:, l * 4 + kv * 2 + c2:l * 4 + kv * 2 + c2 + 1], scale=1.0),
                                      reads=[bpo, b_const], writes=[bds])
                              S.op("dve", lambda e: e.reciprocal(ds[:], ds[:]), reads=[bds], writes=[bds])
                              S.op("dve", lambda e: e.tensor_tensor(
                                  YX[:, kv * 2:(kv + 1) * 2, bc:bc + 128], pso[:, 0:256].rearrange("p (c q) -> p c q", c=2),
                                  ds[:].rearrange("p (c q) -> p c q", c=2), op=ALU.mult),
                                  reads=[bpo, bds], writes=[b_MB] + b_MBt)
                          return [fA, (lambda: None), fB]

                      items = [mk_att(tl, bi, gb, kv) for tl in tiles for bi, gb in enumerate(tl["blocks"]) for kv in range(2)]
                      run_pipeline(items, [0, 1, 2])

                      stage('Aattn')
                      wza, bwza = piece("za")
                      for tl in tiles:
                          t, c0, n = tl["t"], tl["c0"], tl["n"]
                          tc_ = slice(c0, c0 + n)
                          for c2 in range(4):
                              ps, bp = psum.get()
                              inproj(ps[:, 0:n], bp, wza, bwza, c2 * 128, 128, tc_, t)
                              t2, bt2 = silu2(ps[:, 0:n], bp, n)
                              S.op("pool", lambda e, t2=t2, c2=c2: e.tensor_tensor(PA[:, c2, tc_], t2[:, 0:n], YX[:, c2, tc_], op=ALU.mult),
                                   reads=[bt2, b_MB], writes=[b_PA[t]])
                      piece_done()
                      lp = n_prompt_blk
                      S.op("pool", lambda e: e.tensor_copy(KTb[:, l, :], KTe[:, lp * 128:(lp + 1) * 128]), reads=[b_KT[lp]], writes=[b_KTb[l]])
                      S.op("pool", lambda e: e.tensor_copy(Vb[:, l, :], Ve[:, lp, :]), reads=[b_V[lp]], writes=[b_Vb[l]])

                      stage('A')
                      if has_sample:
                          bsc = buf("SCin")
                          S.dma("sp", "scin", SCs[:], sconv_d[l], writes=[bsc])
                      for c in range(4):
                          wb, bwb = piece("b%d" % c)
                          S.op("dve", lambda e, c=c: e.tensor_copy(CE[:, 0:2], CEb[:, l, c, :]), reads=[b_CEb[l]], writes=[b_CE])
                          if has_sample:
                              ps, bp = psum.get()
                              S.op("pe", lambda e, ps=ps, c=c: e.transpose(ps[:, 0:32], SCs[:, c * 128:(c + 1) * 128], IDENT[64:96, 64:96]),
                                   reads=[bsc, b_const], writes=[bp])
                              S.op("act", lambda e, ps=ps: e.copy(CES[:, :, 0:2], ps[:, 0:32].rearrange("p (n r) -> p n r", r=2)),
                                   reads=[bp], writes=[b_CES])
                          for tl in tiles_kv:
                              t, c0, n = tl["t"], tl["c0"], tl["n"]
                              tc_ = slice(c0, c0 + n)
                              deferred = []
                              ps1, bp1 = psum.get()
                              inproj(ps1[:, 0:n], bp1, wb, bwb, 0, 128, tc_, t)
                              gc, bgc = tmps.get()
                              S.op("act", lambda e, ps1=ps1, gc=gc: e.copy(gc[:, 0:n], ps1[:, 0:n]), reads=[bp1], writes=[bgc])
                              ps2, bp2 = psum.get()
                              inproj(ps2[:, 0:n], bp2, wb, bwb, 128, 128, tc_, t)
                              co, bco = tmps.get()
                              if tl["kind"] == "P":
                                  S.op("dve", lambda e, ps2=ps2, gc=gc: e.tensor_tensor(CE[:, 2 + c0:2 + c0 + n], ps2[:, 0:n], gc[:, 0:n], op=ALU.mult),
                                       reads=[bp2, bgc], writes=[b_CE])
                                  for r_ in range(3):
                                      src = CE[:, c0 + r_:c0 + r_ + n]
                                      wcol = CW[:, l * 12 + r_ * 4 + c:l * 12 + r_ * 4 + c + 1]
                                      if r_ == 0:
                                          S.op("dve", lambda e, src=src, wcol=wcol, co=co: e.tensor_scalar(co[:, 0:n], src, wcol, None, op0=ALU.mult),
                                               reads=[b_CE, b_const], writes=[bco])
                                      else:
                                          S.op("dve", lambda e, src=src, wcol=wcol, co=co: e.scalar_tensor_tensor(
                                              co[:, 0:n], src, wcol, co[:, 0:n], op0=ALU.mult, op1=ALU.add),
                                              reads=[b_CE, b_const, bco], writes=[bco])
                                  if last_gb in tl["blocks"]:
                                      e0 = 2 + c0 + n - 2
                                      bpc = buf("PCs")
                                      deferred.append(lambda e0=e0, bpc=bpc, c=c: transpose_out(
                                          CE[:, e0:e0 + 2], b_CE, 128, 2, PCs, bpc, slice(c * 128, (c + 1) * 128)))
                              else:
                                  ces_in = CES[:, :, 2:10]
                                  S.op("dve", lambda e, ps2=ps2, gc=gc: e.tensor_tensor(
                                      ces_in, ps2[:, 0:n].rearrange("p (n t) -> p n t", t=TS), gc[:, 0:n].rearrange("p (n t) -> p n t", t=TS), op=ALU.mult),
                                      reads=[bp2, bgc], writes=[b_CES])
                                  co3 = co[:, 0:n].rearrange("p (n t) -> p n t", t=TS)
                                  for r_ in range(3):
                                      src = CES[:, :, r_:r_ + TS]
                                      wcol = CW[:, l * 12 + r_ * 4 + c:l * 12 + r_ * 4 + c + 1]
                                      if r_ == 0:
                                          S.op("dve", lambda e, src=src, wcol=wcol, co3=co3: e.tensor_scalar(co3, src, wcol, None, op0=ALU.mult),
                                               reads=[b_CES, b_const], writes=[bco])
                                      else:
                                          S.op("dve", lambda e, src=src, wcol=wcol, co3=co3: e.scalar_tensor_tensor(
                                              co3, src, wcol, co3, op0=ALU.mult, op1=ALU.add),
                                              reads=[b_CES, b_const, bco], writes=[bco])
                                  bc32 = buf("C32")
                                  S.op("dve", lambda e: e.tensor_copy(C32[:].rearrange("p (n r) -> p n r", r=2), CES[:, :, 8:10]),
                                       reads=[b_CES], writes=[bc32])
                                  bscs = buf("SCs2")
                                  deferred.append(lambda bc32=bc32, bscs=bscs, c=c: transpose_out(
                                      C32[:], bc32, 128, 32, SCo, bscs, slice(c * 128, (c + 1) * 128)))
                              ps3, bp3 = psum.get()
                              inproj(ps3[:, 0:n], bp3, wb, bwb, 256, 128, tc_, t)
                              S.op("dve", lambda e, ps3=ps3, co=co: e.tensor_tensor(co[:, 0:n], ps3[:, 0:n], co[:, 0:n], op=ALU.mult),
                                   reads=[bp3, bco], writes=[bco])
                              ps4, bp4 = psum.get()
                              inproj(ps4[:, 0:n], bp4, wb, bwb, 384, 128, tc_, t)
                              for fn_ in deferred:
                                  fn_()
                              t2, bt2 = silu2(ps4[:, 0:n], bp4, n)
                              S.op("pool", lambda e, t2=t2, co=co, c=c: e.tensor_tensor(PB[:, c, tc_], t2[:, 0:n], co[:, 0:n], op=ALU.mult),
                                   reads=[bt2, bco], writes=[b_PB[t]])
                          e0 = 2 + n_prompt_blk * 128 - 2
                          S.op("dve", lambda e, c=c, e0=e0: e.tensor_copy(CEb[:, l, c, :], CE[:, e0:e0 + 2]), reads=[b_CE], writes=[b_CEb[l]])
                          piece_done()
                      if last_gb in [gb for tl in tiles_kv for gb in tl["blocks"]]:
                          S.dma("sp", "pcout", pc_o[l], PCs[:], reads=[buf("PCs")])
                      if has_sample:
                          S.dma("sp", "scout", sc_o[l], SCo[:], reads=[buf("SCs2")])

                      stage('B')
                      wvc, bwvc = piece("vc")

                      def mk_vc(tl, bi, gb):
                          t, c0, n = tl["t"], tl["c0"], tl["n"]
                          bc = c0 + bi * 128
                          st_ = {}

                          def fA():
                              psv, bpv = psum.get()
                              mm_group(psv[:], bpv, [(xnT[:, k, bc:bc + 128], wvc[:, k, :]) for k in range(NCH)], reads=[bwvc, b_xnT[t]])
                              junk, bj = tmps.get()
                              SS, bss = sss.get()
                              S.op("act", lambda e: e.activation(junk[:], psv[:], AF.Square, accum_out=SS[:, 0:1]),
                                   reads=[bpv], writes=[bj, bss])
                              S.op("dve", lambda e: e.tensor_scalar(SS[:, 1:2], SS[:, 0:1], 1.0 / 512, EPS, op0=ALU.mult, op1=ALU.add),
                                   reads=[bss], writes=[bss])
                              S.op("pool", lambda e: e.tensor_tensor(SS[:, 2:3], SS[:, 1:2], NEGH[:, 0:1], op=ALU.pow), reads=[bss, b_const], writes=[bss])
                              st_.update(psv=psv, bpv=bpv, SS=SS, bss=bss)

                          def fB():
                              psv, bpv, SS, bss = st_["psv"], st_["bpv"], st_["SS"], st_["bss"]
                              vcn, bvcn = bts.get()
                              S.op("dve", lambda e: e.scalar_tensor_tensor(
                                  vcn[:], psv[:], SS[:, 2:3], VNW[:], op0=ALU.mult, op1=ALU.mult), reads=[bpv, bss, b_lay], writes=[bvcn])
                              if gb == "S":
                                  vf, bvf = tmps.get()
                                  S.op("dve", lambda e: e.scalar_tensor_tensor(
                                      vf[:], psv[:], SS[:, 2:3], VNW[:], op0=ALU.mult, op1=ALU.mult), reads=[bpv, bss, b_lay], writes=[bvf])
                                  S.dma("sp", "scvout", scv_o[l], vf[:], reads=[bvf])
                              st_.update(vcn=vcn, bvcn=bvcn)

                          def fC():
                              vcn, bvcn = st_["vcn"], st_["bvcn"]
                              pss, bps = psum.get()
                              wm = WMTS if gb == "S" else WMT
                              for g in range(4):
                                  S.op("pe", lambda e, g=g: e.matmul(
                                      pss[:, g * 128:(g + 1) * 128], vcn[:, g * 128:(g + 1) * 128], wm[:, l, g, :], start=True, stop=True),
                                      reads=[bvcn, b_const], writes=[bps])
                              bt_ = BSPTS if gb == "S" else BSPT
                              S.op("dve", lambda e: e.tensor_tensor(
                                  YX[:, :, bc:bc + 128], pss[:].rearrange("p (g t) -> p g t", g=4), bt_[:].rearrange("p (g t) -> p g t", g=4), op=ALU.add),
                                  reads=[bps, b_lay], writes=[b_MB] + b_MBt)
                          return [fA, fB, fC]

                      items = [mk_vc(tl, bi, gb) for tl in tiles for bi, gb in enumerate(tl["blocks"])]
                      run_pipeline(items, [1, 0, 2])
                      piece_done()
                      wu, bwu = piece("u")
                      for tl in tiles:
                          t, c0, n = tl["t"], tl["c0"], tl["n"]
                          tc_ = slice(c0, c0 + n)
                          for g in range(4):
                              ps, bp = psum.get()
                              inproj(ps[:, 0:n], bp, wu, bwu, g * 128, 128, tc_, t)
                              S.op("dve", lambda e, ps=ps, g=g: e.tensor_tensor(YX[:, g, tc_], ps[:, 0:n], YX[:, g, tc_], op=ALU.mult),
                                   reads=[bp, b_MB], writes=[b_MB] + b_MBt)
                      piece_done()
                      wzc, bwzc = piece("zc")
                      for tl in tiles:
                          t, c0, n = tl["t"], tl["c0"], tl["n"]
                          tc_ = slice(c0, c0 + n)
                          for g in range(4):
                              ps, bp = psum.get()
                              inproj(ps[:, 0:n], bp, wzc, bwzc, g * 128, 128, tc_, t)
                              t2, bt2 = silu2(ps[:, 0:n], bp, n)
                              S.op("pool", lambda e, t2=t2, g=g: e.tensor_tensor(PC[:, g, tc_], t2[:, 0:n], YX[:, g, tc_], op=ALU.mult),
                                   reads=[bt2, b_MB], writes=[b_PC[t]])
                      piece_done()

                      stage('C')
                      Ps = [(PA, b_PA), (PB, b_PB), (PC, b_PC)]
                      for j in range(8):
                          wg, bwg = piece("g%d" % j)
                          for tl in tiles:
                              t, c0, n = tl["t"], tl["c0"], tl["n"]
                              tc_ = slice(c0, c0 + n)
                              acc = None
                              for i in range(3):
                                  Pi, bPi = Ps[i]
                                  psa, bpa = psum.get()
                                  mm_group(psa[:, 0:n], bpa, [(wg[:, 8 + k, i * 128:(i + 1) * 128], Pi[:, k, tc_]) for k in range(4)],
                                           reads=[bwg, bPi[t]])
                                  psg, bpg = psum.get()
                                  inproj(psg[:, 0:n], bpg, wg, bwg, i * 128, 128, tc_, t)
                                  tg, btg = tmps.get()
                                  bcol = l * 24 + i * 8 + j
                                  S.op("act", lambda e, psg=psg, tg=tg, bcol=bcol: e.activation(
                                      tg[:, 0:n], psg[:, 0:n], AF.Tanh, bias=BGH[:, bcol:bcol + 1], scale=0.5), reads=[bpg, b_const], writes=[btg])
                                  S.op("dve", lambda e, psa=psa, tg=tg: e.scalar_tensor_tensor(
                                      tg[:, 0:n], tg[:, 0:n], 1.0, psa[:, 0:n], op0=ALU.add, op1=ALU.mult), reads=[btg, bpa], writes=[btg])
                                  if i == 0:
                                      acc, bacc = tg, btg
                                  elif i == 1:
                                      S.op("pool", lambda e, acc=acc, tg=tg: e.tensor_tensor(acc[:, 0:n], acc[:, 0:n], tg[:, 0:n], op=ALU.add),
                                           reads=[bacc, btg], writes=[bacc])
                                  else:
                                      S.op("pool", lambda e, acc=acc, tg=tg, j=j: e.tensor_tensor(MB[:, j, tc_], acc[:, 0:n], tg[:, 0:n], op=ALU.add),
                                           reads=[bacc, btg], writes=[b_MB, b_MBt[t]])
                          piece_done()

                      stage('G')
                      for h in range(2):
                          wo, bwo = piece("o%d" % h)
                          for tl in tiles:
                              t, c0, n = tl["t"], tl["c0"], tl["n"]
                              tc_ = slice(c0, c0 + n)
                              for e4 in range(4):
                                  ec = h * 4 + e4
                                  ps, bp = psum.get()
                                  mm_group(ps[:, 0:n], bp, [(wo[:, k, e4 * 128:(e4 + 1) * 128], MB[:, k, tc_]) for k in range(NCH)],
                                           reads=[bwo, b_MBt[t]])
                                  S.op("dve", lambda e, ps=ps, ec=ec: e.scalar_tensor_tensor(
                                      xT[:, ec, tc_], ps[:, 0:n], 0.25, xT[:, ec, tc_], op0=ALU.mult, op1=ALU.add),
                                      reads=[bp, b_xT[t][ec]], writes=[b_xT[t][ec]])
                                  if h == 1 and l + 1 == L and e4 == 0:
                                      for tp_ in tiles_full:
                                          if tp_["t"] < t and tp_["t"] not in stored:
                                              emit_store(tp_)
                                  if h == 1 and l + 1 < L and e4 == 0:
                                      for tn in tiles_at(l + 1)[0]:
                                          if (l + 1, tn["t"]) in norm_pre:
                                              norm_C(l + 1, tn)
                              if h == 1 and l + 1 < L:
                                  for tn in tiles_at(l + 1)[0]:
                                      if tn["t"] == t:
                                          norm_AB(l + 1, tn)
                          piece_done()

                  stage('O')
                  for tl in tiles_full:
                      if tl["t"] not in stored:
                          emit_store(tl)

            except StopBuild:
                pass
            if stop_after is None:
                assert wstate["cur"] == len(schedule) - 1
            S.wait_all_dma("sp")
            for e in ("pe", "act", "dve", "pool"):
                pass
            build.stats = dict(nops=dict(S.nops), nwaits=S.nwaits, nsig=dict(S.sig))

        dry = Sched(nc, ctx, signal=None)
        emit_all(dry)
        emit_all(Sched(nc, ctx, signal=dry.signal))
    return nc


def _piece_cols():
    cols = {}
    qperm = np.empty(512, np.int64)
    for g in range(4):
        for kv in range(2):
            qperm[g * 128 + kv * 64:g * 128 + kv * 64 + 64] = O_Q + (kv * 4 + g) * 64 + np.arange(64)
    cols["q"] = qperm
    cols["kv"] = np.concatenate([O_K + np.arange(128), O_V + np.arange(128)])
    cols["za"] = O_ZA + np.arange(512)
    for c in range(4):
        r = np.arange(128) + c * 128
        cols["b%d" % c] = np.concatenate([O_GC + r, O_HB + r, O_GB + r, O_ZB + r])
    cols["vc"] = O_VC + np.arange(512)
    cols["u"] = O_U + np.arange(512)
    cols["zc"] = O_ZC + np.arange(512)
    for j in range(8):
        r = np.arange(128) + j * 128
        cols["g%d" % j] = np.concatenate([O_G + r, O_G + 1024 + r, O_G + 2048 + r])
    return cols


def pack_weights(w_in, w_out_a, w_out_b, w_out_c, w_o):
    L = w_in.shape[0]
    cols = _piece_cols()
    wp = np.empty((L, 128, PCOLS), np.float32)
    for l in range(L):
        for (pn, kk, g) in PIECES:
            off = PIECE_OFF[pn][0]
            if pn.startswith("g"):
                j = int(pn[1:])
                a = w_in[l][:, cols[pn]].reshape(8, 128, g).transpose(1, 0, 2)
                wo = np.concatenate([w_out_a[l][:, j * 128:(j + 1) * 128], w_out_b[l][:, j * 128:(j + 1) * 128],
                                     w_out_c[l][:, j * 128:(j + 1) * 128]], axis=1)
                b = wo.reshape(4, 128, g).transpose(1, 0, 2)
                blk = np.concatenate([a, b], axis=1)
            elif pn.startswith("o"):
                h = int(pn[1:])
                blk = w_o[l][:, h * 512:(h + 1) * 512].reshape(8, 128, g).transpose(1, 0, 2)
            else:
                blk = w_in[l][:, cols[pn]].reshape(8, 128, g).transpose(1, 0, 2)
            wp[l, :, off:off + kk * g] = blk.reshape(128, kk * g)
    return wp


def attn_bias_tiles(first_masked):
    h = np.arange(1, 9, dtype=np.float64)
    slopes = np.exp2(-8.0 * h / 8).reshape(2, 4)
    out = np.empty((10, 128, 512), np.float32)
    j = np.arange(128)[:, None]
    i = np.arange(128)[None, :]
    for kv in range(2):
        for g in range(4):
            s = slopes[kv, g]
            dp = i + 128 - j
            out[0 + kv][:, g * 128:(g + 1) * 128] = np.where(dp < 128, -s * dp, NEG)
            dc = i - j
            out[2 + kv][:, g * 128:(g + 1) * 128] = np.where(dc >= 0, -s * dc, NEG)
        out[4 + kv] = NEG if first_masked else out[0 + kv]
        t = np.arange(TS)[None, :]
        for g in range(4):
            s = slopes[kv, g]
            d = t + 128 - j
            tile_ = np.where(j > t, -s * d, NEG)
            for n in range(NSEQ):
                out[6 + kv][:, n * 32 + g * 8:n * 32 + g * 8 + 8] = tile_
        rn = (np.arange(128) // TS)[:, None]
        rs = (np.arange(128) % TS)[:, None]
        cn = (np.arange(128) // TS)[None, :]
        ct = (np.arange(128) % TS)[None, :]
        for g in range(4):
            s = slopes[kv, g]
            ok = (rn == cn) & (rs <= ct)
            out[8 + kv][:, g * 128:(g + 1) * 128] = np.where(ok, -s * (ct - rs), NEG)
    return np.exp(out.astype(np.float64)).astype(np.float32)


def host_common(inputs, L):
    f = np.float32
    norm_w, b_gate = inputs["norm_w"], inputs["b_gate"]
    c = {}
    c["wp"] = pack_weights(inputs["w_in"], inputs["w_out_a"], inputs["w_out_b"], inputs["w_out_c"], inputs["w_o"])
    c["nw"] = np.ascontiguousarray(norm_w.reshape(L, 8, 128).transpose(2, 0, 1).reshape(128, L * 8)).astype(f)
    c["bg"] = np.ascontiguousarray(b_gate.reshape(L, 24, 128).transpose(2, 0, 1).reshape(128, L * 24)).astype(f)
    c["qw"] = np.ascontiguousarray(np.tile(inputs["q_norm_w"], (1, 2)).T).astype(f)
    c["kw"] = np.ascontiguousarray(np.tile(inputs["k_norm_w"], (1, 2)).T).astype(f)
    sk = np.empty((128, L * 4), f)
    for l in range(L):
        for kv in range(2):
            for c2 in range(2):
                for par in range(2):
                    sk[par * 64:(par + 1) * 64, l * 4 + kv * 2 + c2] = inputs["sinks"][l, kv * 4 + 2 * c2 + par]
    c["sk4"] = sk
    c["cw"] = np.ascontiguousarray(inputs["conv_w"].reshape(L, 3, 4, 128).transpose(3, 0, 1, 2).reshape(128, L * 12)).astype(f)
    c["vnw"] = np.ascontiguousarray(np.broadcast_to(inputs["v_norm_w"][:, None, :], (L, 128, 512))).astype(f)
    c["wsp"] = np.ascontiguousarray(inputs["w_spatial"]).astype(f)
    bs = inputs["b_spatial"]
    c["bspt"] = np.ascontiguousarray(np.broadcast_to(bs.reshape(L, 1, 512), (L, 128, 512))).astype(f)
    bss = np.tile(bs[:, :, :TS], (1, 1, NSEQ))
    c["bspts"] = np.ascontiguousarray(np.broadcast_to(bss.reshape(L, 1, 512), (L, 128, 512))).astype(f)
    c["ident"] = np.eye(128, dtype=f)
    s_ = np.arange(128)
    c["tri"] = (s_[:, None] <= s_[None, :]).astype(f)
    c["bdm"] = ((s_[:, None] // TS) == (s_[None, :] // TS)).astype(f)
    c["bd64"] = ((s_[:, None] // 64) == (s_[None, :] // 64)).astype(f)
    c["repi"] = (np.arange(8)[:, None] == (s_[None, :] % TS)).astype(f)
    return c


def kernel(**inputs):
    L = 4
    n_cores = 8
    x_prompt = np.asarray(inputs["x_prompt"], np.float32)
    x_sample = np.asarray(inputs["x_sample"], np.float32)
    inputs = {k: np.asarray(v) for k, v in inputs.items()}
    B_, S_, _ = x_prompt.shape
    npb, halo = 20, 4
    cores_per_b = n_cores // B_
    tok_per_core = S_ // cores_per_b
    common = host_common(inputs, L)
    bias_mid = attn_bias_tiles(False)
    bias_first = attn_bias_tiles(True)
    in_maps = []
    for c in range(n_cores):
        b, q = divmod(c, cores_per_b)
        t0 = q * tok_per_core
        xp = np.zeros((npb * 128, D), np.float32)
        if q == 0:
            xp[halo * 128:] = x_prompt[b, t0:t0 + tok_per_core]
        else:
            xp[:] = x_prompt[b, t0 - halo * 128:t0 + tok_per_core]
        n0 = c * NSEQ
        m = dict(common)
        m["xp"] = xp
        m["xs"] = np.ascontiguousarray(x_sample[n0:n0 + NSEQ].reshape(NSEQ * TS, D))
        m["ck"] = np.ascontiguousarray(inputs["cache_k"][:, n0:n0 + NSEQ].reshape(L, NSEQ, 128, 128))
        m["cv"] = np.ascontiguousarray(inputs["cache_v"][:, n0:n0 + NSEQ].reshape(L, NSEQ, 128, 128))
        m["sconv"] = np.ascontiguousarray(inputs["state_conv"][:, n0:n0 + NSEQ].reshape(L, NSEQ * 2, 512))
        m["biasall"] = bias_first if q == 0 else bias_mid
        in_maps.append(m)
    nc = build(L, default_sts(), npb, halo)
    res = run_bass_kernel_spmd(nc, in_maps, core_ids=list(range(n_cores)))
    R = res.results
    yp = np.concatenate([R[c]["yp"] for c in range(n_cores)], axis=0).reshape(B_, S_, D)
    ys = np.concatenate([R[c]["ys"] for c in range(n_cores)], axis=0).reshape(n_cores * NSEQ, TS, D)
    last = [b * cores_per_b + cores_per_b - 1 for b in range(B_)]
    pk = np.stack([R[c]["pk"] for c in last], axis=1).reshape(L, B_, 128, 2, 64)
    pv = np.stack([R[c]["pv"] for c in last], axis=1).reshape(L, B_, 128, 2, 64)
    pc = np.stack([R[c]["pc"] for c in last], axis=1).reshape(L, B_, 2, 512)
    sk = np.concatenate([R[c]["sko"] for c in range(n_cores)], axis=1).reshape(L, n_cores * NSEQ, 128, 2, 64)
    sv = np.concatenate([R[c]["svo"] for c in range(n_cores)], axis=1).reshape(L, n_cores * NSEQ, 128, 2, 64)
    sc = np.concatenate([R[c]["sco"].reshape(L, NSEQ, 2, 512) for c in range(n_cores)], axis=1)
    scv = np.concatenate([R[c]["scv"].reshape(L, NSEQ, TS, 512) for c in range(n_cores)], axis=1)
    f = np.float32
    return (yp.astype(f), ys.astype(f), pk.astype(f), pv.astype(f), pc.astype(f),
            sk.astype(f), sv.astype(f), sc.astype(f), scv.astype(f))
```
